# Optimizing a Trainium2 kernel written in Bass

```python
import math
import jax
import jax.numpy as jnp
from jax import lax
import numpy as np

D_MODEL = 1024
BATCH = 16
SEQ = 2048
DEPTH = 4

HEAD_DIM = 64
N_GROUPS = 4
MIX_WIDTH = D_MODEL
GROUP_WIDTH = MIX_WIDTH // N_GROUPS
GROUP_HEADS = GROUP_WIDTH // HEAD_DIM
D_FF = 256 * ((8 * D_MODEL // 3 + 255) // 256)
PLE_DIM = 256
ROPE_THETA = 10000.0
EPS = 1e-6
NEG_INF = -1e30

GRID_W = 64
NA_HEADS = GROUP_HEADS
NA_ROWS = 8
NA_COLS = 16
SWA_HEADS = GROUP_HEADS
SWA_KV_HEADS = GROUP_HEADS // 2
SWA_GROUP = SWA_HEADS // SWA_KV_HEADS
SWA_HALF = 128
SWA_BLOCK = 128
DIL_HEADS = GROUP_HEADS
DIL_PATTERNS = ((128, 1), (512, 4), (2048, 16))
DIL_BLOCK = 64
MLA_HEADS = GROUP_HEADS
MLA_Q_LORA = D_MODEL // 4
MLA_KV_LORA = D_MODEL // 8
MLA_NOPE = HEAD_DIM
MLA_ROPE = HEAD_DIM // 2
MLA_V = GROUP_WIDTH // MLA_HEADS
DENSE_BLOCK = 128

NA_IN = 3 * NA_HEADS * HEAD_DIM
SWA_IN = (SWA_HEADS + 2 * SWA_KV_HEADS) * HEAD_DIM
DIL_IN = 3 * DIL_HEADS * HEAD_DIM
MLA_IN = MLA_Q_LORA + MLA_KV_LORA + MLA_ROPE
IN_COLS = NA_IN + SWA_IN + DIL_IN + MLA_IN
IN_SPLITS = (NA_IN, NA_IN + SWA_IN, NA_IN + SWA_IN + DIL_IN)

kernel_name = "hybrid_parallel_headgroup_encoder"


def rms_norm(x, g):
    xf = x.astype(jnp.float32)
    y = xf * lax.rsqrt(jnp.mean(xf * xf, axis=-1, keepdims=True) + EPS)
    return (y * g.astype(jnp.float32)).astype(x.dtype)


def rope(x, pos):
    half = x.shape[-1] // 2
    inv_freq = ROPE_THETA ** (-jnp.arange(half, dtype=jnp.float32) / half)
    ang = pos.astype(jnp.float32)[:, None] * inv_freq[None, :]
    cos, sin = jnp.cos(ang), jnp.sin(ang)
    xf = x.astype(jnp.float32)
    x1, x2 = xf[..., :half], xf[..., half:]
    return jnp.concatenate([x1 * cos - x2 * sin, x2 * cos + x1 * sin], axis=-1).astype(x.dtype)


def swiglu(u, w_gate, w_up, w_down):
    return (jax.nn.silu(u @ w_gate) * (u @ w_up)) @ w_down


def split_heads(t, n):
    b, s, _ = t.shape
    return t.reshape(b, s, n, -1).transpose(0, 2, 1, 3)


def merge_heads(t):
    b, h, s, d = t.shape
    return t.transpose(0, 2, 1, 3).reshape(b, s, h * d)


def band_attention_stats(q, k, v, half_width, block, scale):
    L = q.shape[-2]
    qb_len = math.gcd(L, block)
    nb = L // qb_len
    kw = qb_len + 2 * half_width
    pad = [(0, 0)] * (k.ndim - 2) + [(half_width, half_width), (0, 0)]
    kp = jnp.pad(k, pad)
    vp = jnp.pad(v, pad)
    starts = jnp.arange(nb) * qb_len
    idx = starts[:, None] + jnp.arange(kw)[None, :]
    kb = jnp.take(kp, idx, axis=-2)
    vb = jnp.take(vp, idx, axis=-2)
    qb = q.reshape(q.shape[:-2] + (nb, qb_len, q.shape[-1]))
    s = jnp.einsum("...gnqd,...nkd->...gnqk", qb, kb).astype(jnp.float32) * scale
    qpos = starts[:, None, None] + jnp.arange(qb_len)[None, :, None]
    kpos = starts[:, None, None] - half_width + jnp.arange(kw)[None, None, :]
    valid = (jnp.abs(kpos - qpos) <= half_width) & (kpos >= 0) & (kpos < L)
    s = jnp.where(valid, s, NEG_INF)
    m = jnp.max(s, axis=-1)
    p = jnp.exp(s - m[..., None])
    l = jnp.sum(p, axis=-1)
    o = jnp.einsum("...gnqk,...nkd->...gnqd", p, vb)
    lead = q.shape[:-2]
    return (m.reshape(lead + (L,)), l.reshape(lead + (L,)), o.reshape(lead + (L, o.shape[-1])))


def neighbourhood_attention(q, k, v, bias_table):
    b, h, t, dh = q.shape
    rows = t // GRID_W
    kr = min(NA_ROWS, rows)
    n_cb = GRID_W // NA_COLS
    kc = 2 * NA_COLS
    r = jnp.arange(rows)
    key_row = jnp.clip(r - kr // 2, 0, rows - kr)[:, None] + jnp.arange(kr)[None, :]
    c0 = jnp.arange(n_cb) * NA_COLS
    key_col = jnp.clip(c0 - NA_COLS // 2, 0, GRID_W - kc)[:, None] + jnp.arange(kc)[None, :]
    key_idx = (key_row[:, None, :, None] * GRID_W + key_col[None, :, None, :]).reshape(rows, n_cb, kr * kc)
    kb = jnp.take(k, key_idx, axis=2)
    vb = jnp.take(v, key_idx, axis=2)
    qb = q.reshape(b, h, rows, n_cb, NA_COLS, dh)
    s = jnp.einsum("bhrcqd,bhrckd->bhrcqk", qb, kb).astype(jnp.float32) * dh ** -0.5
    q_col = c0[:, None] + jnp.arange(NA_COLS)[None, :]
    win_start = jnp.clip(q_col - NA_COLS // 2, 0, GRID_W - NA_COLS)
    kcol = key_col[:, None, None, :]
    ws = win_start[:, :, None, None]
    valid = (kcol >= ws) & (kcol < ws + NA_COLS)
    valid = jnp.broadcast_to(valid, (n_cb, NA_COLS, kr, kc)).reshape(n_cb, NA_COLS, kr * kc)
    d_row = (key_row - r[:, None] + NA_ROWS - 1)[:, None, None, :, None]
    d_col = (jnp.clip(kcol - q_col[:, :, None, None], 1 - NA_COLS, NA_COLS - 1) + NA_COLS - 1)[None]
    bias = bias_table[:, d_row, d_col].reshape(h, rows, n_cb, NA_COLS, kr * kc)
    s = jnp.where(valid, s + bias.astype(jnp.float32), NEG_INF)
    p = jax.nn.softmax(s, axis=-1)
    o = jnp.einsum("bhrcqk,bhrckd->bhrcqd", p.astype(v.dtype), vb)
    return o.reshape(b, h, t, dh)


def na_mixer(z, bias_table):
    q, k, v = jnp.split(z, 3, axis=-1)
    o = neighbourhood_attention(split_heads(q, NA_HEADS), split_heads(k, NA_HEADS),
                                split_heads(v, NA_HEADS), bias_table)
    return merge_heads(o)


def swa_mixer(z, sink, pos):
    b, t, _ = z.shape
    q, k, v = jnp.split(z, [SWA_HEADS * HEAD_DIM, (SWA_HEADS + SWA_KV_HEADS) * HEAD_DIM], axis=-1)
    q = rope(split_heads(q, SWA_HEADS), pos).reshape(b, SWA_KV_HEADS, SWA_GROUP, t, HEAD_DIM)
    k = rope(split_heads(k, SWA_KV_HEADS), pos)
    v = split_heads(v, SWA_KV_HEADS)
    m, l, o = band_attention_stats(q, k, v, SWA_HALF, SWA_BLOCK, HEAD_DIM ** -0.5)
    sk = sink.reshape(SWA_KV_HEADS, SWA_GROUP)[:, :, None].astype(jnp.float32)
    m2 = jnp.maximum(m, sk)
    a = jnp.exp(m - m2)
    o = o * (a / (l * a + jnp.exp(sk - m2)))[..., None]
    return merge_heads(o.reshape(b, SWA_HEADS, t, HEAD_DIM).astype(z.dtype))


def to_residue(x, dil):
    t, d = x.shape[-2], x.shape[-1]
    return jnp.swapaxes(x.reshape(x.shape[:-2] + (t // dil, dil, d)), -3, -2)


def from_residue(x):
    y = jnp.swapaxes(x, -3, -2)
    return y.reshape(y.shape[:-3] + (y.shape[-3] * y.shape[-2], y.shape[-1]))


def dil_mixer(z, pos):
    b, t, _ = z.shape
    q, k, v = jnp.split(z, 3, axis=-1)
    q = rope(split_heads(q, DIL_HEADS), pos)
    k = rope(split_heads(k, DIL_HEADS), pos)
    v = split_heads(v, DIL_HEADS)
    ms, ls, os_ = [], [], []
    for window, dil in DIL_PATTERNS:
        steps = window // 2 // dil
        m, l, o = band_attention_stats(to_residue(q, dil)[..., None, :, :], to_residue(k, dil),
                                       to_residue(v, dil), steps, DIL_BLOCK, HEAD_DIM ** -0.5)
        ms.append(from_residue(m[..., 0, :, None])[..., 0])
        ls.append(from_residue(l[..., 0, :, None])[..., 0])
        os_.append(from_residue(o[..., 0, :, :]))
    m_all = jnp.stack(ms)
    w = jnp.exp(m_all - jnp.max(m_all, axis=0))
    den = jnp.sum(jnp.stack(ls) * w, axis=0)
    num = jnp.sum(jnp.stack(os_) * w[..., None], axis=0)
    return merge_heads((num / den[..., None]).astype(z.dtype))


def dense_block_attention(q, k, v, scale):
    b, h, t, dk = q.shape
    qb_len = math.gcd(t, DENSE_BLOCK)
    qb = q.reshape(b, h, t // qb_len, qb_len, dk).transpose(2, 0, 1, 3, 4)

    def one_block(q_blk):
        s = jnp.einsum("bhqd,bhkd->bhqk", q_blk, k).astype(jnp.float32) * scale
        p = jax.nn.softmax(s, axis=-1)
        return jnp.einsum("bhqk,bhkd->bhqd", p.astype(v.dtype), v)

    o = lax.map(one_block, qb)
    return o.transpose(1, 2, 0, 3, 4).reshape(b, h, t, v.shape[-1])


def mla_mixer(z, q_norm, w_uq, kv_norm, w_ukv, pos):
    b, t, _ = z.shape
    cq, ckv, kpe = jnp.split(z, [MLA_Q_LORA, MLA_Q_LORA + MLA_KV_LORA], axis=-1)
    qf = (rms_norm(cq, q_norm) @ w_uq).reshape(b, t, MLA_HEADS, MLA_NOPE + MLA_ROPE).transpose(0, 2, 1, 3)
    q = jnp.concatenate([qf[..., :MLA_NOPE], rope(qf[..., MLA_NOPE:], pos)], axis=-1)
    kv = (rms_norm(ckv, kv_norm) @ w_ukv).reshape(b, t, MLA_HEADS, MLA_NOPE + MLA_V).transpose(0, 2, 1, 3)
    k_pe = rope(kpe, pos)[:, None]
    k = jnp.concatenate([kv[..., :MLA_NOPE], jnp.broadcast_to(k_pe, (b, MLA_HEADS, t, MLA_ROPE))], axis=-1)
    v = kv[..., MLA_NOPE:]
    o = dense_block_attention(q, k, v, (MLA_NOPE + MLA_ROPE) ** -0.5)
    return merge_heads(o)


def setup_inputs(seed: int = 0) -> dict:
    key = jax.random.key(seed)
    ks = jax.random.split(key, 26)
    f32 = jnp.float32

    def w(k, shape, fan_in):
        return jax.random.normal(k, shape, f32) * (fan_in ** -0.5)

    def g(k, shape):
        return 1.0 + 0.02 * jax.random.normal(k, shape, f32)

    L = DEPTH
    return {
        "x": jax.random.normal(ks[0], (BATCH, SEQ, D_MODEL), f32),
        "p": jax.random.normal(ks[1], (DEPTH, BATCH, SEQ, PLE_DIM), f32),
        "ffn1_norm": g(ks[2], (L, D_MODEL)),
        "ffn1_w_gate": w(ks[3], (L, D_MODEL, D_FF), D_MODEL),
        "ffn1_w_up": w(ks[4], (L, D_MODEL, D_FF), D_MODEL),
        "ffn1_w_down": w(ks[5], (L, D_FF, D_MODEL), D_FF),
        "mix_norm": g(ks[6], (L, D_MODEL)),
        "w_in": w(ks[7], (L, D_MODEL, IN_COLS), D_MODEL),
        "na_bias": 0.1 * jax.random.normal(ks[8], (L, NA_HEADS, 2 * NA_ROWS - 1, 2 * NA_COLS - 1), f32),
        "swa_sink": 0.5 * jax.random.normal(ks[9], (L, SWA_HEADS), f32),
        "mla_q_norm": g(ks[10], (L, MLA_Q_LORA)),
        "mla_w_uq": w(ks[11], (L, MLA_Q_LORA, MLA_HEADS * (MLA_NOPE + MLA_ROPE)), MLA_Q_LORA),
        "mla_kv_norm": g(ks[12], (L, MLA_KV_LORA)),
        "mla_w_ukv": w(ks[13], (L, MLA_KV_LORA, MLA_HEADS * (MLA_NOPE + MLA_V)), MLA_KV_LORA),
        "group_norm": g(ks[14], (L, N_GROUPS, GROUP_WIDTH)),
        "w_out": w(ks[15], (L, MIX_WIDTH, D_MODEL), MIX_WIDTH),
        "ffn2_norm": g(ks[16], (L, D_MODEL)),
        "ffn2_w_gate": w(ks[17], (L, D_MODEL, D_FF), D_MODEL),
        "ffn2_w_up": w(ks[18], (L, D_MODEL, D_FF), D_MODEL),
        "ffn2_w_down": w(ks[19], (L, D_FF, D_MODEL), D_FF),
        "ple_norm": g(ks[20], (L, D_MODEL)),
        "ple_w_gate": w(ks[21], (L, D_MODEL, D_MODEL), D_MODEL),
        "ple_w_proj": w(ks[22], (L, PLE_DIM, D_MODEL), PLE_DIM),
        "final_norm": g(ks[23], (D_MODEL,)),
    }


def reference(x, p, ffn1_norm, ffn1_w_gate, ffn1_w_up, ffn1_w_down, mix_norm, w_in, na_bias,
              swa_sink, mla_q_norm, mla_w_uq, mla_kv_norm, mla_w_ukv, group_norm, w_out,
              ffn2_norm, ffn2_w_gate, ffn2_w_up, ffn2_w_down, ple_norm, ple_w_gate, ple_w_proj,
              final_norm):
    t = x.shape[1]
    pos = jnp.arange(t, dtype=jnp.int32)
    for i in range(DEPTH):
        x = x + 0.5 * swiglu(rms_norm(x, ffn1_norm[i]), ffn1_w_gate[i], ffn1_w_up[i], ffn1_w_down[i])
        z = rms_norm(x, mix_norm[i]) @ w_in[i]
        z_na, z_swa, z_dil, z_mla = jnp.split(z, IN_SPLITS, axis=-1)
        y_na = na_mixer(z_na, na_bias[i])
        y_swa = swa_mixer(z_swa, swa_sink[i], pos)
        y_dil = dil_mixer(z_dil, pos)
        y_mla = mla_mixer(z_mla, mla_q_norm[i], mla_w_uq[i], mla_kv_norm[i], mla_w_ukv[i], pos)
        y = jnp.concatenate([rms_norm(y_na, group_norm[i, 0]), rms_norm(y_swa, group_norm[i, 1]),
                             rms_norm(y_dil, group_norm[i, 2]), rms_norm(y_mla, group_norm[i, 3])], axis=-1)
        x = x + y @ w_out[i]
        x = x + 0.5 * swiglu(rms_norm(x, ffn2_norm[i]), ffn2_w_gate[i], ffn2_w_up[i], ffn2_w_down[i])
        gate = jax.nn.sigmoid(rms_norm(x, ple_norm[i]) @ ple_w_gate[i])
        x = x + gate * (p[i] @ ple_w_proj[i])
    return rms_norm(x, final_norm)
```

```python
import math
from contextlib import ExitStack

import numpy as np
import concourse.bass as bass
import concourse.mybir as mybir
from concourse.bass_utils import run_bass_kernel_spmd

F32 = mybir.dt.float32
BF16 = mybir.dt.bfloat16
AF = mybir.ActivationFunctionType
ALU = mybir.AluOpType
AX = mybir.AxisListType

D = 1024
T = 2048
L_ALL = 4
DFF = 2816
NFC = 22
NCH = 4
CH = 512
EPS = 1e-6
NEG = -30000.0
ALL_FLAGS = ("ffn1", "na", "swa", "dil", "mla", "ffn2", "ple")

G_FFN1, G_MIX, G_FFN2, G_PLE, G_GRP, G_QN, G_KVN, NG = 0, 8, 16, 24, 32, 40, 42, 43

(P_NA_Q, P_NA_K, P_NA_V, P_SW_Q, P_SW_QS, P_SW_K, P_SW_V, P_DI_Q, P_DI_QS, P_DI_K, P_DI_KS,
 P_DI_V, P_ML_CQ, P_ML_CKV, P_ML_KPE) = range(15)
N_WIN = 15

WSPEC = [
    ("w_gu1", 22, 8, 256, G_FFN1),
    ("w_d1", 8, 22, 128, None),
    ("w_in", N_WIN, 8, 256, G_MIX),
    ("w_out", 4, 2, 1024, G_GRP),
    ("w_uq", 1, 2, 768, G_QN),
    ("w_ukv", 1, 1, 512, G_KVN),
    ("w_gu2", 22, 8, 256, G_FFN2),
    ("w_d2", 8, 22, 128, None),
    ("w_pg", 4, 8, 256, G_PLE),
    ("w_pp", 1, 2, 1024, None),
]


class Buf:
    __slots__ = ("name", "last_w", "readers", "dsem", "dcount")

    def __init__(self, name):
        self.name = name
        self.last_w = None
        self.readers = {}
        self.dsem = None
        self.dcount = 0


class Tracker:
    EPOCH = 30000

    def __init__(self, nc, stack):
        self.nc = nc
        self.stack = stack
        self.engs = {}
        self.nsem = 0
        self.dbufs = []
        self.free_sems = []
        for nm, e in (("pe", nc.tensor), ("act", nc.scalar), ("dve", nc.vector),
                      ("pool", nc.gpsimd), ("sp", nc.sync)):
            self.engs[nm] = dict(name=nm, eng=e, sem=self._newsem("s_" + nm), count=0, seen={}, old=[])

    def _newsem(self, name):
        self.nsem += 1
        return self.stack.enter_context(self.nc.semaphore("%s_%d" % (name, self.nsem)))

    def _waits(self, E, reads, writes):
        deps = {}

        def add(rec, raw):
            sem, val = rec
            if sem is E["sem"] and not raw and E["name"] == "pe":
                return
            k = id(sem)
            if k not in deps or deps[k][1] < val:
                deps[k] = (sem, val)

        for b in reads:
            if b.last_w is not None:
                add(b.last_w, True)
        for b in writes:
            if b.last_w is not None:
                add(b.last_w, False)
            for rec in b.readers.values():
                add(rec, False)
        for k, (sem, val) in deps.items():
            if E["seen"].get(k, 0) < val:
                E["eng"].wait_ge(sem, val)
                E["seen"][k] = val

    def _record(self, rec, reads, writes):
        k = id(rec[0])
        for b in reads:
            b.readers[k] = rec
        for b in writes:
            b.last_w = rec
            b.readers = {}

    def op(self, en, reads, writes, fn):
        E = self.engs[en]
        self._waits(E, reads, writes)
        inst = fn(E["eng"])
        if E["count"] >= self.EPOCH:
            E["old"].append((E["sem"], E["count"]))
            E["sem"] = self._newsem("s_" + en)
            E["count"] = 0
        E["count"] += 1
        inst.then_inc(E["sem"], 1)
        self._record((E["sem"], E["count"]), reads, writes)

    def dma(self, out, in_, reads, writes, sembuf=None, qn="sp"):
        E = self.engs[qn]
        self._waits(E, reads, writes)
        sb = sembuf if sembuf is not None else writes[0]
        if sb.dsem is None:
            if self.free_sems:
                sb.dsem, sb.dcount = self.free_sems.pop()
            else:
                sb.dsem, sb.dcount = self._newsem("d"), 0
            self.dbufs.append(sb)
        sb.dcount += 16
        E["eng"].dma_start(out=out, in_=in_).then_inc(sb.dsem, 16)
        self._record((sb.dsem, sb.dcount), reads, writes)

    def barrier(self):
        recs = []
        for E in self.engs.values():
            if E["count"] > 0:
                recs.append((E["sem"], E["count"]))
        for b in self.dbufs:
            if b.dcount > 0:
                recs.append((b.dsem, b.dcount))
        for E in self.engs.values():
            for sem, val in recs:
                k = id(sem)
                if E["seen"].get(k, 0) < val:
                    E["eng"].wait_ge(sem, val)
                    E["seen"][k] = val
        for b in self.dbufs:
            if b.dcount < 24000:
                self.free_sems.append((b.dsem, b.dcount))
            b.dsem = None
        self.dbufs = []

    def finish(self, bufs, qn="sp"):
        self._waits(self.engs[qn], bufs, [])


class Arena:
    def __init__(self, t, nbytes):
        self.t = t
        self.nbytes = nbytes
        self.off = 0

    def reset(self):
        self.off = 0

    def alloc(self, nbytes, dtype=BF16):
        nbytes = (nbytes + 63) // 64 * 64
        o = self.off
        self.off += nbytes
        assert self.off <= self.nbytes, ("arena overflow", self.off, self.nbytes)
        v = self.t[:, o // 2:(o + nbytes) // 2]
        if dtype == F32:
            v = v.bitcast(F32)
        return v


def build_program(n_layers=L_ALL, n_seq=2, flags=ALL_FLAGS):
    nc = bass.Bass("TRN2", target_bir_lowering=False)
    L = n_layers
    flags = set(flags)
    dram = {}

    def din(name, shape, dt=F32):
        dram[name] = nc.dram_tensor(name, list(shape), dt, kind="ExternalInput").ap()
        return dram[name]

    x_d = din("x", [n_seq, T, D])
    p_d = din("p", [L, n_seq, T, 256])
    wsrc = {}
    wscr = {}
    for (nm, npc, nk, C, goff) in WSPEC:
        wsrc[nm] = din(nm, [L, npc, 128, nk * C])
        wscr[nm] = nc.dram_tensor("s_" + nm, [L, npc, 128, nk * C], BF16, kind="Internal").ap()
    g_d = din("g_all", [128, L, NG])
    gfin_d = din("final_g", [128, 8])
    nag_d = din("na_g", [L, 4, 2, 128, 1024])
    sink_d = din("sink_b", [128, L, 4])
    rope_d = din("rope_t", [4, 128, T])
    dmask_d = din("dil_mask", [128, 2176])
    smask_d = din("swa_mask", [128, 384])
    ident_d = din("ident", [128, 128])
    y_d = nc.dram_tensor("y", [n_seq, T, D], F32, kind="ExternalOutput").ap()

    with ExitStack() as st:
        TR = Tracker(nc, st)
        sb = nc.alloc_sbuf_tensor
        xT = sb("xT", [128, 8, T], F32)
        uT = sb("uT", [128, 8, T], BF16)
        rope = sb("rope", [128, 4, T], BF16)
        dmask = sb("dmask", [128, 2176], BF16)
        smask = sb("smask", [128, 384], BF16)
        ident = sb("ident_sb", [128, 128], F32)
        ones_bf = sb("ones_bf", [128, 128], BF16)
        onesA = sb("onesA", [128, 128], BF16)
        onesB = sb("onesB", [128, 128], BF16)
        g_sb = sb("g_sb", [128, L, NG], F32)
        gfin = sb("gfin", [128, 8], F32)
        sink_sb = sb("sink_sb", [128, L, 4], F32)
        nhalf = sb("nhalf", [128, 1], F32)
        eps_t = sb("eps_t", [128, 1], F32)
        stat = sb("stat", [128, 64], F32)
        ARENA_BYTES = 87 * 1024
        arena_t = sb("arena", [128, ARENA_BYTES // 2], BF16)
        AR = Arena(arena_t, ARENA_BYTES)
        banks = [nc.alloc_psum_tensor("bank%d" % i, [128, 512], F32) for i in range(8)]
        Bbanks = [Buf("bank%d" % i) for i in range(8)]
        bank_rr = [0]
        obank_rr = [0]

        def gbank():
            i = bank_rr[0] % 6
            bank_rr[0] += 1
            return banks[i], Bbanks[i]

        def obank():
            i = 6 + obank_rr[0] % 2
            obank_rr[0] += 1
            return banks[i], Bbanks[i]

        BxT = [Buf("xT%d" % c) for c in range(NCH)]
        BuT = [Buf("uT%d" % c) for c in range(NCH)]
        Bconst = Buf("const")
        Bstat = Buf("stat")
        Bscr = Buf("scratch")
        rr = {"n": 0}

        def alt(a="dve", b="pool"):
            rr["n"] += 1
            return a if rr["n"] % 2 else b

        st.enter_context(nc.Block())

        AR.reset()
        c32 = AR.alloc(2176 * 4, F32)
        Bc32 = Buf("c32")
        for i in range(4):
            TR.dma(c32[:, 0:T], rope_d[i], [], [Bc32])
            TR.op("dve", [Bc32], [Bconst], lambda e, i=i: e.tensor_copy(out=rope[:, i, :], in_=c32[:, 0:T]))
        TR.dma(c32[:, 0:2176], dmask_d, [], [Bc32])
        TR.op("dve", [Bc32], [Bconst], lambda e: e.tensor_copy(out=dmask[:, :], in_=c32[:, 0:2176]))
        TR.dma(c32[:, 0:384], smask_d, [], [Bc32])
        TR.op("dve", [Bc32], [Bconst], lambda e: e.tensor_copy(out=smask[:, :], in_=c32[:, 0:384]))
        TR.dma(ident[:, :], ident_d, [], [Bconst])
        TR.dma(g_sb[:, :, :], g_d, [], [Bconst])
        TR.dma(gfin[:, :], gfin_d, [], [Bconst])
        TR.dma(sink_sb[:, :, :], sink_d, [], [Bconst])
        TR.op("dve", [], [Bconst], lambda e: e.memset(ones_bf[:, :], 1.0))
        TR.op("dve", [], [Bconst], lambda e: e.memset(onesA[:, :], 0.0))
        TR.op("dve", [], [Bconst], lambda e: e.memset(onesB[:, :], 0.0))
        TR.op("dve", [Bconst], [Bconst], lambda e: e.memset(onesA[0:64, :], 1.0))
        TR.op("dve", [Bconst], [Bconst], lambda e: e.memset(onesB[64:128, :], 1.0))
        TR.op("dve", [], [Bconst], lambda e: e.memset(nhalf[:, :], -0.5))
        TR.op("dve", [], [Bconst], lambda e: e.memset(eps_t[:, :], EPS))
        TR.barrier()

        AR.reset()
        FMAX = 2816
        NSL = 5
        s32 = [AR.alloc(FMAX * 4, F32) for _ in range(NSL)]
        s16 = [AR.alloc(FMAX * 2) for _ in range(NSL)]
        Bs32 = [Buf("s32_%d" % i) for i in range(NSL)]
        Bs16 = [Buf("s16_%d" % i) for i in range(NSL)]
        Bst = [Buf("st_%d" % i) for i in range(NSL)]
        used = set()
        if "ffn1" in flags:
            used |= {"w_gu1", "w_d1"}
        if "ffn2" in flags:
            used |= {"w_gu2", "w_d2"}
        if flags & {"na", "swa", "dil", "mla"}:
            used |= {"w_in", "w_out"}
        if "mla" in flags:
            used |= {"w_uq", "w_ukv"}
        if "ple" in flags:
            used |= {"w_pg", "w_pp"}
        BG_OK = ("ffn1" in flags) and ("ffn2" in flags)
        tasks = []
        bg_tasks = {}
        for l in range(L):
            for (nm, npc, nk, C, goff) in WSPEC:
                if nm in used:
                    for pi in range(npc):
                        if l == 0 or not BG_OK:
                            tasks.append((l, nm, pi, nk, C, goff))
                        else:
                            F_ = nk * C
                            if F_ <= 2048:
                                bg_tasks.setdefault(l, []).append((l, nm, pi, nk, C, goff, 0, F_))
                            else:
                                assert goff is None and F_ % 2 == 0
                                bg_tasks.setdefault(l, []).append((l, nm, pi, nk, C, goff, 0, F_ // 2))
                                bg_tasks.setdefault(l, []).append((l, nm, pi, nk, C, goff, F_ // 2, F_))

        def pp_load(k):
            l, nm, pi, nk, C, goff = tasks[k]
            s = k % NSL
            TR.dma(s32[s][:, 0:nk * C], wsrc[nm][l, pi], [], [Bs32[s]])

        for k in range(min(NSL - 1, len(tasks))):
            pp_load(k)
        for k in range(len(tasks)):
            l, nm, pi, nk, C, goff = tasks[k]
            F = nk * C
            s = k % NSL
            if k + NSL - 1 < len(tasks):
                pp_load(k + NSL - 1)
            en = "pool" if k % 4 == 3 else "dve"
            if goff is None:
                TR.op(en, [Bs32[s]], [Bs16[s]],
                      lambda e: e.tensor_copy(out=s16[s][:, 0:F], in_=s32[s][:, 0:F]))
            else:
                go = goff + (2 * pi if nm == "w_out" else 0)
                gv = g_sb[:, l, go:go + nk].unsqueeze(2).to_broadcast([128, nk, C])
                TR.op(en, [Bs32[s], Bconst], [Bs16[s]],
                      lambda e: e.tensor_tensor(
                          out=s16[s][:, 0:F].rearrange("p (k c) -> p k c", k=nk),
                          in0=s32[s][:, 0:F].rearrange("p (k c) -> p k c", k=nk),
                          in1=gv, op=ALU.mult))
            TR.dma(wscr[nm][l, pi], s16[s][:, 0:F], [Bs16[s]], [], sembuf=Bst[s], qn="act")
        TR.barrier()

        def chs(c):
            return slice(c * CH, (c + 1) * CH)

        bg = {"q": [], "t": 0, "pipe": {}}

        def bg_begin():
            bg["s32"] = [AR.alloc(2048 * 4, F32) for _ in range(2)]
            bg["s16"] = [AR.alloc(2048 * 2) for _ in range(2)]
            bg["B32"] = [Buf("b32_0"), Buf("b32_1")]
            bg["B16"] = [Buf("b16_0"), Buf("b16_1")]
            bg["Bst"] = [Buf("bst_0"), Buf("bst_1")]
            bg["pipe"] = {}

        def bg_step(allow_load=True):
            t = bg["t"]
            pipe = bg["pipe"]
            if (t - 2) in pipe:
                (l_, nm, pi, nk, C, goff, a, b) = pipe.pop(t - 2)
                i = (t - 2) % 2
                TR.dma(wscr[nm][l_, pi][:, a:b], bg["s16"][i][:, 0:b - a], [bg["B16"][i]], [], sembuf=bg["Bst"][i], qn="act")
            if (t - 1) in pipe:
                (l_, nm, pi, nk, C, goff, a, b) = pipe[t - 1]
                i = (t - 1) % 2
                F_ = b - a
                if goff is None:
                    TR.op("dve", [bg["B32"][i]], [bg["B16"][i]],
                          lambda e: e.tensor_copy(out=bg["s16"][i][:, 0:F_], in_=bg["s32"][i][:, 0:F_]))
                else:
                    go = goff + (2 * pi if nm == "w_out" else 0)
                    gv = g_sb[:, l_, go:go + nk].unsqueeze(2).to_broadcast([128, nk, C])
                    TR.op("dve", [bg["B32"][i], Bconst], [bg["B16"][i]],
                          lambda e: e.tensor_tensor(
                              out=bg["s16"][i][:, 0:F_].rearrange("p (k c) -> p k c", k=nk),
                              in0=bg["s32"][i][:, 0:F_].rearrange("p (k c) -> p k c", k=nk),
                              in1=gv, op=ALU.mult))
            if allow_load and bg["q"]:
                task = bg["q"].pop(0)
                (l_, nm, pi, nk, C, goff, a, b) = task
                i = t % 2
                TR.dma(bg["s32"][i][:, 0:b - a], wsrc[nm][l_, pi][:, a:b], [], [bg["B32"][i]], qn="act")
                pipe[t] = task
            bg["t"] = t + 1

        def bg_drain(everything):
            while (everything and bg["q"]) or bg["pipe"]:
                bg_step(allow_load=everything)

        def rsqrt_act(dst, Bdst, src, Bsrc, mult):
            TR.op("act", [Bsrc, Bconst], [Bdst],
                  lambda e: e.activation(out=dst, in_=src, func=AF.Ln, bias=eps_t[:, 0:1], scale=mult))
            TR.op("act", [Bdst], [Bdst],
                  lambda e: e.activation(out=dst, in_=dst, func=AF.Exp, scale=-0.5))

        def norm_a(c, sq, Bsq):
            cs = chs(c)
            TR.op("dve", [BxT[c]], [Bsq],
                  lambda e: e.tensor_tensor(out=sq[:, 0:3, :], in0=xT[:, 0:3, cs], in1=xT[:, 0:3, cs], op=ALU.mult))
            TR.op("pool", [BxT[c]], [Bsq],
                  lambda e: e.tensor_tensor(out=sq[:, 3:8, :], in0=xT[:, 3:8, cs], in1=xT[:, 3:8, cs], op=ALU.mult))

        def norm_chunk(c, sq, Bsq, rstd, Brstd, do_a=True):
            cs = chs(c)
            if do_a:
                norm_a(c, sq, Bsq)
            ps, Bps = gbank()

            def mm(e):
                for kc in range(8):
                    ins = e.matmul(ps[:, :], ones_bf[:, :], sq[:, kc, :], start=(kc == 0), stop=(kc == 7))
                return ins
            TR.op("pe", [Bsq, Bconst], [Bps], mm)
            rsqrt_act(rstd, Brstd, ps[:, :], Bps, 1.0 / D)
            TR.op("dve", [BxT[c], Brstd], [BuT[c]],
                  lambda e: e.tensor_tensor(out=uT[:, 0:5, cs], in0=xT[:, 0:5, cs],
                                            in1=rstd.unsqueeze(1).to_broadcast([128, 5, CH]), op=ALU.mult))
            TR.op("pool", [BxT[c], Brstd], [BuT[c]],
                  lambda e: e.tensor_tensor(out=uT[:, 5:8, cs], in0=xT[:, 5:8, cs],
                                            in1=rstd.unsqueeze(1).to_broadcast([128, 3, CH]), op=ALU.mult))

        def load_piece(dst, Bdst, nm, l, pi, F):
            TR.dma(dst[:, 0:F], wscr[nm][l, pi], [], [Bdst])

        ut_ready = [False]

        def ffn_phase(l, which, post_norm=False, pre_normed=False, bg_on=False):
            gu, dn = ("w_gu1", "w_d1") if which == 1 else ("w_gu2", "w_d2")
            AR.reset()
            hT = AR.alloc(NFC * CH * 2).rearrange("p (f t) -> p f t", f=NFC)
            BhT = Buf("hT")
            wg = [AR.alloc(2 * 2048 * 2).rearrange("p (g k c) -> p g k c", g=2, k=8) for _ in range(2)]
            Bwg = [Buf("wg%d" % i) for i in range(2)]
            wd = [AR.alloc(2816 * 2).rearrange("p (f c) -> p f c", f=NFC) for _ in range(2)]
            Bwd = [Buf("wd%d" % i) for i in range(2)]
            sq = AR.alloc(8 * CH * 2).rearrange("p (k t) -> p k t", k=8)
            Bsq = Buf("sq")
            rstd = AR.alloc(CH * 4, F32)
            Brstd = Buf("rstd")
            sg = [AR.alloc(CH * 4, F32) for _ in range(2)]
            Bsg = [Buf("sg%d" % i) for i in range(2)]
            bg_on = bg_on and bool(bg["q"])
            if bg_on:
                bg_begin()

            def load_gu(c, j):
                s = (c * 11 + j) % 2
                TR.dma(wg[s][:, 0], wscr[gu][l, j].rearrange("p (k c) -> p k c", k=8), [], [Bwg[s]])
                TR.dma(wg[s][:, 1], wscr[gu][l, 11 + j].rearrange("p (k c) -> p k c", k=8), [], [Bwg[s]])

            def load_d(c, dc):
                s = (c * 8 + dc) % 2
                TR.dma(wd[s], wscr[dn][l, dc].rearrange("p (f c) -> p f c", f=NFC), [], [Bwd[s]])

            if not pre_normed:
                norm_chunk(0, sq, Bsq, rstd, Brstd)
            for c in range(NCH):
                cs = chs(c)
                load_gu(c, 0)
                for j in range(11):
                    if j + 1 < 11:
                        load_gu(c, j + 1)
                    else:
                        load_d(c, 0)
                    if post_norm and c > 0:
                        if j == 1:
                            norm_a(c - 1, sq, Bsq)
                        if j == 4:
                            norm_chunk(c - 1, sq, Bsq, rstd, Brstd, do_a=False)
                    if (not pre_normed) and c + 1 < NCH:
                        if j == 6:
                            norm_a(c + 1, sq, Bsq)
                        if j == 9:
                            norm_chunk(c + 1, sq, Bsq, rstd, Brstd, do_a=False)
                    if bg_on:
                        bg_step()
                    s = (c * 11 + j) % 2
                    for fi in range(2):
                        fc = 2 * j + fi
                        psg, Bg = gbank()
                        psu, Bu = gbank()

                        def mmg(e, s=s, fi=fi, g=0, ps=psg):
                            for kc in range(8):
                                ins = e.matmul(ps[:, :], wg[s][:, g, kc, fi * 128:(fi + 1) * 128], uT[:, kc, cs],
                                               start=(kc == 0), stop=(kc == 7))
                            return ins
                        TR.op("pe", [Bwg[s], BuT[c]], [Bg], mmg)
                        TR.op("pe", [Bwg[s], BuT[c]], [Bu], lambda e, s=s, fi=fi, ps=psu: mmg(e, s, fi, 1, ps))
                        q = fc % 2
                        TR.op("act", [Bg], [Bsg[q]],
                              lambda e, q=q, ps=psg: e.activation(out=sg[q], in_=ps[:, :], func=AF.Silu))
                        TR.op("dve", [Bsg[q], Bu], [BhT],
                              lambda e, q=q, ps=psu, fc=fc: e.tensor_tensor(out=hT[:, fc, :], in0=sg[q], in1=ps[:, :],
                                                                            op=ALU.mult))
                for dc in range(8):
                    if dc + 1 < 8:
                        load_d(c, dc + 1)
                    if bg_on and dc % 2 == 0:
                        bg_step()
                    s = (c * 8 + dc) % 2
                    ps, Bps = gbank()

                    def mmd(e, s=s, ps=ps):
                        for fc in range(NFC):
                            ins = e.matmul(ps[:, :], wd[s][:, fc, :], hT[:, fc, :], start=(fc == 0), stop=(fc == NFC - 1))
                        return ins
                    TR.op("pe", [Bwd[s], BhT], [Bps], mmd)
                    TR.op("dve", [Bps, BxT[c]], [BxT[c]],
                          lambda e, ps=ps, dc=dc: e.scalar_tensor_tensor(out=xT[:, dc, cs], in0=ps[:, :], scalar=0.5,
                                                                         in1=xT[:, dc, cs], op0=ALU.mult, op1=ALU.add))
            if post_norm:
                norm_chunk(NCH - 1, sq, Bsq, rstd, Brstd)
                ut_ready[0] = (which == 1)
            if bg_on:
                bg_drain(everything=(which == 2))
            TR.barrier()

        def ple_phase(l, s_idx, pre_normed=False, post_norm=False):
            AR.reset()
            wpg = AR.alloc(4 * 2048 * 2).rearrange("p (j k c) -> p j k c", j=4, k=8)
            wpp = AR.alloc(2048 * 2).rearrange("p (k c) -> p k c", k=2)
            Bw = Buf("wple")
            sq = AR.alloc(8 * CH * 2).rearrange("p (k t) -> p k t", k=8)
            Bsq = Buf("sq")
            rstd = AR.alloc(CH * 4, F32)
            Brstd = Buf("rstd")
            pin = [AR.alloc(256 * 4, F32) for _ in range(2)]
            Bpin = [Buf("pin%d" % i) for i in range(2)]
            pT = AR.alloc(2 * CH * 2).rearrange("p (k t) -> p k t", k=2)
            BpT = Buf("pT")
            sg = [AR.alloc(CH * 4, F32) for _ in range(2)]
            Bsg = [Buf("sg%d" % i) for i in range(2)]
            tt = [AR.alloc(CH * 4, F32) for _ in range(2)]
            Btt = [Buf("tt%d" % i) for i in range(2)]
            for j in range(4):
                TR.dma(wpg[:, j], wscr["w_pg"][l, j].rearrange("p (k c) -> p k c", k=8), [], [Bw])
            TR.dma(wpp, wscr["w_pp"][l, 0].rearrange("p (k c) -> p k c", k=2), [], [Bw])
            for c in range(NCH):
                cs = chs(c)
                if not pre_normed:
                    norm_chunk(c, sq, Bsq, rstd, Brstd)
                pss = [gbank(), gbank()]
                for ti in range(4):
                    s = ti % 2
                    tok0 = c * CH + ti * 128
                    TR.dma(pin[s], p_d[l, s_idx, tok0:tok0 + 128, :], [], [Bpin[s]])
                    for pc in range(2):
                        TR.op("pe", [Bpin[s], Bconst], [pss[pc][1]],
                              lambda e, s=s, pc=pc, ti=ti: e.transpose(pss[pc][0][:, ti * 128:(ti + 1) * 128],
                                                                       pin[s][:, pc * 128:(pc + 1) * 128], ident[:, :]))
                for pc in range(2):
                    TR.op("act", [pss[pc][1]], [BpT],
                          lambda e, pc=pc: e.copy(out=pT[:, pc, :], in_=pss[pc][0][:, :]))
                for dc in range(8):
                    if post_norm and c > 0 and dc == 0:
                        norm_a(c - 1, sq, Bsq)
                    if post_norm and c > 0 and dc == 4:
                        norm_chunk(c - 1, sq, Bsq, rstd, Brstd, do_a=False)
                    psg, Bg = gbank()
                    psp, Bp = gbank()
                    j, co = dc // 2, (dc % 2) * 128

                    def mmg(e, ps=psg, j=j, co=co):
                        for kc in range(8):
                            ins = e.matmul(ps[:, :], wpg[:, j, kc, co:co + 128], uT[:, kc, cs], start=(kc == 0), stop=(kc == 7))
                        return ins

                    def mmp(e, ps=psp, dc=dc):
                        for pc in range(2):
                            ins = e.matmul(ps[:, :], wpp[:, pc, dc * 128:(dc + 1) * 128], pT[:, pc, :], start=(pc == 0), stop=(pc == 1))
                        return ins
                    TR.op("pe", [Bw, BuT[c]], [Bg], mmg)
                    TR.op("pe", [Bw, BpT], [Bp], mmp)
                    q = dc % 2
                    TR.op("act", [Bg], [Bsg[q]], lambda e, q=q, ps=psg: e.activation(out=sg[q], in_=ps[:, :], func=AF.Sigmoid))
                    TR.op("dve", [Bsg[q], Bp], [Btt[q]],
                          lambda e, q=q, ps=psp: e.tensor_tensor(out=tt[q], in0=sg[q], in1=ps[:, :], op=ALU.mult))
                    TR.op("pool", [Btt[q], BxT[c]], [BxT[c]],
                          lambda e, q=q, dc=dc: e.tensor_tensor(out=xT[:, dc, cs], in0=xT[:, dc, cs], in1=tt[q], op=ALU.add))
            if post_norm:
                norm_chunk(NCH - 1, sq, Bsq, rstd, Brstd)
            TR.barrier()

        def mixer_phase(l):
            AR.reset()
            sq = AR.alloc(8 * CH * 2).rearrange("p (k t) -> p k t", k=8)
            Bsq = Buf("sq")
            rstd = AR.alloc(CH * 4, F32)
            Brstd = Buf("rstd")
            if not ut_ready[0]:
                for c in range(NCH):
                    norm_chunk(c, sq, Bsq, rstd, Brstd)
                TR.barrier()
            ut_ready[0] = False
            for gi, mx in enumerate(("na", "swa", "dil", "mla")):
                if mx in flags:
                    one_mixer(l, gi, mx)
                    TR.barrier()

        def one_mixer(l, gi, mx):
            AR.reset()
            is_mla = mx == "mla"
            dk = 96 if is_mla else 64
            scale = dk ** -0.5
            nkt = 4 if is_mla else 2
            KT = [AR.alloc(T * 2) for _ in range(nkt)]
            BKT = Buf("KT")
            Vflat = AR.alloc(16 * 4 * 65 * 2)
            V = Vflat[:, 0:16 * 4 * 65].rearrange("p (k h d) -> p k h d", k=16, h=4)
            BV = Buf("V")
            nqt = 4
            QT = [[AR.alloc(CH * 2) for _ in range(nqt)] for _ in range(2)]
            BQT = [Buf("QT0"), Buf("QT1")]
            NPT = 5
            PT = [AR.alloc(CH * 2) for _ in range(NPT)]
            BPT = [Buf("PT%d" % i) for i in range(NPT)]
            ych = [AR.alloc(4 * 256 * 4, F32).rearrange("p (j h d) -> p j h d", j=4, h=4)] * 2
            Bych = [Buf("y0")] * 2
            if not is_mla:
                ysq = AR.alloc(4 * 256 * 4, F32).rearrange("p (j c) -> p j c", j=4)
                Bysq = Buf("ysq")
            ynT = [AR.alloc(2 * CH * 2).rearrange("p (k t) -> p k t", k=2)] * 2
            BynT = [Buf("ynT0")] * 2
            wst = [AR.alloc(2048 * 2).rearrange("p (k c) -> p k c", k=8) for _ in range(2)]
            Bwst = [Buf("wst0"), Buf("wst1")]
            wq = [AR.alloc(2048 * 2).rearrange("p (k c) -> p k c", k=8) for _ in range(2 if mx in ("swa", "dil") else 1)]
            Bwq = Buf("wq")
            wo = AR.alloc(2048 * 2).rearrange("p (k c) -> p k c", k=2)
            Bwo = Buf("wo")
            sqs = [AR.alloc(CH * 2) for _ in range(2)]
            Bsqs = [Buf("sqs0"), Buf("sqs1")]
            if mx != "na":
                r1 = [AR.alloc(CH * 4, F32)] * 2
                r2 = [AR.alloc(CH * 4, F32)] * 2
                Br = [Buf("r0")] * 2
            if is_mla:
                wuq = AR.alloc(1536 * 2).rearrange("p (k h a d) -> p k h a d", k=2, h=4, a=2)
                wukv = AR.alloc(512 * 2)
                Bwm = Buf("wmla")
                cqn = [AR.alloc(2 * CH * 2).rearrange("p (k t) -> p k t", k=2) for _ in range(2)]
                Bcqn = [Buf("cqn0"), Buf("cqn1")]
                ckvn = AR.alloc(T * 2)
                Bckvn = Buf("ckvn")
                lat32 = AR.alloc(2 * CH * 4, F32).rearrange("p (k t) -> p k t", k=2)
                Blat = Buf("lat32")
                ysq = lat32.rearrange("p k t -> p (k t)").rearrange("p (j c) -> p j c", j=4)
                Bysq = Blat
                lrs = AR.alloc(CH * 4, F32)
                Blrs = Buf("lrs")
                lsq = AR.alloc(2 * CH * 2).rearrange("p (k t) -> p k t", k=2)
                Blsq = Buf("lsq")
            pv_T = mx in ("dil", "mla")
            if mx == "dil":
                otmp = AR.alloc(CH * 4, F32)
                Botmp = Buf("otmp")
            elif is_mla:
                otmp, Botmp = lrs, Blrs
            if mx == "na":
                nam = AR.alloc(4 * 2 * 1024 * 2).rearrange("p (h v c) -> p h v c", h=4, v=2)
                Bnam = Buf("nam")
                nst = AR.alloc(1024 * 4, F32)
                Bnst = Buf("nst")
            wrr = [0]

            def wslot():
                i = wrr[0] % 2
                wrr[0] += 1
                return wst[i], Bwst[i]

            TR.dma(wo, wscr["w_out"][l, gi].rearrange("p (k c) -> p k c", k=2), [], [Bwo])
            TR.op("pool", [], [BV], lambda e: e.memset(Vflat, 1.0))

            def proj_fm(wt, Bw, col0, M, rhs_fn, nk, extra_reads, N=CH):
                ps, Bps = gbank()

                def mm(e):
                    for kc in range(nk):
                        ins = e.matmul(ps[0:M, 0:N], wt[:, kc, col0:col0 + M], rhs_fn(kc), start=(kc == 0), stop=(kc == nk - 1))
                    return ins
                TR.op("pe", [Bw] + extra_reads, [Bps], mm)
                return ps, Bps

            def sumsq_max(src, Bsrc, rows, lhs, stat_col, part_col):
                i = alt("0", "1") == "0"
                i = 0 if i else 1
                TR.op("pool", [Bsrc], [Bsqs[i]],
                      lambda e: e.tensor_tensor(out=sqs[i][rows, :], in0=src[rows, :], in1=src[rows, :], op=ALU.mult))
                ps, Bps = gbank()
                TR.op("pe", [Bsqs[i], Bconst], [Bps],
                      lambda e: e.matmul(ps[:, :], lhs[rows, :], sqs[i][rows, :], start=True, stop=True))
                TR.op("dve", [Bps], [Bstat],
                      lambda e: e.reduce_max(out=stat[:, part_col:part_col + 1], in_=ps[:, :], axis=AX.X))
                if stat_col is not None:
                    TR.op("dve", [Bstat], [Bstat],
                          lambda e: e.tensor_tensor(out=stat[:, stat_col:stat_col + 1], in0=stat[:, stat_col:stat_col + 1],
                                                    in1=stat[:, part_col:part_col + 1], op=ALU.max))

            def rope_evac(psA, BA, psB, BB, rows, ti, cs, dst, Bdst, dsts=None):
                i = alt("0", "1") == "0"
                i = 0 if i else 1
                TR.op("dve", [BA, Bconst], [Br[i]],
                      lambda e: e.tensor_tensor(out=r1[i][rows, :], in0=psA[rows, :], in1=rope[rows, ti, cs], op=ALU.mult))
                TR.op("dve", [BB, Bconst, Br[i]], [Br[i]],
                      lambda e: e.tensor_tensor(out=r2[i][rows, :], in0=psB[rows, :], in1=rope[rows, ti + 1, cs], op=ALU.mult))
                for (d_, rw) in (dsts if dsts is not None else [(dst, rows)]):
                    TR.op("pool", [Br[i]], [Bdst],
                          lambda e: e.tensor_tensor(out=d_[rw, :], in0=r1[i][rw, :], in1=r2[i][rw, :], op=ALU.add))

            full = slice(0, 128)
            TR.op("dve", [], [Bstat], lambda e: e.memset(stat[:, 0:4], 0.0))
            if not is_mla:
                for qb_ in range(2):
                    for h in range(4):
                        TR.op("pool", [], [BQT[qb_]], lambda e, h=h: e.memset(QT[qb_][h], 0.0))

            if not is_mla:
                roped = mx in ("swa", "dil")
                pk = {"na": P_NA_K, "swa": P_SW_K, "dil": P_DI_K}[mx]
                pv = {"na": P_NA_V, "swa": P_SW_V, "dil": P_DI_V}[mx]
                nkt_used = 1 if mx == "swa" else 2
                wk, Bwk = wslot()
                load_piece(wk.rearrange("p k c -> p (k c)"), Bwk, "w_in", l, pk, 2048)
                if mx == "dil":
                    wks, Bwks = wslot()
                    load_piece(wks.rearrange("p k c -> p (k c)"), Bwks, "w_in", l, P_DI_KS, 2048)
                for t in range(nkt_used):
                    for c in range(NCH):
                        cs = chs(c)
                        psA, BA = proj_fm(wk, Bwk, t * 128, 128, lambda kc: uT[:, kc, cs], 8, [BuT[c]])
                        if roped:
                            if mx == "swa":
                                psB, BB = proj_fm(wk, Bwk, 128, 128, lambda kc: uT[:, kc, cs], 8, [BuT[c]])
                            else:
                                psB, BB = proj_fm(wks, Bwks, t * 128, 128, lambda kc: uT[:, kc, cs], 8, [BuT[c]])
                            rope_evac(psA, BA, psB, BB, full, 0, cs, KT[t][:, cs], BKT)
                        else:
                            TR.op("act", [BA], [BKT], lambda e, t=t, cs=cs, ps=psA: e.copy(out=KT[t][:, cs], in_=ps[:, :]))
                wv, Bwv = wslot()
                load_piece(wv.rearrange("p k c -> p (k c)"), Bwv, "w_in", l, pv, 2048)
                nvh = 2 if mx == "swa" else 4
                for kt in range(16):
                    ps, Bps = gbank()

                    def mmv(e, ps=ps, kt=kt):
                        for kc in range(8):
                            ins = e.matmul(ps[:, 0:nvh * 64], uT[:, kc, kt * 128:(kt + 1) * 128], wv[:, kc, 0:nvh * 64],
                                           start=(kc == 0), stop=(kc == 7))
                        return ins
                    TR.op("pe", [Bwv, BuT[kt // 4]], [Bps], mmv)
                    TR.op("act", [Bps], [BV],
                          lambda e, ps=ps, kt=kt: e.copy(out=V[:, kt, 0:nvh, 0:64],
                                                         in_=ps[:, 0:nvh * 64].rearrange("p (h d) -> p h d", h=nvh)))
                for t in range(nkt_used):
                    for c in range(NCH):
                        cs = chs(c)
                        sumsq_max(KT[t][:, cs], BKT, full, onesA, 2 * t, 12)
                        sumsq_max(KT[t][:, cs], BKT, full, onesB, 2 * t + 1, 13)
            else:
                TR.dma(wuq.rearrange("p k h a d -> p (k h a d)"), wscr["w_uq"][l, 0], [], [Bwm])
                TR.dma(wukv, wscr["w_ukv"][l, 0], [], [Bwm])
                wckv, Bwckv = wslot()
                load_piece(wckv.rearrange("p k c -> p (k c)"), Bwckv, "w_in", l, P_ML_CKV, 2048)
                wkpe, Bwkpe = wslot()
                load_piece(wkpe.rearrange("p k c -> p (k c)"), Bwkpe, "w_in", l, P_ML_KPE, 2048)
                R = slice(64, 96)

                def mla_A(c):
                    cs = chs(c)
                    lb = c % 2
                    ps, Bps = proj_fm(wckv, Bwckv, 0, 128, lambda kc: uT[:, kc, cs], 8, [BuT[c]])
                    TR.op("act", [Bps], [Blat], lambda e: e.copy(out=lat32[:, lb, :], in_=ps[:, :]))
                    psA, BA = proj_fm(wkpe, Bwkpe, 0, 128, lambda kc: uT[:, kc, cs], 8, [BuT[c]])
                    psB, BB = proj_fm(wkpe, Bwkpe, 128, 128, lambda kc: uT[:, kc, cs], 8, [BuT[c]])
                    TR.op("pool", [Blat], [Bsqs[0]],
                          lambda e: e.tensor_tensor(out=sqs[0], in0=lat32[:, lb, :], in1=lat32[:, lb, :], op=ALU.mult))
                    ps2, Bps2 = gbank()
                    TR.op("pe", [Bsqs[0], Bconst], [Bps2],
                          lambda e: e.matmul(ps2[:, :], ones_bf[:, :], sqs[0], start=True, stop=True))
                    rope_evac(psA, BA, psB, BB, R, 2, cs, KT[0][:, cs], BKT)
                    rsqrt_act(lrs, Blrs, ps2[:, :], Bps2, 1.0 / 128)
                    TR.op("dve", [Blat, Blrs], [Bckvn],
                          lambda e: e.tensor_tensor(out=ckvn[:, cs], in0=lat32[:, lb, :], in1=lrs, op=ALU.mult))
                    for h in range(1, 4):
                        TR.op("pool", [BKT], [BKT],
                              lambda e, h=h: e.tensor_copy(out=KT[h][R, cs], in_=KT[0][R, cs]))

                def mla_B(c):
                    cs = chs(c)
                    for h in range(4):
                        ps, Bps = gbank()
                        TR.op("pe", [Bwm, Bckvn], [Bps],
                              lambda e, ps=ps, h=h: e.matmul(ps[0:64, :], wukv[:, h * 64:(h + 1) * 64], ckvn[:, cs],
                                                             start=True, stop=True))
                        TR.op("act", [Bps], [BKT],
                              lambda e, ps=ps, h=h: e.copy(out=KT[h][0:64, cs], in_=ps[0:64, :]))
                    for ti in range(4):
                        kt = c * 4 + ti
                        ps, Bps = gbank()
                        TR.op("pe", [Bwm, Bckvn], [Bps],
                              lambda e, ps=ps, kt=kt: e.matmul(ps[:, 0:256], ckvn[:, kt * 128:(kt + 1) * 128], wukv[:, 256:512],
                                                               start=True, stop=True))
                        TR.op("act", [Bps], [BV],
                              lambda e, ps=ps, kt=kt: e.copy(out=V[:, kt, :, 0:64],
                                                             in_=ps[:, 0:256].rearrange("p (h d) -> p h d", h=4)))
                    for h in range(4):
                        sumsq_max(KT[h][:, cs], BKT, slice(0, 96), ones_bf, h, 12)

                mla_A(0)
                for c in range(NCH):
                    if c + 1 < NCH:
                        mla_A(c + 1)
                    mla_B(c)

            if mx == "na":
                for h in range(4):
                    for v in range(2):
                        TR.dma(nst, nag_d[l, h, v], [], [Bnst])
                        TR.op("act", [Bnst], [Bnam], lambda e, h=h, v=v: e.activation(out=nam[:, h, v, :], in_=nst, func=AF.Exp))
            if not is_mla:
                pq = {"na": P_NA_Q, "swa": P_SW_Q, "dil": P_DI_Q}[mx]
                TR.dma(wq[0].rearrange("p k c -> p (k c)"), wscr["w_in"][l, pq], [], [Bwq])
                if mx in ("swa", "dil"):
                    pqs = {"swa": P_SW_QS, "dil": P_DI_QS}[mx]
                    TR.dma(wq[1].rearrange("p k c -> p (k c)"), wscr["w_in"][l, pqs], [], [Bwq])
            else:
                TR.dma(wq[0].rearrange("p k c -> p (k c)"), wscr["w_in"][l, P_ML_CQ], [], [Bwq])

            def head_map(h):
                if is_mla:
                    return h, slice(0, 96), h, h
                if mx == "swa":
                    half = h // 2
                    return h, slice(half * 64, half * 64 + 64), 0, half
                return h, slice((h % 2) * 64, (h % 2) * 64 + 64), h // 2, h

            def qtile_of(h):
                return (h % 2) if mx == "swa" else (h // 2)

            def pairs(qc, h):
                out = []
                for kt in range(16):
                    js = []
                    for j in range(4):
                        qt = 4 * qc + j
                        if mx == "mla":
                            ok = True
                        elif mx == "swa":
                            ok = abs(qt - kt) <= 1
                        elif mx == "dil":
                            ok = abs(qt - kt) <= 8
                        else:
                            if qt <= 1:
                                ok = kt <= 3
                            elif qt >= 14:
                                ok = kt >= 12
                            else:
                                ok = abs(qt - kt) <= 2
                        if ok:
                            js.append(j)
                    if js:
                        out.append((kt, js[0], js[-1]))
                return out

            def mask_ap(kt, qc, j0, j1, h):
                n = (j1 - j0 + 1) * 128
                q0 = (4 * qc + j0) * 128
                if mx == "swa":
                    o = q0 - 128 * kt + 128
                    return [(0, n, smask[:, o:o + n])]
                if mx == "dil":
                    o = q0 - 128 * kt + 1024
                    return [(0, n, dmask[:, o:o + n])]
                if mx == "na":
                    res = []
                    j = j0
                    while j <= j1:
                        qt = 4 * qc + j
                        v = 1 if (qt <= 1 or qt >= 14) else 0
                        j2 = j
                        while j2 + 1 <= j1 and (1 if (4 * qc + j2 + 1 <= 1 or 4 * qc + j2 + 1 >= 14) else 0) == v:
                            j2 += 1
                        o = (7 - 2 * (kt - qt)) * 64
                        w = (j2 - j + 1) * 128
                        res.append(((j - j0) * 128, w, nam[:, h, v, o:o + w]))
                        j = j2 + 1
                    return res
                return []

            def ss_sq(src, Bsrc, rows, i):
                TR.op("pool", [Bsrc], [Bsqs[i]],
                      lambda e: e.tensor_tensor(out=sqs[i][rows, :], in0=src[rows, :], in1=src[rows, :], op=ALU.mult))

            def ss_mm(rows, i, col):
                ps, Bps = gbank()
                TR.op("pe", [Bsqs[i], Bconst], [Bps],
                      lambda e: e.matmul(ps[:, :], ones_bf[rows, :], sqs[i][rows, :], start=True, stop=True))
                TR.op("dve", [Bps], [Bstat],
                      lambda e: e.reduce_max(out=stat[:, col:col + 1], in_=ps[:, :], axis=AX.X))

            def q_stages(qc):
                cs = chs(qc)
                qb = qc % 2
                SO = 32 * qb
                srows = slice(0, 96) if is_mla else full
                stages = []
                if not is_mla:
                    roped = mx in ("swa", "dil")

                    def s_proj():
                        for t in range(2):
                            hh = [h for h in range(4) if qtile_of(h) == t]
                            hh.sort(key=lambda h: head_map(h)[1].start)
                            dsts = [(QT[qb][h], head_map(h)[1]) for h in hh]
                            psA, BA = proj_fm(wq[0], Bwq, t * 128, 128, lambda kc: uT[:, kc, cs], 8, [BuT[qc]])
                            if roped:
                                psB, BB = proj_fm(wq[1], Bwq, t * 128, 128, lambda kc: uT[:, kc, cs], 8, [BuT[qc]])
                                rope_evac(psA, BA, psB, BB, full, 0, cs, None, BQT[qb], dsts=dsts)
                            else:
                                for (d_, rw) in dsts:
                                    TR.op("act", [BA], [BQT[qb]], lambda e, ps=psA: e.copy(out=d_[rw, :], in_=ps[rw, :]))
                    stages.append(s_proj)

                    def k2col(h):
                        _, rows, kt_, _ = head_map(h)
                        return 2 * kt_ + (1 if rows.start == 64 else 0)
                else:
                    cb = qc % 2

                    def s_lat():
                        for t in range(2):
                            ps, Bps = proj_fm(wq[0], Bwq, t * 128, 128, lambda kc: uT[:, kc, cs], 8, [BuT[qc]])
                            TR.op("act", [Bps], [Blat], lambda e, ps=ps, t=t: e.copy(out=lat32[:, t, :], in_=ps[:, :]))
                        TR.op("pool", [Blat], [Blsq],
                              lambda e: e.tensor_tensor(out=lsq, in0=lat32, in1=lat32, op=ALU.mult))

                    def s_norm():
                        ps2, Bps2 = gbank()

                        def mmss(e):
                            for kc in range(2):
                                ins = e.matmul(ps2[:, :], ones_bf[:, :], lsq[:, kc, :], start=(kc == 0), stop=(kc == 1))
                            return ins
                        TR.op("pe", [Blsq, Bconst], [Bps2], mmss)
                        rsqrt_act(lrs, Blrs, ps2[:, :], Bps2, 1.0 / 256)
                        TR.op("dve", [Blat, Blrs], [Bcqn[cb]],
                              lambda e: e.tensor_tensor(out=cqn[cb], in0=lat32,
                                                        in1=lrs.unsqueeze(1).to_broadcast([128, 2, CH]), op=ALU.mult))

                    def s_uq():
                        for h in range(4):
                            psA, BA = gbank()
                            psB, BB = gbank()

                            def mmq(e, ps, a, h=h):
                                for kc in range(2):
                                    ins = e.matmul(ps[0:96, :], wuq[:, kc, h, a, :], cqn[cb][:, kc, :], start=(kc == 0), stop=(kc == 1))
                                return ins
                            TR.op("pe", [Bwm, Bcqn[cb]], [BA], lambda e, ps=psA: mmq(e, ps, 0))
                            TR.op("pe", [Bwm, Bcqn[cb]], [BB], lambda e, ps=psB: mmq(e, ps, 1))
                            TR.op("act", [BA], [BQT[qb]], lambda e, ps=psA, h=h: e.copy(out=QT[qb][h][0:64, :], in_=ps[0:64, :]))
                            rope_evac(psA, BA, psB, BB, slice(64, 96), 2, cs, QT[qb][h], BQT[qb])
                    stages += [s_lat, s_norm, s_uq]

                    def k2col(h):
                        return h

                def s_sq01():
                    ss_sq(QT[qb][0], BQT[qb], srows, 0)
                    ss_sq(QT[qb][1], BQT[qb], srows, 1)

                def s_mm01():
                    ss_mm(srows, 0, SO + 4)
                    ss_mm(srows, 1, SO + 5)
                    ss_sq(QT[qb][2], BQT[qb], srows, 0)
                    ss_sq(QT[qb][3], BQT[qb], srows, 1)

                def s_mm23():
                    ss_mm(srows, 0, SO + 6)
                    ss_mm(srows, 1, SO + 7)

                def s_negm():
                    for h in range(4):
                        TR.op("dve", [Bstat], [Bstat],
                              lambda e, h=h: e.tensor_tensor(out=stat[:, SO + 8 + h:SO + 9 + h], in0=stat[:, SO + 4 + h:SO + 5 + h],
                                                             in1=stat[:, k2col(h):k2col(h) + 1], op=ALU.mult))
                    TR.op("act", [Bstat], [Bstat],
                          lambda e: e.activation(out=stat[:, SO + 8:SO + 12], in_=stat[:, SO + 8:SO + 12], func=AF.Ln))
                    TR.op("act", [Bstat], [Bstat],
                          lambda e: e.activation(out=stat[:, SO + 8:SO + 12], in_=stat[:, SO + 8:SO + 12], func=AF.Exp, scale=0.5))
                    TR.op("dve", [Bstat], [Bstat],
                          lambda e: e.tensor_scalar(out=stat[:, SO + 8:SO + 12], in0=stat[:, SO + 8:SO + 12], scalar1=-scale,
                                                    scalar2=None, op0=ALU.mult))
                    if mx == "swa":
                        for h in range(4):
                            TR.op("act", [Bstat, Bconst], [Bstat],
                                  lambda e, h=h: e.activation(out=stat[:, SO + 16 + h:SO + 17 + h], in_=sink_sb[:, l, h:h + 1],
                                                              func=AF.Exp, bias=stat[:, SO + 8 + h:SO + 9 + h], scale=1.0))
                stages += [s_sq01, s_mm01, s_mm23, s_negm]
                return stages

            def attention(qc, inject):
                cs = chs(qc)
                qb = qc % 2
                SO = 32 * qb
                yb = 0
                items = []
                for h in range(4):
                    prs = pairs(qc, h)
                    for ii, (kt, j0, j1) in enumerate(prs):
                        items.append((h, kt, j0, j1, ii == 0, ii == len(prs) - 1))
                DEPTH = NPT - 1
                hob = {}
                hwr = {}
                slots = {}

                def emit_score(i):
                    h, kt, j0, j1, _, _ = items[i]
                    qt_i, rows, kt_i, vh = head_map(h)
                    n = (j1 - j0 + 1) * 128
                    crow = rows if is_mla else full
                    ps, Bps = gbank()
                    TR.op("pe", [BKT, BQT[qb]], [Bps],
                          lambda e: e.matmul(ps[:, 0:n], KT[kt_i][crow, kt * 128:(kt + 1) * 128],
                                             QT[qb][qt_i][crow, j0 * 128:j0 * 128 + n], start=True, stop=True))
                    rr["pt"] = rr.get("pt", 0) + 1
                    pi = rr["pt"] % NPT
                    slots[i] = pi
                    TR.op("act", [Bps, Bstat], [BPT[pi]],
                          lambda e: e.activation(out=PT[pi][:, 0:n], in_=ps[:, 0:n], func=AF.Exp,
                                                 bias=stat[:, SO + 8 + h:SO + 9 + h], scale=scale))
                    for (o, w, map_) in mask_ap(kt, qc, j0, j1, h):
                        rr["mk"] = rr.get("mk", 0) + 1
                        en = "pool" if rr["mk"] % 4 == 0 else "dve"
                        rd = [Bconst] if mx != "na" else [Bnam]
                        TR.op(en, [BPT[pi]] + rd, [BPT[pi]],
                              lambda e: e.tensor_tensor(out=PT[pi][:, o:o + w], in0=PT[pi][:, o:o + w], in1=map_, op=ALU.mult))

                def emit_pv(i):
                    h, kt, j0, j1, first, last = items[i]
                    qt_i, rows, kt_i, vh = head_map(h)
                    pi = slots[i]
                    if first:
                        hob[h] = obank()
                    ob, Bob = hob[h]
                    n = (j1 - j0 + 1) * 128

                    def pv_t(e):
                        wr = hwr.setdefault(h, set())
                        runs = []
                        for j in range(j0, j1 + 1):
                            st_ = j in wr
                            if runs and runs[-1][2] == st_:
                                runs[-1][1] = j
                            else:
                                runs.append([j, j, st_])
                            wr.add(j)
                        for ri, (a_, b_, _) in enumerate(runs):
                            ins = e.matmul(ob[0:65, a_ * 128:(b_ + 1) * 128], V[:, kt, vh, :],
                                           PT[pi][:, (a_ - j0) * 128:(b_ - j0 + 1) * 128],
                                           start=(first and ri == 0), stop=(last and ri == len(runs) - 1))
                        return ins

                    def pv(e):
                        for j in range(j0, j1 + 1):
                            ins = e.matmul(ob[:, j * 128:j * 128 + 65], PT[pi][:, (j - j0) * 128:(j - j0 + 1) * 128],
                                           V[:, kt, vh, :], start=(first and j == j0), stop=(last and j == j1))
                        return ins
                    TR.op("pe", [BPT[pi], BV], [Bob], pv_t if pv_T else pv)
                    if last and pv_T:
                        TR.op("act", [Bob], [Botmp], lambda e: e.copy(out=otmp[0:65, :], in_=ob[0:65, :]))
                        tb, Btb = gbank()

                        def tps(e):
                            for j in range(4):
                                ins = e.transpose(tb[:, j * 128:j * 128 + 65], otmp[0:65, j * 128:(j + 1) * 128], ident[0:65, 0:65])
                            return ins
                        TR.op("pe", [Botmp, Bconst], [Btb], tps)
                        ob, Bob = tb, Btb
                    if last:
                        obv = ob[:, :].rearrange("p (j c) -> p j c", j=4)
                        if mx == "swa":
                            TR.op("dve", [Bob, Bstat], [Bstat],
                                  lambda e: e.tensor_scalar(out=stat[:, 20:24], in0=obv[:, :, 64], scalar1=stat[:, SO + 16 + h:SO + 17 + h],
                                                            scalar2=None, op0=ALU.add))
                            TR.op("dve", [Bstat], [Bstat], lambda e: e.reciprocal(out=stat[:, 20:24], in_=stat[:, 20:24]))
                        else:
                            TR.op("dve", [Bob], [Bstat], lambda e: e.reciprocal(out=stat[:, 20:24], in_=obv[:, :, 64]))
                        TR.op("dve", [Bob, Bstat], [Bych[yb]],
                              lambda e: e.tensor_tensor(out=ych[yb][:, :, h, :], in0=obv[:, :, 0:64],
                                                        in1=stat[:, 20:24].unsqueeze(2).to_broadcast([128, 4, 64]),
                                                        op=ALU.mult))

                for i in range(len(items) + DEPTH):
                    if i < len(items):
                        emit_score(i)
                    if i - DEPTH >= 0:
                        emit_pv(i - DEPTH)
                    while inject and inject[0][0] <= i:
                        inject.pop(0)[1]()
                while inject:
                    inject.pop(0)[1]()

            def fin1(qc):
                yb = 0
                yv = ych[yb].rearrange("p j h d -> p j (h d)")
                TR.op("pool", [Bych[yb]], [Bysq], lambda e, yv=yv: e.tensor_tensor(out=ysq, in0=yv, in1=yv, op=ALU.mult))
                TR.op("dve", [Bysq], [Bstat], lambda e: e.reduce_sum(out=stat[:, 24:28], in_=ysq, axis=AX.X))
                rsqrt_act(stat[:, 24:28], Bstat, stat[:, 24:28], Bstat, 1.0 / 256)
                TR.op("dve", [Bych[yb], Bstat], [Bych[yb]],
                      lambda e, yv=yv: e.tensor_tensor(out=yv, in0=yv, in1=stat[:, 24:28].unsqueeze(2).to_broadcast([128, 4, 256]),
                                                       op=ALU.mult))

            def fin2(qc):
                cs = chs(qc)
                yb = 0
                yv = ych[yb].rearrange("p j h d -> p j (h d)")
                for cc in range(2):
                    ps, Bps = gbank()

                    def tp(e, ps=ps, cc=cc):
                        for j in range(4):
                            ins = e.transpose(ps[:, j * 128:(j + 1) * 128], yv[:, j, cc * 128:(cc + 1) * 128], ident[:, :])
                        return ins
                    TR.op("pe", [Bych[yb], Bconst], [Bps], tp)
                    TR.op("act", [Bps], [BynT[yb]], lambda e, ps=ps, cc=cc: e.copy(out=ynT[yb][:, cc, :], in_=ps[:, :]))
                for dc in range(8):
                    ps, Bps = gbank()

                    def mmo(e, ps=ps, dc=dc):
                        for cc in range(2):
                            ins = e.matmul(ps[:, :], wo[:, cc, dc * 128:(dc + 1) * 128], ynT[yb][:, cc, :], start=(cc == 0), stop=(cc == 1))
                        return ins
                    TR.op("pe", [Bwo, BynT[yb]], [Bps], mmo)
                    TR.op("dve", [Bps, BxT[qc]], [BxT[qc]],
                          lambda e, ps=ps, dc=dc: e.tensor_tensor(out=xT[:, dc, cs], in0=ps[:, :], in1=xT[:, dc, cs], op=ALU.add))

            for st_ in q_stages(0):
                st_()
            for qc in range(NCH):
                inj = []
                if qc > 0:
                    inj.append((3, lambda qc=qc: fin2(qc - 1)))
                if qc + 1 < NCH:
                    for k_, st_ in enumerate(q_stages(qc + 1)):
                        inj.append((6 + 3 * k_, st_))
                attention(qc, inj)
                fin1(qc)
            fin2(NCH - 1)

        for s_idx in range(n_seq):
            AR.reset()
            xin = [AR.alloc(D * 4, F32) for _ in range(2)]
            Bxin = [Buf("xin0"), Buf("xin1")]
            for tt in range(16):
                s = tt % 2
                c = tt // 4
                TR.dma(xin[s], x_d[s_idx, tt * 128:(tt + 1) * 128, :], [], [Bxin[s]])
                for hb in range(2):
                    ps, Bps = gbank()

                    def tp(e, ps=ps, s=s, hb=hb):
                        for i in range(4):
                            kc = hb * 4 + i
                            ins = e.transpose(ps[:, i * 128:(i + 1) * 128], xin[s][:, kc * 128:(kc + 1) * 128], ident[:, :])
                        return ins
                    TR.op("pe", [Bxin[s], Bconst], [Bps], tp)
                    TR.op(alt("act", "dve"), [Bps], [BxT[c]],
                          lambda e, ps=ps, hb=hb, tt=tt: (e.copy if e is nc.scalar else e.tensor_copy)(
                              out=xT[:, hb * 4:hb * 4 + 4, tt * 128:(tt + 1) * 128],
                              in_=ps[:, :].rearrange("p (k t) -> p k t", k=4)))
            TR.barrier()
            carry = False
            for l in range(L):
                has_mix = bool(flags & {"na", "swa", "dil", "mla"})
                bg_on = BG_OK and s_idx == 0 and (l + 1) in bg_tasks
                if bg_on:
                    bg["q"] = list(bg_tasks[l + 1])
                if "ffn1" in flags:
                    ffn_phase(l, 1, post_norm=has_mix, pre_normed=carry, bg_on=bg_on)
                    carry = False
                if has_mix:
                    mixer_phase(l)
                if "ffn2" in flags:
                    ffn_phase(l, 2, post_norm=("ple" in flags), bg_on=bg_on)
                if "ple" in flags:
                    nxt = ("ffn1" in flags) and (l + 1 < L)
                    ple_phase(l, s_idx, pre_normed=("ffn2" in flags), post_norm=nxt)
                    carry = nxt
            AR.reset()
            sq = AR.alloc(8 * CH * 2).rearrange("p (k t) -> p k t", k=8)
            Bsq = Buf("sq")
            rstd = AR.alloc(CH * 4, F32)
            Brstd = Buf("rstd")
            yn = [AR.alloc(8 * 128 * 4, F32).rearrange("p (k t) -> p k t", k=8) for _ in range(2)]
            Byn = [Buf("yn0"), Buf("yn1")]
            yo = [AR.alloc(D * 4, F32) for _ in range(2)]
            Byo = [Buf("yo0"), Buf("yo1")]
            Bout = [Buf("out0"), Buf("out1")]
            for c in range(NCH):
                cs = chs(c)
                TR.op("pool", [BxT[c]], [Bsq],
                      lambda e, cs=cs: e.tensor_tensor(out=sq, in0=xT[:, :, cs], in1=xT[:, :, cs], op=ALU.mult))
                ps, Bps = gbank()

                def mm(e, ps=ps):
                    for kc in range(8):
                        ins = e.matmul(ps[:, :], ones_bf[:, :], sq[:, kc, :], start=(kc == 0), stop=(kc == 7))
                    return ins
                TR.op("pe", [Bsq, Bconst], [Bps], mm)
                rsqrt_act(rstd, Brstd, ps[:, :], Bps, 1.0 / D)
                for ti in range(4):
                    tt = c * 4 + ti
                    s = tt % 2
                    tsl = slice(tt * 128, (tt + 1) * 128)
                    TR.op("dve", [BxT[c], Brstd], [Byn[s]],
                          lambda e, s=s, tsl=tsl, ti=ti: e.tensor_tensor(
                              out=yn[s], in0=xT[:, :, tsl],
                              in1=rstd[:, ti * 128:(ti + 1) * 128].unsqueeze(1).to_broadcast([128, 8, 128]), op=ALU.mult))
                    TR.op("pool", [Byn[s], Bconst], [Byn[s]],
                          lambda e, s=s: e.tensor_tensor(out=yn[s], in0=yn[s], in1=gfin[:, :].unsqueeze(2).to_broadcast([128, 8, 128]),
                                                         op=ALU.mult))
                    for hb in range(2):
                        ps, Bps = gbank()

                        def tp(e, ps=ps, s=s, hb=hb):
                            for i in range(4):
                                ins = e.transpose(ps[:, i * 128:(i + 1) * 128], yn[s][:, hb * 4 + i, :], ident[:, :])
                            return ins
                        TR.op("pe", [Byn[s], Bconst], [Bps], tp)
                        TR.op(alt("act", "dve"), [Bps], [Byo[s]],
                              lambda e, ps=ps, s=s, hb=hb: (e.copy if e is nc.scalar else e.tensor_copy)(
                                  out=yo[s][:, hb * 512:(hb + 1) * 512], in_=ps[:, :]))
                    TR.dma(y_d[s_idx, tsl, :], yo[s], [Byo[s]], [], sembuf=Bout[s])
            TR.barrier()
        TR.barrier()
        info = {k: v["count"] + sum(c for _, c in v["old"]) for k, v in TR.engs.items()}
        info["nsem"] = TR.nsem
        print("program: engine op counts", info)
    return nc


def _pieces_kc(w, C):
    Lw, K, N = w.shape
    nk = K // 128
    a = w.reshape(Lw, nk, 128, N // C, C).transpose(0, 3, 2, 1, 4)
    return np.ascontiguousarray(a.reshape(Lw, N // C, 128, nk * C))


def _swap_halves(w, hd):
    sh = w.shape
    a = w.reshape(sh[:-1] + (sh[-1] // hd, 2, hd // 2))
    return a[..., ::-1, :].reshape(sh)


def _gain_cols(g):
    Lg, K = g.shape
    return g.reshape(Lg, K // 128, 128).transpose(2, 0, 1)


def prepare_weights(inp, Lw):
    f = lambda a: np.asarray(a, dtype=np.float32)
    out = {}
    ffn_w = {1: (inp["ffn1_w_gate"], inp["ffn1_w_up"], inp["ffn1_w_down"]),
             2: (inp["ffn2_w_gate"], inp["ffn2_w_up"], inp["ffn2_w_down"])}
    for i in (1, 2):
        g = _pieces_kc(f(ffn_w[i][0])[:Lw], 256)
        u = _pieces_kc(f(ffn_w[i][1])[:Lw], 256)
        out["w_gu%d" % i] = np.ascontiguousarray(np.concatenate([g, u], axis=1))
        wd = f(ffn_w[i][2])[:Lw]
        a = wd.reshape(Lw, NFC, 128, 8, 128).transpose(0, 3, 2, 1, 4)
        out["w_d%d" % i] = np.ascontiguousarray(a.reshape(Lw, 8, 128, NFC * 128))
    win = f(inp["w_in"])[:Lw]
    z = np.zeros((Lw, D, 64), np.float32)
    na, sw, di, ml = win[..., 0:768], win[..., 768:1280], win[..., 1280:2048], win[..., 2048:2464]
    swq = sw[..., 0:256].reshape(Lw, D, 4, 64)[:, :, [0, 2, 1, 3], :].reshape(Lw, D, 256)
    swk = sw[..., 256:384]
    kpe = ml[..., 384:416]
    z32 = z[..., 0:32]
    kpe_t = np.concatenate([z, kpe, z32, z, _swap_halves(kpe, 32), z32], axis=-1)
    pad128 = np.zeros((Lw, D, 128), np.float32)
    cols = [
        na[..., 0:256], na[..., 256:512], na[..., 512:768],
        swq, _swap_halves(swq, 64), np.concatenate([swk, _swap_halves(swk, 64)], axis=-1),
        np.concatenate([sw[..., 384:512], pad128], axis=-1),
        di[..., 0:256], _swap_halves(di[..., 0:256], 64), di[..., 256:512], _swap_halves(di[..., 256:512], 64),
        di[..., 512:768],
        ml[..., 0:256], np.concatenate([ml[..., 256:384], pad128], axis=-1), kpe_t,
    ]
    out["w_in"] = np.ascontiguousarray(np.concatenate([_pieces_kc(np.ascontiguousarray(c), 256) for c in cols], axis=1))
    wo = f(inp["w_out"])[:Lw]
    out["w_out"] = np.ascontiguousarray(
        wo.reshape(Lw, 4, 2, 128, 1024).transpose(0, 1, 3, 2, 4).reshape(Lw, 4, 128, 2048))
    out["w_pg"] = _pieces_kc(f(inp["ple_w_gate"])[:Lw], 256)
    out["w_pp"] = _pieces_kc(f(inp["ple_w_proj"])[:Lw], 1024)
    uq = f(inp["mla_w_uq"])[:Lw].reshape(Lw, 256, 4, 96)
    uqB = np.concatenate([np.zeros((Lw, 256, 4, 64), np.float32), _swap_halves(uq[..., 64:96], 32)], axis=-1)
    uqAB = np.stack([uq, uqB], axis=3)
    out["w_uq"] = np.ascontiguousarray(
        uqAB.reshape(Lw, 2, 128, 768).transpose(0, 2, 1, 3).reshape(Lw, 1, 128, 1536))
    ukv = f(inp["mla_w_ukv"])[:Lw].reshape(Lw, 128, 4, 128)
    out["w_ukv"] = np.ascontiguousarray(
        np.concatenate([ukv[..., 0:64].reshape(Lw, 128, 256), ukv[..., 64:128].reshape(Lw, 128, 256)], axis=-1)
    ).reshape(Lw, 1, 128, 512)
    g = np.zeros((128, Lw, NG), np.float32)
    g[:, :, G_FFN1:G_FFN1 + 8] = _gain_cols(f(inp["ffn1_norm"])[:Lw])
    g[:, :, G_MIX:G_MIX + 8] = _gain_cols(f(inp["mix_norm"])[:Lw])
    g[:, :, G_FFN2:G_FFN2 + 8] = _gain_cols(f(inp["ffn2_norm"])[:Lw])
    g[:, :, G_PLE:G_PLE + 8] = _gain_cols(f(inp["ple_norm"])[:Lw])
    g[:, :, G_GRP:G_GRP + 8] = _gain_cols(f(inp["group_norm"])[:Lw].reshape(Lw, 1024))
    g[:, :, G_QN:G_QN + 2] = _gain_cols(f(inp["mla_q_norm"])[:Lw])
    g[:, :, G_KVN:G_KVN + 1] = _gain_cols(f(inp["mla_kv_norm"])[:Lw])
    out["g_all"] = g
    out["final_g"] = np.ascontiguousarray(f(inp["final_norm"]).reshape(8, 128).T)
    out["sink_b"] = np.ascontiguousarray(np.broadcast_to(f(inp["swa_sink"])[:Lw][None], (128, Lw, 4)))
    nb = f(inp["na_bias"])[:Lw]
    kap = np.arange(64)[:, None]
    cc = np.arange(64)[None, :]
    ws = np.clip(cc - 8, 0, 48)
    cvalid = (kap >= ws) & (kap < ws + 16)
    dcol = np.clip(kap - cc, -15, 15) + 15
    nag = np.full((Lw, 4, 2, 128, 16, 64), NEG, np.float32)
    for i in range(2):
        for slot in range(16):
            dr = 14 - slot + i
            if dr < 0 or dr > 14:
                continue
            vals = nb[:, :, dr][:, :, dcol]
            vals = np.where(cvalid[None, None], vals, np.float32(NEG))
            nag[:, :, 1, i * 64:(i + 1) * 64, slot, :] = vals
            if 3 <= dr <= 10:
                nag[:, :, 0, i * 64:(i + 1) * 64, slot, :] = vals
    out["na_g"] = nag.reshape(Lw, 4, 2, 128, 1024)
    return out


def constants():
    pos = np.arange(T, dtype=np.float32)
    rope = np.zeros((4, 128, T), np.float32)
    inv64 = (10000.0 ** (-np.arange(32, dtype=np.float32) / 32)).astype(np.float32)
    ang = pos[None, :] * inv64[:, None]
    c64, s64 = np.cos(ang), np.sin(ang)
    for hh in range(2):
        rope[0, hh * 64:hh * 64 + 32] = c64
        rope[0, hh * 64 + 32:hh * 64 + 64] = c64
        rope[1, hh * 64:hh * 64 + 32] = -s64
        rope[1, hh * 64 + 32:hh * 64 + 64] = s64
    inv32 = (10000.0 ** (-np.arange(16, dtype=np.float32) / 16)).astype(np.float32)
    ang = pos[None, :] * inv32[:, None]
    c32, s32 = np.cos(ang), np.sin(ang)
    rope[2, 64:80] = c32
    rope[2, 80:96] = c32
    rope[3, 64:80] = -s32
    rope[3, 80:96] = s32
    kap = np.arange(128)[:, None]
    j = np.arange(2176)[None, :]
    d = j - kap - 1024
    dm = ((np.abs(d) <= 64).astype(np.float32) + ((d % 4 == 0) & (np.abs(d) <= 256)).astype(np.float32)
          + ((d % 16 == 0) & (np.abs(d) <= 1024)).astype(np.float32))
    j = np.arange(384)[None, :]
    sm = (np.abs(j - kap - 128) <= 128).astype(np.float32)
    return {"rope_t": rope.astype(np.float32), "dil_mask": dm.astype(np.float32), "swa_mask": sm,
            "ident": np.eye(128, dtype=np.float32)}


_CACHE = {}


def kernel(**inputs):
    n_cores = 8
    x = np.asarray(inputs["x"], dtype=np.float32)
    p = np.asarray(inputs["p"], dtype=np.float32)
    w = prepare_weights(inputs, L_ALL)
    w.update(constants())
    if "nc" not in _CACHE:
        _CACHE["nc"] = build_program(L_ALL, 2, ALL_FLAGS)
    nc = _CACHE["nc"]
    in_maps = []
    for c in range(n_cores):
        m = dict(w)
        m["x"] = np.ascontiguousarray(x[2 * c:2 * c + 2])
        m["p"] = np.ascontiguousarray(p[:, 2 * c:2 * c + 2])
        in_maps.append(m)
    res = run_bass_kernel_spmd(nc, in_maps, core_ids=list(range(n_cores)))
    return np.concatenate([np.asarray(r["y"], dtype=np.float32) for r in res.results], axis=0)
```

```python
import math
from contextlib import ExitStack

import numpy as np
import concourse.bass as bass
import concourse.mybir as mybir
from concourse.bass_utils import run_bass_kernel_spmd

F32 = mybir.dt.float32
BF16 = mybir.dt.bfloat16
AF = mybir.ActivationFunctionType
ALU = mybir.AluOpType
AX = mybir.AxisListType

D = 1024
T = 2048
L_ALL = 4
DFF = 2816
NFC = 22
NCH = 4
CH = 512
EPS = 1e-6
NEG = -30000.0
ALL_FLAGS = ("ffn1", "na", "swa", "dil", "mla", "ffn2", "ple")

G_FFN1, G_MIX, G_FFN2, G_PLE, G_GRP, G_QN, G_KVN, NG = 0, 8, 16, 24, 32, 40, 42, 43

(P_NA_Q, P_NA_K, P_NA_V, P_SW_Q, P_SW_QS, P_SW_K, P_SW_V, P_DI_Q, P_DI_QS, P_DI_K, P_DI_KS,
 P_DI_V, P_ML_CQ, P_ML_CKV, P_ML_KPE) = range(15)
N_WIN = 15

WSPEC = [
    ("w_gu1", 22, 8, 256, G_FFN1),
    ("w_d1", 8, 22, 128, None),
    ("w_in", N_WIN, 8, 256, G_MIX),
    ("w_out", 4, 2, 1024, G_GRP),
    ("w_uq", 1, 2, 768, G_QN),
    ("w_ukv", 1, 1, 512, G_KVN),
    ("w_gu2", 22, 8, 256, G_FFN2),
    ("w_d2", 8, 22, 128, None),
    ("w_pg", 4, 8, 256, G_PLE),
    ("w_pp", 1, 2, 1024, None),
]


class Buf:
    __slots__ = ("name", "last_w", "readers", "dsem", "dcount")

    def __init__(self, name):
        self.name = name
        self.last_w = None
        self.readers = {}
        self.dsem = None
        self.dcount = 0


class Tracker:
    EPOCH = 30000

    def __init__(self, nc, stack):
        self.nc = nc
        self.stack = stack
        self.engs = {}
        self.nsem = 0
        self.dbufs = []
        self.free_sems = []
        for nm, e in (("pe", nc.tensor), ("act", nc.scalar), ("dve", nc.vector),
                      ("pool", nc.gpsimd), ("sp", nc.sync)):
            self.engs[nm] = dict(name=nm, eng=e, sem=self._newsem("s_" + nm), count=0, seen={}, old=[])

    def _newsem(self, name):
        self.nsem += 1
        return self.stack.enter_context(self.nc.semaphore("%s_%d" % (name, self.nsem)))

    def _waits(self, E, reads, writes):
        deps = {}

        def add(rec, raw):
            sem, val = rec
            if sem is E["sem"] and not raw and E["name"] == "pe":
                return
            k = id(sem)
            if k not in deps or deps[k][1] < val:
                deps[k] = (sem, val)

        for b in reads:
            if b.last_w is not None:
                add(b.last_w, True)
        for b in writes:
            if b.last_w is not None:
                add(b.last_w, False)
            for rec in b.readers.values():
                add(rec, False)
        for k, (sem, val) in deps.items():
            if E["seen"].get(k, 0) < val:
                E["eng"].wait_ge(sem, val)
                E["seen"][k] = val

    def _record(self, rec, reads, writes):
        k = id(rec[0])
        for b in reads:
            b.readers[k] = rec
        for b in writes:
            b.last_w = rec
            b.readers = {}

    def op(self, en, reads, writes, fn):
        E = self.engs[en]
        self._waits(E, reads, writes)
        inst = fn(E["eng"])
        if E["count"] >= self.EPOCH:
            E["old"].append((E["sem"], E["count"]))
            E["sem"] = self._newsem("s_" + en)
            E["count"] = 0
        E["count"] += 1
        inst.then_inc(E["sem"], 1)
        self._record((E["sem"], E["count"]), reads, writes)

    def dma(self, out, in_, reads, writes, sembuf=None, qn="sp"):
        E = self.engs[qn]
        self._waits(E, reads, writes)
        sb = sembuf if sembuf is not None else writes[0]
        if sb.dsem is None:
            if self.free_sems:
                sb.dsem, sb.dcount = self.free_sems.pop()
            else:
                sb.dsem, sb.dcount = self._newsem("d"), 0
            self.dbufs.append(sb)
        sb.dcount += 16
        E["eng"].dma_start(out=out, in_=in_).then_inc(sb.dsem, 16)
        self._record((sb.dsem, sb.dcount), reads, writes)

    def barrier(self):
        recs = []
        for E in self.engs.values():
            if E["count"] > 0:
                recs.append((E["sem"], E["count"]))
        for b in self.dbufs:
            if b.dcount > 0:
                recs.append((b.dsem, b.dcount))
        for E in self.engs.values():
            for sem, val in recs:
                k = id(sem)
                if E["seen"].get(k, 0) < val:
                    E["eng"].wait_ge(sem, val)
                    E["seen"][k] = val
        for b in self.dbufs:
            if b.dcount < 24000:
                self.free_sems.append((b.dsem, b.dcount))
            b.dsem = None
        self.dbufs = []

    def finish(self, bufs, qn="sp"):
        self._waits(self.engs[qn], bufs, [])


class Arena:
    def __init__(self, t, nbytes):
        self.t = t
        self.nbytes = nbytes
        self.off = 0

    def reset(self):
        self.off = 0

    def alloc(self, nbytes, dtype=BF16):
        nbytes = (nbytes + 63) // 64 * 64
        o = self.off
        self.off += nbytes
        assert self.off <= self.nbytes, ("arena overflow", self.off, self.nbytes)
        v = self.t[:, o // 2:(o + nbytes) // 2]
        if dtype == F32:
            v = v.bitcast(F32)
        return v


def build_program(n_layers=L_ALL, n_seq=2, flags=ALL_FLAGS):
    nc = bass.Bass("TRN2", target_bir_lowering=False)
    L = n_layers
    flags = set(flags)
    dram = {}

    def din(name, shape, dt=F32):
        dram[name] = nc.dram_tensor(name, list(shape), dt, kind="ExternalInput").ap()
        return dram[name]

    x_d = din("x", [n_seq, T, D])
    p_d = din("p", [L, n_seq, T, 256])
    wsrc = {}
    wscr = {}
    for (nm, npc, nk, C, goff) in WSPEC:
        wsrc[nm] = din(nm, [L, npc, 128, nk * C])
        wscr[nm] = nc.dram_tensor("s_" + nm, [L, npc, 128, nk * C], BF16, kind="Internal").ap()
    g_d = din("g_all", [128, L, NG])
    gfin_d = din("final_g", [128, 8])
    nag_d = din("na_g", [L, 4, 2, 128, 1024])
    sink_d = din("sink_b", [128, L, 4])
    rope_d = din("rope_t", [4, 128, T])
    dmask_d = din("dil_mask", [128, 2176])
    smask_d = din("swa_mask", [128, 384])
    ident_d = din("ident", [128, 128])
    y_d = nc.dram_tensor("y", [n_seq, T, D], F32, kind="ExternalOutput").ap()

    with ExitStack() as st:
        TR = Tracker(nc, st)
        sb = nc.alloc_sbuf_tensor
        xT = sb("xT", [128, 8, T], F32)
        uT = sb("uT", [128, 8, T], BF16)
        rope = sb("rope", [128, 4, T], BF16)
        dmask = sb("dmask", [128, 2176], BF16)
        smask = sb("smask", [128, 384], BF16)
        ident = sb("ident_sb", [128, 128], F32)
        ones_bf = sb("ones_bf", [128, 128], BF16)
        onesA = sb("onesA", [128, 128], BF16)
        onesB = sb("onesB", [128, 128], BF16)
        g_sb = sb("g_sb", [128, L, NG], F32)
        gfin = sb("gfin", [128, 8], F32)
        sink_sb = sb("sink_sb", [128, L, 4], F32)
        nhalf = sb("nhalf", [128, 1], F32)
        eps_t = sb("eps_t", [128, 1], F32)
        stat = sb("stat", [128, 64], F32)
        ARENA_BYTES = 87 * 1024
        arena_t = sb("arena", [128, ARENA_BYTES // 2], BF16)
        AR = Arena(arena_t, ARENA_BYTES)
        banks = [nc.alloc_psum_tensor("bank%d" % i, [128, 512], F32) for i in range(8)]
        Bbanks = [Buf("bank%d" % i) for i in range(8)]
        bank_rr = [0]
        obank_rr = [0]

        def gbank():
            i = bank_rr[0] % 6
            bank_rr[0] += 1
            return banks[i], Bbanks[i]

        def obank():
            i = 6 + obank_rr[0] % 2
            obank_rr[0] += 1
            return banks[i], Bbanks[i]

        BxT = [Buf("xT%d" % c) for c in range(NCH)]
        BuT = [Buf("uT%d" % c) for c in range(NCH)]
        Bconst = Buf("const")
        Bstat = Buf("stat"); Bk2 = Buf("k2"); Bq2 = [Buf("q2a"), Buf("q2b")]; Bneg = [Buf("nga"), Buf("ngb")]; Brec = Buf("rec"); Bgn = Buf("gn")
        Bscr = Buf("scratch")
        rr = {"n": 0}

        def alt(a="dve", b="pool"):
            rr["n"] += 1
            return a if rr["n"] % 2 else b

        st.enter_context(nc.Block())

        AR.reset()
        c32 = AR.alloc(2176 * 4, F32)
        Bc32 = Buf("c32")
        for i in range(4):
            TR.dma(c32[:, 0:T], rope_d[i], [], [Bc32])
            TR.op("dve", [Bc32], [Bconst], lambda e, i=i: e.tensor_copy(out=rope[:, i, :], in_=c32[:, 0:T]))
        TR.dma(c32[:, 0:2176], dmask_d, [], [Bc32])
        TR.op("dve", [Bc32], [Bconst], lambda e: e.tensor_copy(out=dmask[:, :], in_=c32[:, 0:2176]))
        TR.dma(c32[:, 0:384], smask_d, [], [Bc32])
        TR.op("dve", [Bc32], [Bconst], lambda e: e.tensor_copy(out=smask[:, :], in_=c32[:, 0:384]))
        TR.dma(ident[:, :], ident_d, [], [Bconst])
        TR.dma(g_sb[:, :, :], g_d, [], [Bconst])
        TR.dma(gfin[:, :], gfin_d, [], [Bconst])
        TR.dma(sink_sb[:, :, :], sink_d, [], [Bconst])
        TR.op("dve", [], [Bconst], lambda e: e.memset(ones_bf[:, :], 1.0))
        TR.op("dve", [], [Bconst], lambda e: e.memset(onesA[:, :], 0.0))
        TR.op("dve", [], [Bconst], lambda e: e.memset(onesB[:, :], 0.0))
        TR.op("dve", [Bconst], [Bconst], lambda e: e.memset(onesA[0:64, :], 1.0))
        TR.op("dve", [Bconst], [Bconst], lambda e: e.memset(onesB[64:128, :], 1.0))
        TR.op("dve", [], [Bconst], lambda e: e.memset(nhalf[:, :], -0.5))
        TR.op("dve", [], [Bconst], lambda e: e.memset(eps_t[:, :], EPS))
        TR.barrier()

        AR.reset()
        FMAX = 2816
        NSL = 5
        s32 = [AR.alloc(FMAX * 4, F32) for _ in range(NSL)]
        s16 = [AR.alloc(FMAX * 2) for _ in range(NSL)]
        Bs32 = [Buf("s32_%d" % i) for i in range(NSL)]
        Bs16 = [Buf("s16_%d" % i) for i in range(NSL)]
        Bst = [Buf("st_%d" % i) for i in range(NSL)]
        used = set()
        if "ffn1" in flags:
            used |= {"w_gu1", "w_d1"}
        if "ffn2" in flags:
            used |= {"w_gu2", "w_d2"}
        if flags & {"na", "swa", "dil", "mla"}:
            used |= {"w_in", "w_out"}
        if "mla" in flags:
            used |= {"w_uq", "w_ukv"}
        if "ple" in flags:
            used |= {"w_pg", "w_pp"}
        BG_OK = ("ffn1" in flags) and ("ffn2" in flags)
        tasks = []
        bg_tasks = {}
        for l in range(L):
            for (nm, npc, nk, C, goff) in WSPEC:
                if nm in used:
                    for pi in range(npc):
                        if l == 0 or not BG_OK:
                            tasks.append((l, nm, pi, nk, C, goff))
                        else:
                            F_ = nk * C
                            if F_ <= 2048:
                                bg_tasks.setdefault(l, []).append((l, nm, pi, nk, C, goff, 0, F_))
                            else:
                                assert goff is None and F_ % 2 == 0
                                bg_tasks.setdefault(l, []).append((l, nm, pi, nk, C, goff, 0, F_ // 2))
                                bg_tasks.setdefault(l, []).append((l, nm, pi, nk, C, goff, F_ // 2, F_))

        def pp_load(k):
            l, nm, pi, nk, C, goff = tasks[k]
            s = k % NSL
            TR.dma(s32[s][:, 0:nk * C], wsrc[nm][l, pi], [], [Bs32[s]])

        for k in range(min(NSL - 1, len(tasks))):
            pp_load(k)
        for k in range(len(tasks)):
            l, nm, pi, nk, C, goff = tasks[k]
            F = nk * C
            s = k % NSL
            if k + NSL - 1 < len(tasks):
                pp_load(k + NSL - 1)
            en = "pool" if k % 4 == 3 else "dve"
            if goff is None:
                TR.op(en, [Bs32[s]], [Bs16[s]],
                      lambda e: e.tensor_copy(out=s16[s][:, 0:F], in_=s32[s][:, 0:F]))
            else:
                go = goff + (2 * pi if nm == "w_out" else 0)
                gv = g_sb[:, l, go:go + nk].unsqueeze(2).to_broadcast([128, nk, C])
                TR.op(en, [Bs32[s], Bconst], [Bs16[s]],
                      lambda e: e.tensor_tensor(
                          out=s16[s][:, 0:F].rearrange("p (k c) -> p k c", k=nk),
                          in0=s32[s][:, 0:F].rearrange("p (k c) -> p k c", k=nk),
                          in1=gv, op=ALU.mult))
            TR.dma(wscr[nm][l, pi], s16[s][:, 0:F], [Bs16[s]], [], sembuf=Bst[s], qn="act")
        TR.barrier()

        def chs(c):
            return slice(c * CH, (c + 1) * CH)

        bg = {"q": [], "t": 0, "pipe": {}}

        def bg_begin():
            bg["s32"] = [AR.alloc(2048 * 4, F32) for _ in range(2)]
            bg["s16"] = [AR.alloc(2048 * 2) for _ in range(2)]
            bg["B32"] = [Buf("b32_0"), Buf("b32_1")]
            bg["B16"] = [Buf("b16_0"), Buf("b16_1")]
            bg["Bst"] = [Buf("bst_0"), Buf("bst_1")]
            bg["pipe"] = {}

        def bg_step(allow_load=True):
            t = bg["t"]
            pipe = bg["pipe"]
            if (t - 2) in pipe:
                (l_, nm, pi, nk, C, goff, a, b) = pipe.pop(t - 2)
                i = (t - 2) % 2
                TR.dma(wscr[nm][l_, pi][:, a:b], bg["s16"][i][:, 0:b - a], [bg["B16"][i]], [], sembuf=bg["Bst"][i], qn="sp")
            if (t - 1) in pipe:
                (l_, nm, pi, nk, C, goff, a, b) = pipe[t - 1]
                i = (t - 1) % 2
                F_ = b - a
                if goff is None:
                    TR.op("dve", [bg["B32"][i]], [bg["B16"][i]],
                          lambda e: e.tensor_copy(out=bg["s16"][i][:, 0:F_], in_=bg["s32"][i][:, 0:F_]))
                else:
                    go = goff + (2 * pi if nm == "w_out" else 0)
                    gv = g_sb[:, l_, go:go + nk].unsqueeze(2).to_broadcast([128, nk, C])
                    TR.op("dve", [bg["B32"][i], Bconst], [bg["B16"][i]],
                          lambda e: e.tensor_tensor(
                              out=bg["s16"][i][:, 0:F_].rearrange("p (k c) -> p k c", k=nk),
                              in0=bg["s32"][i][:, 0:F_].rearrange("p (k c) -> p k c", k=nk),
                              in1=gv, op=ALU.mult))
            if allow_load and bg["q"]:
                task = bg["q"].pop(0)
                (l_, nm, pi, nk, C, goff, a, b) = task
                i = t % 2
                TR.dma(bg["s32"][i][:, 0:b - a], wsrc[nm][l_, pi][:, a:b], [], [bg["B32"][i]], qn="sp")
                pipe[t] = task
            bg["t"] = t + 1

        def bg_drain(everything):
            while (everything and bg["q"]) or bg["pipe"]:
                bg_step(allow_load=everything)

        def rsqrt_act(dst, Bdst, src, Bsrc, mult):
            TR.op("act", [Bsrc, Bconst], [Bdst],
                  lambda e: e.activation(out=dst, in_=src, func=AF.Ln, bias=eps_t[:, 0:1], scale=mult))
            TR.op("act", [Bdst], [Bdst],
                  lambda e: e.activation(out=dst, in_=dst, func=AF.Exp, scale=-0.5))

        def norm_a(c, sq, Bsq):
            cs = chs(c)
            TR.op("dve", [BxT[c]], [Bsq],
                  lambda e: e.tensor_tensor(out=sq[:, 0:3, :], in0=xT[:, 0:3, cs], in1=xT[:, 0:3, cs], op=ALU.mult))
            TR.op("pool", [BxT[c]], [Bsq],
                  lambda e: e.tensor_tensor(out=sq[:, 3:8, :], in0=xT[:, 3:8, cs], in1=xT[:, 3:8, cs], op=ALU.mult))

        def norm_chunk(c, sq, Bsq, rstd, Brstd, do_a=True):
            cs = chs(c)
            if do_a:
                norm_a(c, sq, Bsq)
            ps, Bps = gbank()

            def mm(e):
                for kc in range(8):
                    ins = e.matmul(ps[:, :], ones_bf[:, :], sq[:, kc, :], start=(kc == 0), stop=(kc == 7))
                return ins
            TR.op("pe", [Bsq, Bconst], [Bps], mm)
            rsqrt_act(rstd, Brstd, ps[:, :], Bps, 1.0 / D)
            TR.op("dve", [BxT[c], Brstd], [BuT[c]],
                  lambda e: e.tensor_tensor(out=uT[:, 0:5, cs], in0=xT[:, 0:5, cs],
                                            in1=rstd.unsqueeze(1).to_broadcast([128, 5, CH]), op=ALU.mult))
            TR.op("pool", [BxT[c], Brstd], [BuT[c]],
                  lambda e: e.tensor_tensor(out=uT[:, 5:8, cs], in0=xT[:, 5:8, cs],
                                            in1=rstd.unsqueeze(1).to_broadcast([128, 3, CH]), op=ALU.mult))

        def load_piece(dst, Bdst, nm, l, pi, F):
            TR.dma(dst[:, 0:F], wscr[nm][l, pi], [], [Bdst])

        ut_ready = [False]

        def ffn_phase(l, which, post_norm=False, pre_normed=False, bg_on=False):
            gu, dn = ("w_gu1", "w_d1") if which == 1 else ("w_gu2", "w_d2")
            AR.reset()
            hT = AR.alloc(NFC * CH * 2).rearrange("p (f t) -> p f t", f=NFC)
            BhT = Buf("hT")
            wg = [AR.alloc(2 * 2048 * 2).rearrange("p (g k c) -> p g k c", g=2, k=8) for _ in range(2)]
            Bwg = [Buf("wg%d" % i) for i in range(2)]
            wd = [AR.alloc(2816 * 2).rearrange("p (f c) -> p f c", f=NFC) for _ in range(2)]
            Bwd = [Buf("wd%d" % i) for i in range(2)]
            sq = AR.alloc(8 * CH * 2).rearrange("p (k t) -> p k t", k=8)
            Bsq = Buf("sq")
            rstd = AR.alloc(CH * 4, F32)
            Brstd = Buf("rstd")
            sg = [AR.alloc(CH * 4, F32) for _ in range(2)]
            Bsg = [Buf("sg%d" % i) for i in range(2)]
            bg_on = bg_on and bool(bg["q"])
            if bg_on:
                bg_begin()

            def load_gu(c, j):
                s = (c * 11 + j) % 2
                TR.dma(wg[s][:, 0], wscr[gu][l, j].rearrange("p (k c) -> p k c", k=8), [], [Bwg[s]])
                TR.dma(wg[s][:, 1], wscr[gu][l, 11 + j].rearrange("p (k c) -> p k c", k=8), [], [Bwg[s]])

            def load_d(c, dc):
                s = (c * 8 + dc) % 2
                TR.dma(wd[s], wscr[dn][l, dc].rearrange("p (f c) -> p f c", f=NFC), [], [Bwd[s]])

            if not pre_normed:
                norm_chunk(0, sq, Bsq, rstd, Brstd)
            for c in range(NCH):
                cs = chs(c)
                load_gu(c, 0)
                for j in range(11):
                    if j + 1 < 11:
                        load_gu(c, j + 1)
                    else:
                        load_d(c, 0)
                    if post_norm and c > 0:
                        if j == 1:
                            norm_a(c - 1, sq, Bsq)
                        if j == 4:
                            norm_chunk(c - 1, sq, Bsq, rstd, Brstd, do_a=False)
                    if (not pre_normed) and c + 1 < NCH:
                        if j == 6:
                            norm_a(c + 1, sq, Bsq)
                        if j == 9:
                            norm_chunk(c + 1, sq, Bsq, rstd, Brstd, do_a=False)
                    if bg_on:
                        bg_step()
                    s = (c * 11 + j) % 2
                    for fi in range(2):
                        fc = 2 * j + fi
                        psg, Bg = gbank()
                        psu, Bu = gbank()

                        def mmg(e, s=s, fi=fi, g=0, ps=psg):
                            for kc in range(8):
                                ins = e.matmul(ps[:, :], wg[s][:, g, kc, fi * 128:(fi + 1) * 128], uT[:, kc, cs],
                                               start=(kc == 0), stop=(kc == 7))
                            return ins
                        TR.op("pe", [Bwg[s], BuT[c]], [Bg], mmg)
                        TR.op("pe", [Bwg[s], BuT[c]], [Bu], lambda e, s=s, fi=fi, ps=psu: mmg(e, s, fi, 1, ps))
                        q = fc % 2
                        TR.op("act", [Bg], [Bsg[q]],
                              lambda e, q=q, ps=psg: e.activation(out=sg[q], in_=ps[:, :], func=AF.Silu))
                        TR.op("dve", [Bsg[q], Bu], [BhT],
                              lambda e, q=q, ps=psu, fc=fc: e.tensor_tensor(out=hT[:, fc, :], in0=sg[q], in1=ps[:, :],
                                                                            op=ALU.mult))
                for dc in range(8):
                    if dc + 1 < 8:
                        load_d(c, dc + 1)
                    if bg_on and dc % 2 == 0:
                        bg_step()
                    s = (c * 8 + dc) % 2
                    ps, Bps = gbank()

                    def mmd(e, s=s, ps=ps):
                        for fc in range(NFC):
                            ins = e.matmul(ps[:, :], wd[s][:, fc, :], hT[:, fc, :], start=(fc == 0), stop=(fc == NFC - 1))
                        return ins
                    TR.op("pe", [Bwd[s], BhT], [Bps], mmd)
                    TR.op("dve", [Bps, BxT[c]], [BxT[c]],
                          lambda e, ps=ps, dc=dc: e.scalar_tensor_tensor(out=xT[:, dc, cs], in0=ps[:, :], scalar=0.5,
                                                                         in1=xT[:, dc, cs], op0=ALU.mult, op1=ALU.add))
            if post_norm:
                norm_chunk(NCH - 1, sq, Bsq, rstd, Brstd)
                ut_ready[0] = (which == 1)
            if bg_on:
                bg_drain(everything=(which == 2))
            TR.barrier()

        def ple_phase(l, s_idx, pre_normed=False, post_norm=False):
            AR.reset()
            wpg = AR.alloc(4 * 2048 * 2).rearrange("p (j k c) -> p j k c", j=4, k=8)
            wpp = AR.alloc(2048 * 2).rearrange("p (k c) -> p k c", k=2)
            Bw = Buf("wple")
            sq = AR.alloc(8 * CH * 2).rearrange("p (k t) -> p k t", k=8)
            Bsq = Buf("sq")
            rstd = AR.alloc(CH * 4, F32)
            Brstd = Buf("rstd")
            pin = [AR.alloc(256 * 4, F32) for _ in range(2)]
            Bpin = [Buf("pin%d" % i) for i in range(2)]
            pT = AR.alloc(2 * CH * 2).rearrange("p (k t) -> p k t", k=2)
            BpT = Buf("pT")
            sg = [AR.alloc(CH * 4, F32) for _ in range(2)]
            Bsg = [Buf("sg%d" % i) for i in range(2)]
            tt = [AR.alloc(CH * 4, F32) for _ in range(2)]
            Btt = [Buf("tt%d" % i) for i in range(2)]
            for j in range(4):
                TR.dma(wpg[:, j], wscr["w_pg"][l, j].rearrange("p (k c) -> p k c", k=8), [], [Bw])
            TR.dma(wpp, wscr["w_pp"][l, 0].rearrange("p (k c) -> p k c", k=2), [], [Bw])
            for c in range(NCH):
                cs = chs(c)
                if not pre_normed:
                    norm_chunk(c, sq, Bsq, rstd, Brstd)
                pss = [gbank(), gbank()]
                for ti in range(4):
                    s = ti % 2
                    tok0 = c * CH + ti * 128
                    TR.dma(pin[s], p_d[l, s_idx, tok0:tok0 + 128, :], [], [Bpin[s]])
                    for pc in range(2):
                        TR.op("pe", [Bpin[s], Bconst], [pss[pc][1]],
                              lambda e, s=s, pc=pc, ti=ti: e.transpose(pss[pc][0][:, ti * 128:(ti + 1) * 128],
                                                                       pin[s][:, pc * 128:(pc + 1) * 128], ident[:, :]))
                for pc in range(2):
                    TR.op("act", [pss[pc][1]], [BpT],
                          lambda e, pc=pc: e.copy(out=pT[:, pc, :], in_=pss[pc][0][:, :]))
                for dc in range(8):
                    if post_norm and c > 0 and dc == 0:
                        norm_a(c - 1, sq, Bsq)
                    if post_norm and c > 0 and dc == 4:
                        norm_chunk(c - 1, sq, Bsq, rstd, Brstd, do_a=False)
                    psg, Bg = gbank()
                    psp, Bp = gbank()
                    j, co = dc // 2, (dc % 2) * 128

                    def mmg(e, ps=psg, j=j, co=co):
                        for kc in range(8):
                            ins = e.matmul(ps[:, :], wpg[:, j, kc, co:co + 128], uT[:, kc, cs], start=(kc == 0), stop=(kc == 7))
                        return ins

                    def mmp(e, ps=psp, dc=dc):
                        for pc in range(2):
                            ins = e.matmul(ps[:, :], wpp[:, pc, dc * 128:(dc + 1) * 128], pT[:, pc, :], start=(pc == 0), stop=(pc == 1))
                        return ins
                    TR.op("pe", [Bw, BuT[c]], [Bg], mmg)
                    TR.op("pe", [Bw, BpT], [Bp], mmp)
                    q = dc % 2
                    TR.op("act", [Bg], [Bsg[q]], lambda e, q=q, ps=psg: e.activation(out=sg[q], in_=ps[:, :], func=AF.Sigmoid))
                    TR.op("dve", [Bsg[q], Bp], [Btt[q]],
                          lambda e, q=q, ps=psp: e.tensor_tensor(out=tt[q], in0=sg[q], in1=ps[:, :], op=ALU.mult))
                    TR.op("pool", [Btt[q], BxT[c]], [BxT[c]],
                          lambda e, q=q, dc=dc: e.tensor_tensor(out=xT[:, dc, cs], in0=xT[:, dc, cs], in1=tt[q], op=ALU.add))
            if post_norm:
                norm_chunk(NCH - 1, sq, Bsq, rstd, Brstd)
            TR.barrier()

        def mixer_phase(l):
            AR.reset()
            sq = AR.alloc(8 * CH * 2).rearrange("p (k t) -> p k t", k=8)
            Bsq = Buf("sq")
            rstd = AR.alloc(CH * 4, F32)
            Brstd = Buf("rstd")
            if not ut_ready[0]:
                for c in range(NCH):
                    norm_chunk(c, sq, Bsq, rstd, Brstd)
                TR.barrier()
            ut_ready[0] = False
            for gi, mx in enumerate(("na", "swa", "dil", "mla")):
                if mx in flags:
                    one_mixer(l, gi, mx)
                    TR.barrier()

        def one_mixer(l, gi, mx):
            AR.reset()
            is_mla = mx == "mla"
            dk = 96 if is_mla else 64
            scale = dk ** -0.5
            nkt = 4 if is_mla else 2
            KT = [AR.alloc(T * 2) for _ in range(nkt)]
            BKT = Buf("KT")
            Vflat = AR.alloc(16 * 4 * 65 * 2)
            V = Vflat[:, 0:16 * 4 * 65].rearrange("p (k h d) -> p k h d", k=16, h=4)
            BV = Buf("V")
            nqt = 4
            QT = [[AR.alloc(CH * 2) for _ in range(nqt)] for _ in range(2)]
            BQT = [Buf("QT0"), Buf("QT1")]
            NPT = 5
            PT = [AR.alloc(CH * 2) for _ in range(NPT)]
            BPT = [Buf("PT%d" % i) for i in range(NPT)]
            ych = [AR.alloc(4 * 256 * 4, F32).rearrange("p (j h d) -> p j h d", j=4, h=4)] * 2
            Bych = [Buf("y0")] * 2
            if not is_mla:
                ysq = AR.alloc(4 * 256 * 4, F32).rearrange("p (j c) -> p j c", j=4)
                Bysq = Buf("ysq")
            ynT = [AR.alloc(2 * CH * 2).rearrange("p (k t) -> p k t", k=2)] * 2
            BynT = [Buf("ynT0")] * 2
            wst = [AR.alloc(2048 * 2).rearrange("p (k c) -> p k c", k=8) for _ in range(2)]
            Bwst = [Buf("wst0"), Buf("wst1")]
            wq = [AR.alloc(2048 * 2).rearrange("p (k c) -> p k c", k=8) for _ in range(2 if mx in ("swa", "dil") else 1)]
            Bwq = Buf("wq")
            wo = AR.alloc(2048 * 2).rearrange("p (k c) -> p k c", k=2)
            Bwo = Buf("wo")
            sqs = [AR.alloc(CH * 2) for _ in range(2)]
            Bsqs = [Buf("sqs0"), Buf("sqs1")]
            if mx != "na":
                r1 = [AR.alloc(CH * 4, F32)] * 2
                r2 = [AR.alloc(CH * 4, F32)] * 2
                Br = [Buf("r0")] * 2
            if is_mla:
                wuq = AR.alloc(1536 * 2).rearrange("p (k h a d) -> p k h a d", k=2, h=4, a=2)
                wukv = AR.alloc(512 * 2)
                Bwm = Buf("wmla")
                cqn = [AR.alloc(2 * CH * 2).rearrange("p (k t) -> p k t", k=2) for _ in range(2)]
                Bcqn = [Buf("cqn0"), Buf("cqn1")]
                ckvn = AR.alloc(T * 2)
                Bckvn = Buf("ckvn")
                lat32 = AR.alloc(2 * CH * 4, F32).rearrange("p (k t) -> p k t", k=2)
                Blat = Buf("lat32")
                ysq = lat32.rearrange("p k t -> p (k t)").rearrange("p (j c) -> p j c", j=4)
                Bysq = Blat
                lrs = AR.alloc(CH * 4, F32)
                Blrs = Buf("lrs")
                lsq = AR.alloc(2 * CH * 2).rearrange("p (k t) -> p k t", k=2)
                Blsq = Buf("lsq")
            if mx == "na":
                nam = AR.alloc(4 * 2 * 1024 * 2).rearrange("p (h v c) -> p h v c", h=4, v=2)
                Bnam = Buf("nam")
                nst = AR.alloc(1024 * 4, F32)
                Bnst = Buf("nst")
            wrr = [0]

            def wslot():
                i = wrr[0] % 2
                wrr[0] += 1
                return wst[i], Bwst[i]

            TR.dma(wo, wscr["w_out"][l, gi].rearrange("p (k c) -> p k c", k=2), [], [Bwo])
            TR.op("pool", [], [BV], lambda e: e.memset(Vflat, 1.0))

            def proj_fm(wt, Bw, col0, M, rhs_fn, nk, extra_reads, N=CH):
                ps, Bps = gbank()

                def mm(e):
                    for kc in range(nk):
                        ins = e.matmul(ps[0:M, 0:N], wt[:, kc, col0:col0 + M], rhs_fn(kc), start=(kc == 0), stop=(kc == nk - 1))
                    return ins
                TR.op("pe", [Bw] + extra_reads, [Bps], mm)
                return ps, Bps

            def sumsq_max(src, Bsrc, rows, lhs, stat_col, part_col):
                i = alt("0", "1") == "0"
                i = 0 if i else 1
                TR.op("pool", [Bsrc], [Bsqs[i]],
                      lambda e: e.tensor_tensor(out=sqs[i][rows, :], in0=src[rows, :], in1=src[rows, :], op=ALU.mult))
                ps, Bps = gbank()
                TR.op("pe", [Bsqs[i], Bconst], [Bps],
                      lambda e: e.matmul(ps[:, :], lhs[rows, :], sqs[i][rows, :], start=True, stop=True))
                TR.op("dve", [Bps], [Bk2],
                      lambda e: e.reduce_max(out=stat[:, part_col:part_col + 1], in_=ps[:, :], axis=AX.X))
                if stat_col is not None:
                    TR.op("dve", [Bk2], [Bk2],
                          lambda e: e.tensor_tensor(out=stat[:, stat_col:stat_col + 1], in0=stat[:, stat_col:stat_col + 1],
                                                    in1=stat[:, part_col:part_col + 1], op=ALU.max))

            def rope_evac(psA, BA, psB, BB, rows, ti, cs, dst, Bdst, dsts=None):
                i = alt("0", "1") == "0"
                i = 0 if i else 1
                TR.op("dve", [BA, Bconst], [Br[i]],
                      lambda e: e.tensor_tensor(out=r1[i][rows, :], in0=psA[rows, :], in1=rope[rows, ti, cs], op=ALU.mult))
                TR.op("dve", [BB, Bconst, Br[i]], [Br[i]],
                      lambda e: e.tensor_tensor(out=r2[i][rows, :], in0=psB[rows, :], in1=rope[rows, ti + 1, cs], op=ALU.mult))
                for (d_, rw) in (dsts if dsts is not None else [(dst, rows)]):
                    TR.op("pool", [Br[i]], [Bdst],
                          lambda e: e.tensor_tensor(out=d_[rw, :], in0=r1[i][rw, :], in1=r2[i][rw, :], op=ALU.add))

            full = slice(0, 128)
            TR.op("dve", [], [Bk2], lambda e: e.memset(stat[:, 0:4], 0.0))
            if not is_mla:
                for qb_ in range(2):
                    for h in range(4):
                        TR.op("pool", [], [BQT[qb_]], lambda e, h=h: e.memset(QT[qb_][h], 0.0))

            if not is_mla:
                roped = mx in ("swa", "dil")
                pk = {"na": P_NA_K, "swa": P_SW_K, "dil": P_DI_K}[mx]
                pv = {"na": P_NA_V, "swa": P_SW_V, "dil": P_DI_V}[mx]
                nkt_used = 1 if mx == "swa" else 2
                wk, Bwk = wslot()
                load_piece(wk.rearrange("p k c -> p (k c)"), Bwk, "w_in", l, pk, 2048)
                if mx == "dil":
                    wks, Bwks = wslot()
                    load_piece(wks.rearrange("p k c -> p (k c)"), Bwks, "w_in", l, P_DI_KS, 2048)
                for t in range(nkt_used):
                    for c in range(NCH):
                        cs = chs(c)
                        psA, BA = proj_fm(wk, Bwk, t * 128, 128, lambda kc: uT[:, kc, cs], 8, [BuT[c]])
                        if roped:
                            if mx == "swa":
                                psB, BB = proj_fm(wk, Bwk, 128, 128, lambda kc: uT[:, kc, cs], 8, [BuT[c]])
                            else:
                                psB, BB = proj_fm(wks, Bwks, t * 128, 128, lambda kc: uT[:, kc, cs], 8, [BuT[c]])
                            rope_evac(psA, BA, psB, BB, full, 0, cs, KT[t][:, cs], BKT)
                        else:
                            TR.op("act", [BA], [BKT], lambda e, t=t, cs=cs, ps=psA: e.copy(out=KT[t][:, cs], in_=ps[:, :]))
                wv, Bwv = wslot()
                load_piece(wv.rearrange("p k c -> p (k c)"), Bwv, "w_in", l, pv, 2048)
                nvh = 2 if mx == "swa" else 4
                for kt in range(16):
                    ps, Bps = gbank()

                    def mmv(e, ps=ps, kt=kt):
                        for kc in range(8):
                            ins = e.matmul(ps[:, 0:nvh * 64], uT[:, kc, kt * 128:(kt + 1) * 128], wv[:, kc, 0:nvh * 64],
                                           start=(kc == 0), stop=(kc == 7))
                        return ins
                    TR.op("pe", [Bwv, BuT[kt // 4]], [Bps], mmv)
                    TR.op("act", [Bps], [BV],
                          lambda e, ps=ps, kt=kt: e.copy(out=V[:, kt, 0:nvh, 0:64],
                                                         in_=ps[:, 0:nvh * 64].rearrange("p (h d) -> p h d", h=nvh)))
                for t in range(nkt_used):
                    for c in range(NCH):
                        cs = chs(c)
                        sumsq_max(KT[t][:, cs], BKT, full, onesA, 2 * t, 12)
                        sumsq_max(KT[t][:, cs], BKT, full, onesB, 2 * t + 1, 13)
            else:
                TR.dma(wuq.rearrange("p k h a d -> p (k h a d)"), wscr["w_uq"][l, 0], [], [Bwm])
                TR.dma(wukv, wscr["w_ukv"][l, 0], [], [Bwm])
                wckv, Bwckv = wslot()
                load_piece(wckv.rearrange("p k c -> p (k c)"), Bwckv, "w_in", l, P_ML_CKV, 2048)
                wkpe, Bwkpe = wslot()
                load_piece(wkpe.rearrange("p k c -> p (k c)"), Bwkpe, "w_in", l, P_ML_KPE, 2048)
                R = slice(64, 96)

                def mla_A(c):
                    cs = chs(c)
                    lb = c % 2
                    ps, Bps = proj_fm(wckv, Bwckv, 0, 128, lambda kc: uT[:, kc, cs], 8, [BuT[c]])
                    TR.op("act", [Bps], [Blat], lambda e: e.copy(out=lat32[:, lb, :], in_=ps[:, :]))
                    psA, BA = proj_fm(wkpe, Bwkpe, 0, 128, lambda kc: uT[:, kc, cs], 8, [BuT[c]])
                    psB, BB = proj_fm(wkpe, Bwkpe, 128, 128, lambda kc: uT[:, kc, cs], 8, [BuT[c]])
                    TR.op("pool", [Blat], [Bsqs[0]],
                          lambda e: e.tensor_tensor(out=sqs[0], in0=lat32[:, lb, :], in1=lat32[:, lb, :], op=ALU.mult))
                    ps2, Bps2 = gbank()
                    TR.op("pe", [Bsqs[0], Bconst], [Bps2],
                          lambda e: e.matmul(ps2[:, :], ones_bf[:, :], sqs[0], start=True, stop=True))
                    rope_evac(psA, BA, psB, BB, R, 2, cs, KT[0][:, cs], BKT)
                    rsqrt_act(lrs, Blrs, ps2[:, :], Bps2, 1.0 / 128)
                    TR.op("dve", [Blat, Blrs], [Bckvn],
                          lambda e: e.tensor_tensor(out=ckvn[:, cs], in0=lat32[:, lb, :], in1=lrs, op=ALU.mult))
                    for h in range(1, 4):
                        TR.op("pool", [BKT], [BKT],
                              lambda e, h=h: e.tensor_copy(out=KT[h][R, cs], in_=KT[0][R, cs]))

                def mla_B(c):
                    cs = chs(c)
                    for h in range(4):
                        ps, Bps = gbank()
                        TR.op("pe", [Bwm, Bckvn], [Bps],
                              lambda e, ps=ps, h=h: e.matmul(ps[0:64, :], wukv[:, h * 64:(h + 1) * 64], ckvn[:, cs],
                                                             start=True, stop=True))
                        TR.op("act", [Bps], [BKT],
                              lambda e, ps=ps, h=h: e.copy(out=KT[h][0:64, cs], in_=ps[0:64, :]))
                    for ti in range(4):
                        kt = c * 4 + ti
                        ps, Bps = gbank()
                        TR.op("pe", [Bwm, Bckvn], [Bps],
                              lambda e, ps=ps, kt=kt: e.matmul(ps[:, 0:256], ckvn[:, kt * 128:(kt + 1) * 128], wukv[:, 256:512],
                                                               start=True, stop=True))
                        TR.op("act", [Bps], [BV],
                              lambda e, ps=ps, kt=kt: e.copy(out=V[:, kt, :, 0:64],
                                                             in_=ps[:, 0:256].rearrange("p (h d) -> p h d", h=4)))
                    for h in range(4):
                        sumsq_max(KT[h][:, cs], BKT, slice(0, 96), ones_bf, h, 12)

                mla_A(0)
                for c in range(NCH):
                    if c + 1 < NCH:
                        mla_A(c + 1)
                    mla_B(c)

            if mx == "na":
                for h in range(4):
                    for v in range(2):
                        TR.dma(nst, nag_d[l, h, v], [], [Bnst])
                        TR.op("act", [Bnst], [Bnam], lambda e, h=h, v=v: e.activation(out=nam[:, h, v, :], in_=nst, func=AF.Exp))
            if not is_mla:
                pq = {"na": P_NA_Q, "swa": P_SW_Q, "dil": P_DI_Q}[mx]
                TR.dma(wq[0].rearrange("p k c -> p (k c)"), wscr["w_in"][l, pq], [], [Bwq])
                if mx in ("swa", "dil"):
                    pqs = {"swa": P_SW_QS, "dil": P_DI_QS}[mx]
                    TR.dma(wq[1].rearrange("p k c -> p (k c)"), wscr["w_in"][l, pqs], [], [Bwq])
            else:
                TR.dma(wq[0].rearrange("p k c -> p (k c)"), wscr["w_in"][l, P_ML_CQ], [], [Bwq])

            def head_map(h):
                if is_mla:
                    return h, slice(0, 96), h, h
                if mx == "swa":
                    half = h // 2
                    return h, slice(half * 64, half * 64 + 64), 0, half
                return h, slice((h % 2) * 64, (h % 2) * 64 + 64), h // 2, h

            def qtile_of(h):
                return (h % 2) if mx == "swa" else (h // 2)

            def pairs(qc, h):
                out = []
                for kt in range(16):
                    js = []
                    for j in range(4):
                        qt = 4 * qc + j
                        if mx == "mla":
                            ok = True
                        elif mx == "swa":
                            ok = abs(qt - kt) <= 1
                        elif mx == "dil":
                            ok = abs(qt - kt) <= 8
                        else:
                            if qt <= 1:
                                ok = kt <= 3
                            elif qt >= 14:
                                ok = kt >= 12
                            else:
                                ok = abs(qt - kt) <= 2
                        if ok:
                            js.append(j)
                    if js:
                        out.append((kt, js[0], js[-1]))
                return out

            def mask_ap(kt, qc, j0, j1, h):
                n = (j1 - j0 + 1) * 128
                q0 = (4 * qc + j0) * 128
                if mx == "swa":
                    o = q0 - 128 * kt + 128
                    return [(0, n, smask[:, o:o + n])]
                if mx == "dil":
                    o = q0 - 128 * kt + 1024
                    return [(0, n, dmask[:, o:o + n])]
                if mx == "na":
                    res = []
                    j = j0
                    while j <= j1:
                        qt = 4 * qc + j
                        v = 1 if (qt <= 1 or qt >= 14) else 0
                        j2 = j
                        while j2 + 1 <= j1 and (1 if (4 * qc + j2 + 1 <= 1 or 4 * qc + j2 + 1 >= 14) else 0) == v:
                            j2 += 1
                        o = (7 - 2 * (kt - qt)) * 64
                        w = (j2 - j + 1) * 128
                        res.append(((j - j0) * 128, w, nam[:, h, v, o:o + w]))
                        j = j2 + 1
                    return res
                return []

            def ss_sq(src, Bsrc, rows, i):
                TR.op("pool", [Bsrc], [Bsqs[i]],
                      lambda e: e.tensor_tensor(out=sqs[i][rows, :], in0=src[rows, :], in1=src[rows, :], op=ALU.mult))

            def ss_mm(rows, i, col):
                ps, Bps = gbank()
                TR.op("pe", [Bsqs[i], Bconst], [Bps],
                      lambda e: e.matmul(ps[:, :], ones_bf[rows, :], sqs[i][rows, :], start=True, stop=True))
                TR.op("dve", [Bps], [Bq2[col // 32]],
                      lambda e: e.reduce_max(out=stat[:, col:col + 1], in_=ps[:, :], axis=AX.X))

            def q_stages(qc):
                cs = chs(qc)
                qb = qc % 2
                SO = 32 * qb
                srows = slice(0, 96) if is_mla else full
                stages = []
                if not is_mla:
                    roped = mx in ("swa", "dil")

                    def s_proj():
                        for t in range(2):
                            hh = [h for h in range(4) if qtile_of(h) == t]
                            hh.sort(key=lambda h: head_map(h)[1].start)
                            dsts = [(QT[qb][h], head_map(h)[1]) for h in hh]
                            psA, BA = proj_fm(wq[0], Bwq, t * 128, 128, lambda kc: uT[:, kc, cs], 8, [BuT[qc]])
                            if roped:
                                psB, BB = proj_fm(wq[1], Bwq, t * 128, 128, lambda kc: uT[:, kc, cs], 8, [BuT[qc]])
                                rope_evac(psA, BA, psB, BB, full, 0, cs, None, BQT[qb], dsts=dsts)
                            else:
                                for (d_, rw) in dsts:
                                    TR.op("act", [BA], [BQT[qb]], lambda e, ps=psA: e.copy(out=d_[rw, :], in_=ps[rw, :]))
                    stages.append(s_proj)

                    def k2col(h):
                        _, rows, kt_, _ = head_map(h)
                        return 2 * kt_ + (1 if rows.start == 64 else 0)
                else:
                    cb = qc % 2

                    def s_lat():
                        for t in range(2):
                            ps, Bps = proj_fm(wq[0], Bwq, t * 128, 128, lambda kc: uT[:, kc, cs], 8, [BuT[qc]])
                            TR.op("act", [Bps], [Blat], lambda e, ps=ps, t=t: e.copy(out=lat32[:, t, :], in_=ps[:, :]))
                        TR.op("pool", [Blat], [Blsq],
                              lambda e: e.tensor_tensor(out=lsq, in0=lat32, in1=lat32, op=ALU.mult))

                    def s_norm():
                        ps2, Bps2 = gbank()

                        def mmss(e):
                            for kc in range(2):
                                ins = e.matmul(ps2[:, :], ones_bf[:, :], lsq[:, kc, :], start=(kc == 0), stop=(kc == 1))
                            return ins
                        TR.op("pe", [Blsq, Bconst], [Bps2], mmss)
                        rsqrt_act(lrs, Blrs, ps2[:, :], Bps2, 1.0 / 256)
                        TR.op("dve", [Blat, Blrs], [Bcqn[cb]],
                              lambda e: e.tensor_tensor(out=cqn[cb], in0=lat32,
                                                        in1=lrs.unsqueeze(1).to_broadcast([128, 2, CH]), op=ALU.mult))

                    def s_uq():
                        for h in range(4):
                            psA, BA = gbank()
                            psB, BB = gbank()

                            def mmq(e, ps, a, h=h):
                                for kc in range(2):
                                    ins = e.matmul(ps[0:96, :], wuq[:, kc, h, a, :], cqn[cb][:, kc, :], start=(kc == 0), stop=(kc == 1))
                                return ins
                            TR.op("pe", [Bwm, Bcqn[cb]], [BA], lambda e, ps=psA: mmq(e, ps, 0))
                            TR.op("pe", [Bwm, Bcqn[cb]], [BB], lambda e, ps=psB: mmq(e, ps, 1))
                            TR.op("act", [BA], [BQT[qb]], lambda e, ps=psA, h=h: e.copy(out=QT[qb][h][0:64, :], in_=ps[0:64, :]))
                            rope_evac(psA, BA, psB, BB, slice(64, 96), 2, cs, QT[qb][h], BQT[qb])
                    stages += [s_lat, s_norm, s_uq]

                    def k2col(h):
                        return h

                def s_sq01():
                    ss_sq(QT[qb][0], BQT[qb], srows, 0)
                    ss_sq(QT[qb][1], BQT[qb], srows, 1)

                def s_mm01():
                    ss_mm(srows, 0, SO + 4)
                    ss_mm(srows, 1, SO + 5)
                    ss_sq(QT[qb][2], BQT[qb], srows, 0)
                    ss_sq(QT[qb][3], BQT[qb], srows, 1)

                def s_mm23():
                    ss_mm(srows, 0, SO + 6)
                    ss_mm(srows, 1, SO + 7)

                def s_negm():
                    for h in range(4):
                        TR.op("dve", [Bq2[qb], Bk2], [Bneg[qb]],
                              lambda e, h=h: e.tensor_tensor(out=stat[:, SO + 8 + h:SO + 9 + h], in0=stat[:, SO + 4 + h:SO + 5 + h],
                                                             in1=stat[:, k2col(h):k2col(h) + 1], op=ALU.mult))
                    TR.op("act", [Bneg[qb]], [Bneg[qb]],
                          lambda e: e.activation(out=stat[:, SO + 8:SO + 12], in_=stat[:, SO + 8:SO + 12], func=AF.Ln))
                    TR.op("act", [Bneg[qb]], [Bneg[qb]],
                          lambda e: e.activation(out=stat[:, SO + 8:SO + 12], in_=stat[:, SO + 8:SO + 12], func=AF.Exp, scale=0.5))
                    TR.op("dve", [Bneg[qb]], [Bneg[qb]],
                          lambda e: e.tensor_scalar(out=stat[:, SO + 8:SO + 12], in0=stat[:, SO + 8:SO + 12], scalar1=-scale,
                                                    scalar2=None, op0=ALU.mult))
                    if mx == "swa":
                        for h in range(4):
                            TR.op("act", [Bneg[qb], Bconst], [Bneg[qb]],
                                  lambda e, h=h: e.activation(out=stat[:, SO + 16 + h:SO + 17 + h], in_=sink_sb[:, l, h:h + 1],
                                                              func=AF.Exp, bias=stat[:, SO + 8 + h:SO + 9 + h], scale=1.0))
                stages += [s_sq01, s_mm01, s_mm23, s_negm]
                return stages

            def attention(qc, inject):
                cs = chs(qc)
                qb = qc % 2
                SO = 32 * qb
                yb = 0
                items = []
                for h in range(4):
                    prs = pairs(qc, h)
                    for ii, (kt, j0, j1) in enumerate(prs):
                        items.append((h, kt, j0, j1, ii == 0, ii == len(prs) - 1))
                DEPTH = NPT - 1
                hob = {}
                slots = {}

                def emit_score(i):
                    h, kt, j0, j1, _, _ = items[i]
                    qt_i, rows, kt_i, vh = head_map(h)
                    n = (j1 - j0 + 1) * 128
                    crow = rows if is_mla else full
                    ps, Bps = gbank()
                    TR.op("pe", [BKT, BQT[qb]], [Bps],
                          lambda e: e.matmul(ps[:, 0:n], KT[kt_i][crow, kt * 128:(kt + 1) * 128],
                                             QT[qb][qt_i][crow, j0 * 128:j0 * 128 + n], start=True, stop=True))
                    rr["pt"] = rr.get("pt", 0) + 1
                    pi = rr["pt"] % NPT
                    slots[i] = pi
                    TR.op("act", [Bps, Bneg[qb]], [BPT[pi]],
                          lambda e: e.activation(out=PT[pi][:, 0:n], in_=ps[:, 0:n], func=AF.Exp,
                                                 bias=stat[:, SO + 8 + h:SO + 9 + h], scale=scale))
                    for (o, w, map_) in mask_ap(kt, qc, j0, j1, h):
                        rr["mk"] = rr.get("mk", 0) + 1
                        en = "pool" if rr["mk"] % 4 == 0 else "dve"
                        rd = [Bconst] if mx != "na" else [Bnam]
                        TR.op(en, [BPT[pi]] + rd, [BPT[pi]],
                              lambda e: e.tensor_tensor(out=PT[pi][:, o:o + w], in0=PT[pi][:, o:o + w], in1=map_, op=ALU.mult))

                def emit_pv(i):
                    h, kt, j0, j1, first, last = items[i]
                    qt_i, rows, kt_i, vh = head_map(h)
                    pi = slots[i]
                    if first:
                        hob[h] = obank()
                    ob, Bob = hob[h]

                    def pv(e):
                        for j in range(j0, j1 + 1):
                            ins = e.matmul(ob[:, j * 128:j * 128 + 65], PT[pi][:, (j - j0) * 128:(j - j0 + 1) * 128],
                                           V[:, kt, vh, :], start=(first and j == j0), stop=(last and j == j1))
                        return ins
                    TR.op("pe", [BPT[pi], BV], [Bob], pv)
                    if last:
                        obv = ob[:, :].rearrange("p (j c) -> p j c", j=4)
                        if mx == "swa":
                            TR.op("dve", [Bob, Bneg[qb]], [Brec],
                                  lambda e: e.tensor_scalar(out=stat[:, 20:24], in0=obv[:, :, 64], scalar1=stat[:, SO + 16 + h:SO + 17 + h],
                                                            scalar2=None, op0=ALU.add))
                            TR.op("dve", [Brec], [Brec], lambda e: e.reciprocal(out=stat[:, 20:24], in_=stat[:, 20:24]))
                        else:
                            TR.op("dve", [Bob], [Brec], lambda e: e.reciprocal(out=stat[:, 20:24], in_=obv[:, :, 64]))
                        TR.op("dve", [Bob, Brec], [Bych[yb]],
                              lambda e: e.tensor_tensor(out=ych[yb][:, :, h, :], in0=obv[:, :, 0:64],
                                                        in1=stat[:, 20:24].unsqueeze(2).to_broadcast([128, 4, 64]),
                                                        op=ALU.mult))

                for i in range(len(items) + DEPTH):
                    if i < len(items):
                        emit_score(i)
                    if i - DEPTH >= 0:
                        emit_pv(i - DEPTH)
                    while inject and inject[0][0] <= i:
                        inject.pop(0)[1]()
                while inject:
                    inject.pop(0)[1]()

            def fin1(qc):
                yb = 0
                yv = ych[yb].rearrange("p j h d -> p j (h d)")
                TR.op("pool", [Bych[yb]], [Bysq], lambda e, yv=yv: e.tensor_tensor(out=ysq, in0=yv, in1=yv, op=ALU.mult))
                TR.op("dve", [Bysq], [Bgn], lambda e: e.reduce_sum(out=stat[:, 24:28], in_=ysq, axis=AX.X))
                rsqrt_act(stat[:, 24:28], Bgn, stat[:, 24:28], Bgn, 1.0 / 256)
                TR.op("dve", [Bych[yb], Bgn], [Bych[yb]],
                      lambda e, yv=yv: e.tensor_tensor(out=yv, in0=yv, in1=stat[:, 24:28].unsqueeze(2).to_broadcast([128, 4, 256]),
                                                       op=ALU.mult))

            def fin2(qc):
                cs = chs(qc)
                yb = 0
                yv = ych[yb].rearrange("p j h d -> p j (h d)")
                for cc in range(2):
                    ps, Bps = gbank()

                    def tp(e, ps=ps, cc=cc):
                        for j in range(4):
                            ins = e.transpose(ps[:, j * 128:(j + 1) * 128], yv[:, j, cc * 128:(cc + 1) * 128], ident[:, :])
                        return ins
                    TR.op("pe", [Bych[yb], Bconst], [Bps], tp)
                    TR.op("act", [Bps], [BynT[yb]], lambda e, ps=ps, cc=cc: e.copy(out=ynT[yb][:, cc, :], in_=ps[:, :]))
                for dc in range(8):
                    ps, Bps = gbank()

                    def mmo(e, ps=ps, dc=dc):
                        for cc in range(2):
                            ins = e.matmul(ps[:, :], wo[:, cc, dc * 128:(dc + 1) * 128], ynT[yb][:, cc, :], start=(cc == 0), stop=(cc == 1))
                        return ins
                    TR.op("pe", [Bwo, BynT[yb]], [Bps], mmo)
                    TR.op("dve", [Bps, BxT[qc]], [BxT[qc]],
                          lambda e, ps=ps, dc=dc: e.tensor_tensor(out=xT[:, dc, cs], in0=ps[:, :], in1=xT[:, dc, cs], op=ALU.add))

            for st_ in q_stages(0):
                st_()
            for qc in range(NCH):
                inj = []
                if qc > 0:
                    inj.append((3, lambda qc=qc: fin2(qc - 1)))
                if qc + 1 < NCH:
                    for k_, st_ in enumerate(q_stages(qc + 1)):
                        inj.append((6 + 3 * k_, st_))
                attention(qc, inj)
                fin1(qc)
            fin2(NCH - 1)

        for s_idx in range(n_seq):
            AR.reset()
            xin = [AR.alloc(D * 4, F32) for _ in range(2)]
            Bxin = [Buf("xin0"), Buf("xin1")]
            for tt in range(16):
                s = tt % 2
                c = tt // 4
                TR.dma(xin[s], x_d[s_idx, tt * 128:(tt + 1) * 128, :], [], [Bxin[s]])
                for hb in range(2):
                    ps, Bps = gbank()

                    def tp(e, ps=ps, s=s, hb=hb):
                        for i in range(4):
                            kc = hb * 4 + i
                            ins = e.transpose(ps[:, i * 128:(i + 1) * 128], xin[s][:, kc * 128:(kc + 1) * 128], ident[:, :])
                        return ins
                    TR.op("pe", [Bxin[s], Bconst], [Bps], tp)
                    TR.op(alt("act", "dve"), [Bps], [BxT[c]],
                          lambda e, ps=ps, hb=hb, tt=tt: (e.copy if e is nc.scalar else e.tensor_copy)(
                              out=xT[:, hb * 4:hb * 4 + 4, tt * 128:(tt + 1) * 128],
                              in_=ps[:, :].rearrange("p (k t) -> p k t", k=4)))
            TR.barrier()
            carry = False
            for l in range(L):
                has_mix = bool(flags & {"na", "swa", "dil", "mla"})
                bg_on = BG_OK and s_idx == 0 and (l + 1) in bg_tasks
                if bg_on:
                    bg["q"] = list(bg_tasks[l + 1])
                if "ffn1" in flags:
                    ffn_phase(l, 1, post_norm=has_mix, pre_normed=carry, bg_on=bg_on)
                    carry = False
                if has_mix:
                    mixer_phase(l)
                if "ffn2" in flags:
                    ffn_phase(l, 2, post_norm=("ple" in flags), bg_on=bg_on)
                if "ple" in flags:
                    nxt = ("ffn1" in flags) and (l + 1 < L)
                    ple_phase(l, s_idx, pre_normed=("ffn2" in flags), post_norm=nxt)
                    carry = nxt
            AR.reset()
            sq = AR.alloc(8 * CH * 2).rearrange("p (k t) -> p k t", k=8)
            Bsq = Buf("sq")
            rstd = AR.alloc(CH * 4, F32)
            Brstd = Buf("rstd")
            yn = [AR.alloc(8 * 128 * 4, F32).rearrange("p (k t) -> p k t", k=8) for _ in range(2)]
            Byn = [Buf("yn0"), Buf("yn1")]
            yo = [AR.alloc(D * 4, F32) for _ in range(2)]
            Byo = [Buf("yo0"), Buf("yo1")]
            Bout = [Buf("out0"), Buf("out1")]
            for c in range(NCH):
                cs = chs(c)
                TR.op("pool", [BxT[c]], [Bsq],
                      lambda e, cs=cs: e.tensor_tensor(out=sq, in0=xT[:, :, cs], in1=xT[:, :, cs], op=ALU.mult))
                ps, Bps = gbank()

                def mm(e, ps=ps):
                    for kc in range(8):
                        ins = e.matmul(ps[:, :], ones_bf[:, :], sq[:, kc, :], start=(kc == 0), stop=(kc == 7))
                    return ins
                TR.op("pe", [Bsq, Bconst], [Bps], mm)
                rsqrt_act(rstd, Brstd, ps[:, :], Bps, 1.0 / D)
                for ti in range(4):
                    tt = c * 4 + ti
                    s = tt % 2
                    tsl = slice(tt * 128, (tt + 1) * 128)
                    TR.op("dve", [BxT[c], Brstd], [Byn[s]],
                          lambda e, s=s, tsl=tsl, ti=ti: e.tensor_tensor(
                              out=yn[s], in0=xT[:, :, tsl],
                              in1=rstd[:, ti * 128:(ti + 1) * 128].unsqueeze(1).to_broadcast([128, 8, 128]), op=ALU.mult))
                    TR.op("pool", [Byn[s], Bconst], [Byn[s]],
                          lambda e, s=s: e.tensor_tensor(out=yn[s], in0=yn[s], in1=gfin[:, :].unsqueeze(2).to_broadcast([128, 8, 128]),
                                                         op=ALU.mult))
                    for hb in range(2):
                        ps, Bps = gbank()

                        def tp(e, ps=ps, s=s, hb=hb):
                            for i in range(4):
                                ins = e.transpose(ps[:, i * 128:(i + 1) * 128], yn[s][:, hb * 4 + i, :], ident[:, :])
                            return ins
                        TR.op("pe", [Byn[s], Bconst], [Bps], tp)
                        TR.op(alt("act", "dve"), [Bps], [Byo[s]],
                              lambda e, ps=ps, s=s, hb=hb: (e.copy if e is nc.scalar else e.tensor_copy)(
                                  out=yo[s][:, hb * 512:(hb + 1) * 512], in_=ps[:, :]))
                    TR.dma(y_d[s_idx, tsl, :], yo[s], [Byo[s]], [], sembuf=Bout[s])
            TR.barrier()
        TR.barrier()
        info = {k: v["count"] + sum(c for _, c in v["old"]) for k, v in TR.engs.items()}
        info["nsem"] = TR.nsem
        print("program: engine op counts", info)
    return nc


def _pieces_kc(w, C):
    Lw, K, N = w.shape
    nk = K // 128
    a = w.reshape(Lw, nk, 128, N // C, C).transpose(0, 3, 2, 1, 4)
    return np.ascontiguousarray(a.reshape(Lw, N // C, 128, nk * C))


def _swap_halves(w, hd):
    sh = w.shape
    a = w.reshape(sh[:-1] + (sh[-1] // hd, 2, hd // 2))
    return a[..., ::-1, :].reshape(sh)


def _gain_cols(g):
    Lg, K = g.shape
    return g.reshape(Lg, K // 128, 128).transpose(2, 0, 1)


def prepare_weights(inp, Lw):
    f = lambda a: np.asarray(a, dtype=np.float32)
    out = {}
    ffn_w = {1: (inp["ffn1_w_gate"], inp["ffn1_w_up"], inp["ffn1_w_down"]),
             2: (inp["ffn2_w_gate"], inp["ffn2_w_up"], inp["ffn2_w_down"])}
    for i in (1, 2):
        g = _pieces_kc(f(ffn_w[i][0])[:Lw], 256)
        u = _pieces_kc(f(ffn_w[i][1])[:Lw], 256)
        out["w_gu%d" % i] = np.ascontiguousarray(np.concatenate([g, u], axis=1))
        wd = f(ffn_w[i][2])[:Lw]
        a = wd.reshape(Lw, NFC, 128, 8, 128).transpose(0, 3, 2, 1, 4)
        out["w_d%d" % i] = np.ascontiguousarray(a.reshape(Lw, 8, 128, NFC * 128))
    win = f(inp["w_in"])[:Lw]
    z = np.zeros((Lw, D, 64), np.float32)
    na, sw, di, ml = win[..., 0:768], win[..., 768:1280], win[..., 1280:2048], win[..., 2048:2464]
    swq = sw[..., 0:256].reshape(Lw, D, 4, 64)[:, :, [0, 2, 1, 3], :].reshape(Lw, D, 256)
    swk = sw[..., 256:384]
    kpe = ml[..., 384:416]
    z32 = z[..., 0:32]
    kpe_t = np.concatenate([z, kpe, z32, z, _swap_halves(kpe, 32), z32], axis=-1)
    pad128 = np.zeros((Lw, D, 128), np.float32)
    cols = [
        na[..., 0:256], na[..., 256:512], na[..., 512:768],
        swq, _swap_halves(swq, 64), np.concatenate([swk, _swap_halves(swk, 64)], axis=-1),
        np.concatenate([sw[..., 384:512], pad128], axis=-1),
        di[..., 0:256], _swap_halves(di[..., 0:256], 64), di[..., 256:512], _swap_halves(di[..., 256:512], 64),
        di[..., 512:768],
        ml[..., 0:256], np.concatenate([ml[..., 256:384], pad128], axis=-1), kpe_t,
    ]
    out["w_in"] = np.ascontiguousarray(np.concatenate([_pieces_kc(np.ascontiguousarray(c), 256) for c in cols], axis=1))
    wo = f(inp["w_out"])[:Lw]
    out["w_out"] = np.ascontiguousarray(
        wo.reshape(Lw, 4, 2, 128, 1024).transpose(0, 1, 3, 2, 4).reshape(Lw, 4, 128, 2048))
    out["w_pg"] = _pieces_kc(f(inp["ple_w_gate"])[:Lw], 256)
    out["w_pp"] = _pieces_kc(f(inp["ple_w_proj"])[:Lw], 1024)
    uq = f(inp["mla_w_uq"])[:Lw].reshape(Lw, 256, 4, 96)
    uqB = np.concatenate([np.zeros((Lw, 256, 4, 64), np.float32), _swap_halves(uq[..., 64:96], 32)], axis=-1)
    uqAB = np.stack([uq, uqB], axis=3)
    out["w_uq"] = np.ascontiguousarray(
        uqAB.reshape(Lw, 2, 128, 768).transpose(0, 2, 1, 3).reshape(Lw, 1, 128, 1536))
    ukv = f(inp["mla_w_ukv"])[:Lw].reshape(Lw, 128, 4, 128)
    out["w_ukv"] = np.ascontiguousarray(
        np.concatenate([ukv[..., 0:64].reshape(Lw, 128, 256), ukv[..., 64:128].reshape(Lw, 128, 256)], axis=-1)
    ).reshape(Lw, 1, 128, 512)
    g = np.zeros((128, Lw, NG), np.float32)
    g[:, :, G_FFN1:G_FFN1 + 8] = _gain_cols(f(inp["ffn1_norm"])[:Lw])
    g[:, :, G_MIX:G_MIX + 8] = _gain_cols(f(inp["mix_norm"])[:Lw])
    g[:, :, G_FFN2:G_FFN2 + 8] = _gain_cols(f(inp["ffn2_norm"])[:Lw])
    g[:, :, G_PLE:G_PLE + 8] = _gain_cols(f(inp["ple_norm"])[:Lw])
    g[:, :, G_GRP:G_GRP + 8] = _gain_cols(f(inp["group_norm"])[:Lw].reshape(Lw, 1024))
    g[:, :, G_QN:G_QN + 2] = _gain_cols(f(inp["mla_q_norm"])[:Lw])
    g[:, :, G_KVN:G_KVN + 1] = _gain_cols(f(inp["mla_kv_norm"])[:Lw])
    out["g_all"] = g
    out["final_g"] = np.ascontiguousarray(f(inp["final_norm"]).reshape(8, 128).T)
    out["sink_b"] = np.ascontiguousarray(np.broadcast_to(f(inp["swa_sink"])[:Lw][None], (128, Lw, 4)))
    nb = f(inp["na_bias"])[:Lw]
    kap = np.arange(64)[:, None]
    cc = np.arange(64)[None, :]
    ws = np.clip(cc - 8, 0, 48)
    cvalid = (kap >= ws) & (kap < ws + 16)
    dcol = np.clip(kap - cc, -15, 15) + 15
    nag = np.full((Lw, 4, 2, 128, 16, 64), NEG, np.float32)
    for i in range(2):
        for slot in range(16):
            dr = 14 - slot + i
            if dr < 0 or dr > 14:
                continue
            vals = nb[:, :, dr][:, :, dcol]
            vals = np.where(cvalid[None, None], vals, np.float32(NEG))
            nag[:, :, 1, i * 64:(i + 1) * 64, slot, :] = vals
            if 3 <= dr <= 10:
                nag[:, :, 0, i * 64:(i + 1) * 64, slot, :] = vals
    out["na_g"] = nag.reshape(Lw, 4, 2, 128, 1024)
    return out


def constants():
    pos = np.arange(T, dtype=np.float32)
    rope = np.zeros((4, 128, T), np.float32)
    inv64 = (10000.0 ** (-np.arange(32, dtype=np.float32) / 32)).astype(np.float32)
    ang = pos[None, :] * inv64[:, None]
    c64, s64 = np.cos(ang), np.sin(ang)
    for hh in range(2):
        rope[0, hh * 64:hh * 64 + 32] = c64
        rope[0, hh * 64 + 32:hh * 64 + 64] = c64
        rope[1, hh * 64:hh * 64 + 32] = -s64
        rope[1, hh * 64 + 32:hh * 64 + 64] = s64
    inv32 = (10000.0 ** (-np.arange(16, dtype=np.float32) / 16)).astype(np.float32)
    ang = pos[None, :] * inv32[:, None]
    c32, s32 = np.cos(ang), np.sin(ang)
    rope[2, 64:80] = c32
    rope[2, 80:96] = c32
    rope[3, 64:80] = -s32
    rope[3, 80:96] = s32
    kap = np.arange(128)[:, None]
    j = np.arange(2176)[None, :]
    d = j - kap - 1024
    dm = ((np.abs(d) <= 64).astype(np.float32) + ((d % 4 == 0) & (np.abs(d) <= 256)).astype(np.float32)
          + ((d % 16 == 0) & (np.abs(d) <= 1024)).astype(np.float32))
    j = np.arange(384)[None, :]
    sm = (np.abs(j - kap - 128) <= 128).astype(np.float32)
    return {"rope_t": rope.astype(np.float32), "dil_mask": dm.astype(np.float32), "swa_mask": sm,
            "ident": np.eye(128, dtype=np.float32)}


_CACHE = {}


def kernel(**inputs):
    n_cores = 8
    x = np.asarray(inputs["x"], dtype=np.float32)
    p = np.asarray(inputs["p"], dtype=np.float32)
    w = prepare_weights(inputs, L_ALL)
    w.update(constants())
    if "nc" not in _CACHE:
        _CACHE["nc"] = build_program(L_ALL, 2, ALL_FLAGS)
    nc = _CACHE["nc"]
    in_maps = []
    for c in range(n_cores):
        m = dict(w)
        m["x"] = np.ascontiguousarray(x[2 * c:2 * c + 2])
        m["p"] = np.ascontiguousarray(p[:, 2 * c:2 * c + 2])
        in_maps.append(m)
    res = run_bass_kernel_spmd(nc, in_maps, core_ids=list(range(n_cores)))
    return np.concatenate([np.asarray(r["y"], dtype=np.float32) for r in res.results], axis=0)
```

```python
import math
from contextlib import ExitStack

import numpy as np
import concourse.bass as bass
import concourse.mybir as mybir
from concourse.bass_utils import run_bass_kernel_spmd

F32 = mybir.dt.float32
BF16 = mybir.dt.bfloat16
AF = mybir.ActivationFunctionType
ALU = mybir.AluOpType
AX = mybir.AxisListType

D = 1024
T = 2048
L_ALL = 4
DFF = 2816
NFC = 22
NCH = 4
CH = 512
EPS = 1e-6
NEG = -30000.0
ALL_FLAGS = ("ffn1", "na", "swa", "dil", "mla", "ffn2", "ple")

G_FFN1, G_MIX, G_FFN2, G_PLE, G_GRP, G_QN, G_KVN, NG = 0, 8, 16, 24, 32, 40, 42, 43

(P_NA_Q, P_NA_K, P_NA_V, P_SW_Q, P_SW_QS, P_SW_K, P_SW_V, P_DI_Q, P_DI_QS, P_DI_K, P_DI_KS,
 P_DI_V, P_ML_CQ, P_ML_CKV, P_ML_KPE) = range(15)
N_WIN = 15

WSPEC = [
    ("w_gu1", 22, 8, 256, G_FFN1),
    ("w_d1", 8, 22, 128, None),
    ("w_in", N_WIN, 8, 256, G_MIX),
    ("w_out", 4, 2, 1024, G_GRP),
    ("w_uq", 1, 2, 768, G_QN),
    ("w_ukv", 1, 1, 512, G_KVN),
    ("w_gu2", 22, 8, 256, G_FFN2),
    ("w_d2", 8, 22, 128, None),
    ("w_pg", 4, 8, 256, G_PLE),
    ("w_pp", 1, 2, 1024, None),
]


class Buf:
    __slots__ = ("name", "last_w", "readers", "dsem", "dcount")

    def __init__(self, name):
        self.name = name
        self.last_w = None
        self.readers = {}
        self.dsem = None
        self.dcount = 0


class Tracker:
    EPOCH = 30000

    def __init__(self, nc, stack):
        self.nc = nc
        self.stack = stack
        self.engs = {}
        self.nsem = 0
        self.dbufs = []
        self.free_sems = []
        for nm, e in (("pe", nc.tensor), ("act", nc.scalar), ("dve", nc.vector),
                      ("pool", nc.gpsimd), ("sp", nc.sync)):
            self.engs[nm] = dict(name=nm, eng=e, sem=self._newsem("s_" + nm), count=0, seen={}, old=[])

    def _newsem(self, name):
        self.nsem += 1
        return self.stack.enter_context(self.nc.semaphore("%s_%d" % (name, self.nsem)))

    def _waits(self, E, reads, writes):
        deps = {}

        def add(rec, raw):
            sem, val = rec
            if sem is E["sem"] and not raw and E["name"] == "pe":
                return
            k = id(sem)
            if k not in deps or deps[k][1] < val:
                deps[k] = (sem, val)

        for b in reads:
            if b.last_w is not None:
                add(b.last_w, True)
        for b in writes:
            if b.last_w is not None:
                add(b.last_w, False)
            for rec in b.readers.values():
                add(rec, False)
        for k, (sem, val) in deps.items():
            if E["seen"].get(k, 0) < val:
                E["eng"].wait_ge(sem, val)
                E["seen"][k] = val

    def _record(self, rec, reads, writes):
        k = id(rec[0])
        for b in reads:
            b.readers[k] = rec
        for b in writes:
            b.last_w = rec
            b.readers = {}

    def op(self, en, reads, writes, fn):
        E = self.engs[en]
        self._waits(E, reads, writes)
        inst = fn(E["eng"])
        if E["count"] >= self.EPOCH:
            E["old"].append((E["sem"], E["count"]))
            E["sem"] = self._newsem("s_" + en)
            E["count"] = 0
        E["count"] += 1
        inst.then_inc(E["sem"], 1)
        self._record((E["sem"], E["count"]), reads, writes)

    def dma(self, out, in_, reads, writes, sembuf=None, qn="sp"):
        E = self.engs[qn]
        self._waits(E, reads, writes)
        sb = sembuf if sembuf is not None else writes[0]
        if sb.dsem is None:
            if self.free_sems:
                sb.dsem, sb.dcount = self.free_sems.pop()
            else:
                sb.dsem, sb.dcount = self._newsem("d"), 0
            self.dbufs.append(sb)
        sb.dcount += 16
        E["eng"].dma_start(out=out, in_=in_).then_inc(sb.dsem, 16)
        self._record((sb.dsem, sb.dcount), reads, writes)

    def barrier(self):
        recs = []
        for E in self.engs.values():
            if E["count"] > 0:
                recs.append((E["sem"], E["count"]))
        for b in self.dbufs:
            if b.dcount > 0:
                recs.append((b.dsem, b.dcount))
        for E in self.engs.values():
            for sem, val in recs:
                k = id(sem)
                if E["seen"].get(k, 0) < val:
                    E["eng"].wait_ge(sem, val)
                    E["seen"][k] = val
        for b in self.dbufs:
            if b.dcount < 24000:
                self.free_sems.append((b.dsem, b.dcount))
            b.dsem = None
        self.dbufs = []

    def finish(self, bufs, qn="sp"):
        self._waits(self.engs[qn], bufs, [])


class Arena:
    def __init__(self, t, nbytes):
        self.t = t
        self.nbytes = nbytes
        self.off = 0

    def reset(self):
        self.off = 0

    def alloc(self, nbytes, dtype=BF16):
        nbytes = (nbytes + 63) // 64 * 64
        o = self.off
        self.off += nbytes
        assert self.off <= self.nbytes, ("arena overflow", self.off, self.nbytes)
        v = self.t[:, o // 2:(o + nbytes) // 2]
        if dtype == F32:
            v = v.bitcast(F32)
        return v


def build_program(n_layers=L_ALL, n_seq=2, flags=ALL_FLAGS):
    nc = bass.Bass("TRN2", target_bir_lowering=False)
    L = n_layers
    flags = set(flags)
    dram = {}

    def din(name, shape, dt=F32):
        dram[name] = nc.dram_tensor(name, list(shape), dt, kind="ExternalInput").ap()
        return dram[name]

    x_d = din("x", [n_seq, T, D])
    p_d = din("p", [L, n_seq, T, 256])
    wsrc = {}
    wscr = {}
    for (nm, npc, nk, C, goff) in WSPEC:
        wsrc[nm] = din(nm, [L, npc, 128, nk * C])
        wscr[nm] = nc.dram_tensor("s_" + nm, [L, npc, 128, nk * C], BF16, kind="Internal").ap()
    g_d = din("g_all", [128, L, NG])
    gfin_d = din("final_g", [128, 8])
    nag_d = din("na_g", [L, 4, 2, 128, 1024])
    sink_d = din("sink_b", [128, L, 4])
    rope_d = din("rope_t", [4, 128, T])
    dmask_d = din("dil_mask", [128, 2176])
    smask_d = din("swa_mask", [128, 384])
    ident_d = din("ident", [128, 128])
    y_d = nc.dram_tensor("y", [n_seq, T, D], F32, kind="ExternalOutput").ap()

    with ExitStack() as st:
        TR = Tracker(nc, st)
        sb = nc.alloc_sbuf_tensor
        xT = sb("xT", [128, 8, T], F32)
        uT = sb("uT", [128, 8, T], BF16)
        rope = sb("rope", [128, 4, T], BF16)
        dmask = sb("dmask", [128, 2176], BF16)
        smask = sb("smask", [128, 384], BF16)
        ident = sb("ident_sb", [128, 128], F32)
        ones_bf = sb("ones_bf", [128, 128], BF16)
        onesA = sb("onesA", [128, 128], BF16)
        onesB = sb("onesB", [128, 128], BF16)
        g_sb = sb("g_sb", [128, L, NG], F32)
        gfin = sb("gfin", [128, 8], F32)
        sink_sb = sb("sink_sb", [128, L, 4], F32)
        nhalf = sb("nhalf", [128, 1], F32)
        eps_t = sb("eps_t", [128, 1], F32)
        stat = sb("stat", [128, 64], F32)
        ARENA_BYTES = 87 * 1024
        arena_t = sb("arena", [128, ARENA_BYTES // 2], BF16)
        AR = Arena(arena_t, ARENA_BYTES)
        banks = [nc.alloc_psum_tensor("bank%d" % i, [128, 512], F32) for i in range(8)]
        Bbanks = [Buf("bank%d" % i) for i in range(8)]
        bank_rr = [0]
        obank_rr = [0]

        def gbank():
            i = bank_rr[0] % 6
            bank_rr[0] += 1
            return banks[i], Bbanks[i]

        def obank():
            i = 6 + obank_rr[0] % 2
            obank_rr[0] += 1
            return banks[i], Bbanks[i]

        BxT = [Buf("xT%d" % c) for c in range(NCH)]
        BuT = [Buf("uT%d" % c) for c in range(NCH)]
        Bconst = Buf("const")
        Bstat = Buf("stat"); Bk2 = Buf("k2"); Bq2 = [Buf("q2a"), Buf("q2b")]; Bneg = [Buf("nga"), Buf("ngb")]; Brecs = [Buf("rec0"), Buf("rec1")]; Bgn = Buf("gn")
        Bscr = Buf("scratch")
        rr = {"n": 0}

        def alt(a="dve", b="pool"):
            rr["n"] += 1
            return a if rr["n"] % 2 else b

        st.enter_context(nc.Block())

        AR.reset()
        c32 = AR.alloc(2176 * 4, F32)
        Bc32 = Buf("c32")
        for i in range(4):
            TR.dma(c32[:, 0:T], rope_d[i], [], [Bc32])
            TR.op("dve", [Bc32], [Bconst], lambda e, i=i: e.tensor_copy(out=rope[:, i, :], in_=c32[:, 0:T]))
        TR.dma(c32[:, 0:2176], dmask_d, [], [Bc32])
        TR.op("dve", [Bc32], [Bconst], lambda e: e.tensor_copy(out=dmask[:, :], in_=c32[:, 0:2176]))
        TR.dma(c32[:, 0:384], smask_d, [], [Bc32])
        TR.op("dve", [Bc32], [Bconst], lambda e: e.tensor_copy(out=smask[:, :], in_=c32[:, 0:384]))
        TR.dma(ident[:, :], ident_d, [], [Bconst])
        TR.dma(g_sb[:, :, :], g_d, [], [Bconst])
        TR.dma(gfin[:, :], gfin_d, [], [Bconst])
        TR.dma(sink_sb[:, :, :], sink_d, [], [Bconst])
        TR.op("dve", [], [Bconst], lambda e: e.memset(ones_bf[:, :], 1.0))
        TR.op("dve", [], [Bconst], lambda e: e.memset(onesA[:, :], 0.0))
        TR.op("dve", [], [Bconst], lambda e: e.memset(onesB[:, :], 0.0))
        TR.op("dve", [Bconst], [Bconst], lambda e: e.memset(onesA[0:64, :], 1.0))
        TR.op("dve", [Bconst], [Bconst], lambda e: e.memset(onesB[64:128, :], 1.0))
        TR.op("dve", [], [Bconst], lambda e: e.memset(nhalf[:, :], -0.5))
        TR.op("dve", [], [Bconst], lambda e: e.memset(eps_t[:, :], EPS))
        TR.barrier()

        AR.reset()
        FMAX = 2816
        NSL = 5
        s32 = [AR.alloc(FMAX * 4, F32) for _ in range(NSL)]
        s16 = [AR.alloc(FMAX * 2) for _ in range(NSL)]
        Bs32 = [Buf("s32_%d" % i) for i in range(NSL)]
        Bs16 = [Buf("s16_%d" % i) for i in range(NSL)]
        Bst = [Buf("st_%d" % i) for i in range(NSL)]
        used = set()
        if "ffn1" in flags:
            used |= {"w_gu1", "w_d1"}
        if "ffn2" in flags:
            used |= {"w_gu2", "w_d2"}
        if flags & {"na", "swa", "dil", "mla"}:
            used |= {"w_in", "w_out"}
        if "mla" in flags:
            used |= {"w_uq", "w_ukv"}
        if "ple" in flags:
            used |= {"w_pg", "w_pp"}
        BG_OK = ("ffn1" in flags) and ("ffn2" in flags)
        tasks = []
        bg_tasks = {}
        for l in range(L):
            for (nm, npc, nk, C, goff) in WSPEC:
                if nm in used:
                    for pi in range(npc):
                        if l == 0 or not BG_OK:
                            tasks.append((l, nm, pi, nk, C, goff))
                        else:
                            F_ = nk * C
                            if F_ <= 2048:
                                bg_tasks.setdefault(l, []).append((l, nm, pi, nk, C, goff, 0, F_))
                            else:
                                assert goff is None and F_ % 2 == 0
                                bg_tasks.setdefault(l, []).append((l, nm, pi, nk, C, goff, 0, F_ // 2))
                                bg_tasks.setdefault(l, []).append((l, nm, pi, nk, C, goff, F_ // 2, F_))

        def pp_load(k):
            l, nm, pi, nk, C, goff = tasks[k]
            s = k % NSL
            TR.dma(s32[s][:, 0:nk * C], wsrc[nm][l, pi], [], [Bs32[s]])

        for k in range(min(NSL - 1, len(tasks))):
            pp_load(k)
        for k in range(len(tasks)):
            l, nm, pi, nk, C, goff = tasks[k]
            F = nk * C
            s = k % NSL
            if k + NSL - 1 < len(tasks):
                pp_load(k + NSL - 1)
            en = "pool" if k % 4 == 3 else "dve"
            if goff is None:
                TR.op(en, [Bs32[s]], [Bs16[s]],
                      lambda e: e.tensor_copy(out=s16[s][:, 0:F], in_=s32[s][:, 0:F]))
            else:
                go = goff + (2 * pi if nm == "w_out" else 0)
                gv = g_sb[:, l, go:go + nk].unsqueeze(2).to_broadcast([128, nk, C])
                TR.op(en, [Bs32[s], Bconst], [Bs16[s]],
                      lambda e: e.tensor_tensor(
                          out=s16[s][:, 0:F].rearrange("p (k c) -> p k c", k=nk),
                          in0=s32[s][:, 0:F].rearrange("p (k c) -> p k c", k=nk),
                          in1=gv, op=ALU.mult))
            TR.dma(wscr[nm][l, pi], s16[s][:, 0:F], [Bs16[s]], [], sembuf=Bst[s], qn="act")
        TR.barrier()

        def chs(c):
            return slice(c * CH, (c + 1) * CH)

        bg = {"q": [], "t": 0, "pipe": {}}

        def bg_begin():
            bg["s32"] = [AR.alloc(2048 * 4, F32) for _ in range(2)]
            bg["s16"] = [AR.alloc(2048 * 2) for _ in range(2)]
            bg["B32"] = [Buf("b32_0"), Buf("b32_1")]
            bg["B16"] = [Buf("b16_0"), Buf("b16_1")]
            bg["Bst"] = [Buf("bst_0"), Buf("bst_1")]
            bg["pipe"] = {}

        def bg_step(allow_load=True):
            t = bg["t"]
            pipe = bg["pipe"]
            if (t - 2) in pipe:
                (l_, nm, pi, nk, C, goff, a, b) = pipe.pop(t - 2)
                i = (t - 2) % 2
                TR.dma(wscr[nm][l_, pi][:, a:b], bg["s16"][i][:, 0:b - a], [bg["B16"][i]], [], sembuf=bg["Bst"][i], qn="sp")
            if (t - 1) in pipe:
                (l_, nm, pi, nk, C, goff, a, b) = pipe[t - 1]
                i = (t - 1) % 2
                F_ = b - a
                if goff is None:
                    TR.op("dve", [bg["B32"][i]], [bg["B16"][i]],
                          lambda e: e.tensor_copy(out=bg["s16"][i][:, 0:F_], in_=bg["s32"][i][:, 0:F_]))
                else:
                    go = goff + (2 * pi if nm == "w_out" else 0)
                    gv = g_sb[:, l_, go:go + nk].unsqueeze(2).to_broadcast([128, nk, C])
                    TR.op("dve", [bg["B32"][i], Bconst], [bg["B16"][i]],
                          lambda e: e.tensor_tensor(
                              out=bg["s16"][i][:, 0:F_].rearrange("p (k c) -> p k c", k=nk),
                              in0=bg["s32"][i][:, 0:F_].rearrange("p (k c) -> p k c", k=nk),
                              in1=gv, op=ALU.mult))
            if allow_load and bg["q"]:
                task = bg["q"].pop(0)
                (l_, nm, pi, nk, C, goff, a, b) = task
                i = t % 2
                TR.dma(bg["s32"][i][:, 0:b - a], wsrc[nm][l_, pi][:, a:b], [], [bg["B32"][i]], qn="sp")
                pipe[t] = task
            bg["t"] = t + 1

        def bg_drain(everything):
            while (everything and bg["q"]) or bg["pipe"]:
                bg_step(allow_load=everything)

        def rsqrt_act(dst, Bdst, src, Bsrc, mult):
            TR.op("act", [Bsrc, Bconst], [Bdst],
                  lambda e: e.activation(out=dst, in_=src, func=AF.Ln, bias=eps_t[:, 0:1], scale=mult))
            TR.op("act", [Bdst], [Bdst],
                  lambda e: e.activation(out=dst, in_=dst, func=AF.Exp, scale=-0.5))

        def norm_a(c, sq, Bsq):
            cs = chs(c)
            TR.op("dve", [BxT[c]], [Bsq],
                  lambda e: e.tensor_tensor(out=sq[:, 0:3, :], in0=xT[:, 0:3, cs], in1=xT[:, 0:3, cs], op=ALU.mult))
            TR.op("pool", [BxT[c]], [Bsq],
                  lambda e: e.tensor_tensor(out=sq[:, 3:8, :], in0=xT[:, 3:8, cs], in1=xT[:, 3:8, cs], op=ALU.mult))

        def norm_chunk(c, sq, Bsq, rstd, Brstd, do_a=True):
            cs = chs(c)
            if do_a:
                norm_a(c, sq, Bsq)
            ps, Bps = gbank()

            def mm(e):
                for kc in range(8):
                    ins = e.matmul(ps[:, :], ones_bf[:, :], sq[:, kc, :], start=(kc == 0), stop=(kc == 7))
                return ins
            TR.op("pe", [Bsq, Bconst], [Bps], mm)
            rsqrt_act(rstd, Brstd, ps[:, :], Bps, 1.0 / D)
            TR.op("dve", [BxT[c], Brstd], [BuT[c]],
                  lambda e: e.tensor_tensor(out=uT[:, 0:5, cs], in0=xT[:, 0:5, cs],
                                            in1=rstd.unsqueeze(1).to_broadcast([128, 5, CH]), op=ALU.mult))
            TR.op("pool", [BxT[c], Brstd], [BuT[c]],
                  lambda e: e.tensor_tensor(out=uT[:, 5:8, cs], in0=xT[:, 5:8, cs],
                                            in1=rstd.unsqueeze(1).to_broadcast([128, 3, CH]), op=ALU.mult))

        def load_piece(dst, Bdst, nm, l, pi, F):
            TR.dma(dst[:, 0:F], wscr[nm][l, pi], [], [Bdst])

        ut_ready = [False]

        def ffn_phase(l, which, post_norm=False, pre_normed=False, bg_on=False):
            gu, dn = ("w_gu1", "w_d1") if which == 1 else ("w_gu2", "w_d2")
            AR.reset()
            hT = AR.alloc(NFC * CH * 2).rearrange("p (f t) -> p f t", f=NFC)
            BhT = Buf("hT")
            wg = [AR.alloc(2 * 2048 * 2).rearrange("p (g k c) -> p g k c", g=2, k=8) for _ in range(2)]
            Bwg = [Buf("wg%d" % i) for i in range(2)]
            wd = [AR.alloc(2816 * 2).rearrange("p (f c) -> p f c", f=NFC) for _ in range(2)]
            Bwd = [Buf("wd%d" % i) for i in range(2)]
            sq = AR.alloc(8 * CH * 2).rearrange("p (k t) -> p k t", k=8)
            Bsq = Buf("sq")
            rstd = AR.alloc(CH * 4, F32)
            Brstd = Buf("rstd")
            sg = [AR.alloc(CH * 4, F32) for _ in range(2)]
            Bsg = [Buf("sg%d" % i) for i in range(2)]
            bg_on = bg_on and bool(bg["q"])
            if bg_on:
                bg_begin()

            def load_gu(c, j):
                s = (c * 11 + j) % 2
                TR.dma(wg[s][:, 0], wscr[gu][l, j].rearrange("p (k c) -> p k c", k=8), [], [Bwg[s]])
                TR.dma(wg[s][:, 1], wscr[gu][l, 11 + j].rearrange("p (k c) -> p k c", k=8), [], [Bwg[s]])

            def load_d(c, dc):
                s = (c * 8 + dc) % 2
                TR.dma(wd[s], wscr[dn][l, dc].rearrange("p (f c) -> p f c", f=NFC), [], [Bwd[s]])

            if not pre_normed:
                norm_chunk(0, sq, Bsq, rstd, Brstd)
            for c in range(NCH):
                cs = chs(c)
                load_gu(c, 0)
                for j in range(11):
                    if j + 1 < 11:
                        load_gu(c, j + 1)
                    else:
                        load_d(c, 0)
                    if post_norm and c > 0:
                        if j == 1:
                            norm_a(c - 1, sq, Bsq)
                        if j == 4:
                            norm_chunk(c - 1, sq, Bsq, rstd, Brstd, do_a=False)
                    if (not pre_normed) and c + 1 < NCH:
                        if j == 6:
                            norm_a(c + 1, sq, Bsq)
                        if j == 9:
                            norm_chunk(c + 1, sq, Bsq, rstd, Brstd, do_a=False)
                    if bg_on:
                        bg_step()
                    s = (c * 11 + j) % 2
                    for fi in range(2):
                        fc = 2 * j + fi
                        psg, Bg = gbank()
                        psu, Bu = gbank()

                        def mmg(e, s=s, fi=fi, g=0, ps=psg):
                            for kc in range(8):
                                ins = e.matmul(ps[:, :], wg[s][:, g, kc, fi * 128:(fi + 1) * 128], uT[:, kc, cs],
                                               start=(kc == 0), stop=(kc == 7))
                            return ins
                        TR.op("pe", [Bwg[s], BuT[c]], [Bg], mmg)
                        TR.op("pe", [Bwg[s], BuT[c]], [Bu], lambda e, s=s, fi=fi, ps=psu: mmg(e, s, fi, 1, ps))
                        q = fc % 2
                        TR.op("act", [Bg], [Bsg[q]],
                              lambda e, q=q, ps=psg: e.activation(out=sg[q], in_=ps[:, :], func=AF.Silu))
                        TR.op("dve", [Bsg[q], Bu], [BhT],
                              lambda e, q=q, ps=psu, fc=fc: e.tensor_tensor(out=hT[:, fc, :], in0=sg[q], in1=ps[:, :],
                                                                            op=ALU.mult))
                for dc in range(8):
                    if dc + 1 < 8:
                        load_d(c, dc + 1)
                    if bg_on and dc % 2 == 0:
                        bg_step()
                    s = (c * 8 + dc) % 2
                    ps, Bps = gbank()

                    def mmd(e, s=s, ps=ps):
                        for fc in range(NFC):
                            ins = e.matmul(ps[:, :], wd[s][:, fc, :], hT[:, fc, :], start=(fc == 0), stop=(fc == NFC - 1))
                        return ins
                    TR.op("pe", [Bwd[s], BhT], [Bps], mmd)
                    TR.op("dve", [Bps, BxT[c]], [BxT[c]],
                          lambda e, ps=ps, dc=dc: e.scalar_tensor_tensor(out=xT[:, dc, cs], in0=ps[:, :], scalar=0.5,
                                                                         in1=xT[:, dc, cs], op0=ALU.mult, op1=ALU.add))
            if post_norm:
                norm_chunk(NCH - 1, sq, Bsq, rstd, Brstd)
                ut_ready[0] = (which == 1)
            if bg_on:
                bg_drain(everything=(which == 2))
            TR.barrier()

        def ple_phase(l, s_idx, pre_normed=False, post_norm=False):
            AR.reset()
            wpg = AR.alloc(4 * 2048 * 2).rearrange("p (j k c) -> p j k c", j=4, k=8)
            wpp = AR.alloc(2048 * 2).rearrange("p (k c) -> p k c", k=2)
            Bw = Buf("wple")
            sq = AR.alloc(8 * CH * 2).rearrange("p (k t) -> p k t", k=8)
            Bsq = Buf("sq")
            rstd = AR.alloc(CH * 4, F32)
            Brstd = Buf("rstd")
            pin = [AR.alloc(256 * 4, F32) for _ in range(2)]
            Bpin = [Buf("pin%d" % i) for i in range(2)]
            pT = AR.alloc(2 * CH * 2).rearrange("p (k t) -> p k t", k=2)
            BpT = Buf("pT")
            sg = [AR.alloc(CH * 4, F32) for _ in range(2)]
            Bsg = [Buf("sg%d" % i) for i in range(2)]
            tt = [AR.alloc(CH * 4, F32) for _ in range(2)]
            Btt = [Buf("tt%d" % i) for i in range(2)]
            for j in range(4):
                TR.dma(wpg[:, j], wscr["w_pg"][l, j].rearrange("p (k c) -> p k c", k=8), [], [Bw])
            TR.dma(wpp, wscr["w_pp"][l, 0].rearrange("p (k c) -> p k c", k=2), [], [Bw])
            for c in range(NCH):
                cs = chs(c)
                if not pre_normed:
                    norm_chunk(c, sq, Bsq, rstd, Brstd)
                pss = [gbank(), gbank()]
                for ti in range(4):
                    s = ti % 2
                    tok0 = c * CH + ti * 128
                    TR.dma(pin[s], p_d[l, s_idx, tok0:tok0 + 128, :], [], [Bpin[s]])
                    for pc in range(2):
                        TR.op("pe", [Bpin[s], Bconst], [pss[pc][1]],
                              lambda e, s=s, pc=pc, ti=ti: e.transpose(pss[pc][0][:, ti * 128:(ti + 1) * 128],
                                                                       pin[s][:, pc * 128:(pc + 1) * 128], ident[:, :]))
                for pc in range(2):
                    TR.op("act", [pss[pc][1]], [BpT],
                          lambda e, pc=pc: e.copy(out=pT[:, pc, :], in_=pss[pc][0][:, :]))
                for dc in range(8):
                    if post_norm and c > 0 and dc == 0:
                        norm_a(c - 1, sq, Bsq)
                    if post_norm and c > 0 and dc == 4:
                        norm_chunk(c - 1, sq, Bsq, rstd, Brstd, do_a=False)
                    psg, Bg = gbank()
                    psp, Bp = gbank()
                    j, co = dc // 2, (dc % 2) * 128

                    def mmg(e, ps=psg, j=j, co=co):
                        for kc in range(8):
                            ins = e.matmul(ps[:, :], wpg[:, j, kc, co:co + 128], uT[:, kc, cs], start=(kc == 0), stop=(kc == 7))
                        return ins

                    def mmp(e, ps=psp, dc=dc):
                        for pc in range(2):
                            ins = e.matmul(ps[:, :], wpp[:, pc, dc * 128:(dc + 1) * 128], pT[:, pc, :], start=(pc == 0), stop=(pc == 1))
                        return ins
                    TR.op("pe", [Bw, BuT[c]], [Bg], mmg)
                    TR.op("pe", [Bw, BpT], [Bp], mmp)
                    q = dc % 2
                    TR.op("act", [Bg], [Bsg[q]], lambda e, q=q, ps=psg: e.activation(out=sg[q], in_=ps[:, :], func=AF.Sigmoid))
                    TR.op("dve", [Bsg[q], Bp], [Btt[q]],
                          lambda e, q=q, ps=psp: e.tensor_tensor(out=tt[q], in0=sg[q], in1=ps[:, :], op=ALU.mult))
                    TR.op("pool", [Btt[q], BxT[c]], [BxT[c]],
                          lambda e, q=q, dc=dc: e.tensor_tensor(out=xT[:, dc, cs], in0=xT[:, dc, cs], in1=tt[q], op=ALU.add))
            if post_norm:
                norm_chunk(NCH - 1, sq, Bsq, rstd, Brstd)
            TR.barrier()

        def mixer_phase(l):
            AR.reset()
            sq = AR.alloc(8 * CH * 2).rearrange("p (k t) -> p k t", k=8)
            Bsq = Buf("sq")
            rstd = AR.alloc(CH * 4, F32)
            Brstd = Buf("rstd")
            if not ut_ready[0]:
                for c in range(NCH):
                    norm_chunk(c, sq, Bsq, rstd, Brstd)
                TR.barrier()
            ut_ready[0] = False
            for gi, mx in enumerate(("na", "swa", "dil", "mla")):
                if mx in flags:
                    one_mixer(l, gi, mx)
                    TR.barrier()

        def one_mixer(l, gi, mx):
            AR.reset()
            is_mla = mx == "mla"
            dk = 96 if is_mla else 64
            scale = dk ** -0.5
            nkt = 4 if is_mla else 2
            KT = [AR.alloc(T * 2) for _ in range(nkt)]
            BKT = Buf("KT")
            Vflat = AR.alloc(16 * 4 * 65 * 2)
            V = Vflat[:, 0:16 * 4 * 65].rearrange("p (k h d) -> p k h d", k=16, h=4)
            BV = Buf("V")
            nqt = 4
            QT = [[AR.alloc(CH * 2) for _ in range(nqt)] for _ in range(2)]
            BQT = [Buf("QT0"), Buf("QT1")]
            NPT = 5
            PT = [AR.alloc(CH * 2) for _ in range(NPT)]
            BPT = [Buf("PT%d" % i) for i in range(NPT)]
            ych = [AR.alloc(4 * 256 * 4, F32).rearrange("p (j h d) -> p j h d", j=4, h=4)] * 2
            Bych = [Buf("y0")] * 2
            if not is_mla:
                ysq = AR.alloc(4 * 256 * 4, F32).rearrange("p (j c) -> p j c", j=4)
                Bysq = Buf("ysq")
            ynT = [AR.alloc(2 * CH * 2).rearrange("p (k t) -> p k t", k=2)] * 2
            BynT = [Buf("ynT0")] * 2
            wst = [AR.alloc(2048 * 2).rearrange("p (k c) -> p k c", k=8) for _ in range(2)]
            Bwst = [Buf("wst0"), Buf("wst1")]
            wq = [AR.alloc(2048 * 2).rearrange("p (k c) -> p k c", k=8) for _ in range(2 if mx in ("swa", "dil") else 1)]
            Bwq = Buf("wq")
            wo = AR.alloc(2048 * 2).rearrange("p (k c) -> p k c", k=2)
            Bwo = Buf("wo")
            sqs = [AR.alloc(CH * 2) for _ in range(2)]
            Bsqs = [Buf("sqs0"), Buf("sqs1")]
            if mx != "na":
                r1 = [AR.alloc(CH * 4, F32)] * 2
                r2 = [AR.alloc(CH * 4, F32)] * 2
                Br = [Buf("r0")] * 2
            if is_mla:
                wuq = AR.alloc(1536 * 2).rearrange("p (k h a d) -> p k h a d", k=2, h=4, a=2)
                wukv = AR.alloc(512 * 2)
                Bwm = Buf("wmla")
                cqn = [AR.alloc(2 * CH * 2).rearrange("p (k t) -> p k t", k=2) for _ in range(2)]
                Bcqn = [Buf("cqn0"), Buf("cqn1")]
                ckvn = AR.alloc(T * 2)
                Bckvn = Buf("ckvn")
                lat32 = AR.alloc(2 * CH * 4, F32).rearrange("p (k t) -> p k t", k=2)
                Blat = Buf("lat32")
                ysq = lat32.rearrange("p k t -> p (k t)").rearrange("p (j c) -> p j c", j=4)
                Bysq = Blat
                lrs = AR.alloc(CH * 4, F32)
                Blrs = Buf("lrs")
                lsq = AR.alloc(2 * CH * 2).rearrange("p (k t) -> p k t", k=2)
                Blsq = Buf("lsq")
            if mx == "na":
                nam = AR.alloc(4 * 2 * 1024 * 2).rearrange("p (h v c) -> p h v c", h=4, v=2)
                Bnam = Buf("nam")
                nst = AR.alloc(1024 * 4, F32)
                Bnst = Buf("nst")
            wrr = [0]

            def wslot():
                i = wrr[0] % 2
                wrr[0] += 1
                return wst[i], Bwst[i]

            TR.dma(wo, wscr["w_out"][l, gi].rearrange("p (k c) -> p k c", k=2), [], [Bwo])
            TR.op("pool", [], [BV], lambda e: e.memset(Vflat, 1.0))

            def proj_fm(wt, Bw, col0, M, rhs_fn, nk, extra_reads, N=CH):
                ps, Bps = gbank()

                def mm(e):
                    for kc in range(nk):
                        ins = e.matmul(ps[0:M, 0:N], wt[:, kc, col0:col0 + M], rhs_fn(kc), start=(kc == 0), stop=(kc == nk - 1))
                    return ins
                TR.op("pe", [Bw] + extra_reads, [Bps], mm)
                return ps, Bps

            def sumsq_max(src, Bsrc, rows, lhs, stat_col, part_col):
                i = alt("0", "1") == "0"
                i = 0 if i else 1
                TR.op("pool", [Bsrc], [Bsqs[i]],
                      lambda e: e.tensor_tensor(out=sqs[i][rows, :], in0=src[rows, :], in1=src[rows, :], op=ALU.mult))
                ps, Bps = gbank()
                TR.op("pe", [Bsqs[i], Bconst], [Bps],
                      lambda e: e.matmul(ps[:, :], lhs[rows, :], sqs[i][rows, :], start=True, stop=True))
                TR.op("dve", [Bps], [Bk2],
                      lambda e: e.reduce_max(out=stat[:, part_col:part_col + 1], in_=ps[:, :], axis=AX.X))
                if stat_col is not None:
                    TR.op("dve", [Bk2], [Bk2],
                          lambda e: e.tensor_tensor(out=stat[:, stat_col:stat_col + 1], in0=stat[:, stat_col:stat_col + 1],
                                                    in1=stat[:, part_col:part_col + 1], op=ALU.max))

            def rope_evac(psA, BA, psB, BB, rows, ti, cs, dst, Bdst, dsts=None):
                i = alt("0", "1") == "0"
                i = 0 if i else 1
                TR.op("dve", [BA, Bconst], [Br[i]],
                      lambda e: e.tensor_tensor(out=r1[i][rows, :], in0=psA[rows, :], in1=rope[rows, ti, cs], op=ALU.mult))
                TR.op("dve", [BB, Bconst, Br[i]], [Br[i]],
                      lambda e: e.tensor_tensor(out=r2[i][rows, :], in0=psB[rows, :], in1=rope[rows, ti + 1, cs], op=ALU.mult))
                for (d_, rw) in (dsts if dsts is not None else [(dst, rows)]):
                    TR.op("pool", [Br[i]], [Bdst],
                          lambda e: e.tensor_tensor(out=d_[rw, :], in0=r1[i][rw, :], in1=r2[i][rw, :], op=ALU.add))

            full = slice(0, 128)
            TR.op("dve", [], [Bk2], lambda e: e.memset(stat[:, 0:4], 0.0))
            if not is_mla:
                for qb_ in range(2):
                    for h in range(4):
                        TR.op("pool", [], [BQT[qb_]], lambda e, h=h: e.memset(QT[qb_][h], 0.0))

            if not is_mla:
                roped = mx in ("swa", "dil")
                pk = {"na": P_NA_K, "swa": P_SW_K, "dil": P_DI_K}[mx]
                pv = {"na": P_NA_V, "swa": P_SW_V, "dil": P_DI_V}[mx]
                nkt_used = 1 if mx == "swa" else 2
                wk, Bwk = wslot()
                load_piece(wk.rearrange("p k c -> p (k c)"), Bwk, "w_in", l, pk, 2048)
                if mx == "dil":
                    wks, Bwks = wslot()
                    load_piece(wks.rearrange("p k c -> p (k c)"), Bwks, "w_in", l, P_DI_KS, 2048)
                for t in range(nkt_used):
                    for c in range(NCH):
                        cs = chs(c)
                        psA, BA = proj_fm(wk, Bwk, t * 128, 128, lambda kc: uT[:, kc, cs], 8, [BuT[c]])
                        if roped:
                            if mx == "swa":
                                psB, BB = proj_fm(wk, Bwk, 128, 128, lambda kc: uT[:, kc, cs], 8, [BuT[c]])
                            else:
                                psB, BB = proj_fm(wks, Bwks, t * 128, 128, lambda kc: uT[:, kc, cs], 8, [BuT[c]])
                            rope_evac(psA, BA, psB, BB, full, 0, cs, KT[t][:, cs], BKT)
                        else:
                            TR.op("act", [BA], [BKT], lambda e, t=t, cs=cs, ps=psA: e.copy(out=KT[t][:, cs], in_=ps[:, :]))
                wv, Bwv = wslot()
                load_piece(wv.rearrange("p k c -> p (k c)"), Bwv, "w_in", l, pv, 2048)
                nvh = 2 if mx == "swa" else 4
                for kt in range(16):
                    ps, Bps = gbank()

                    def mmv(e, ps=ps, kt=kt):
                        for kc in range(8):
                            ins = e.matmul(ps[:, 0:nvh * 64], uT[:, kc, kt * 128:(kt + 1) * 128], wv[:, kc, 0:nvh * 64],
                                           start=(kc == 0), stop=(kc == 7))
                        return ins
                    TR.op("pe", [Bwv, BuT[kt // 4]], [Bps], mmv)
                    TR.op("act", [Bps], [BV],
                          lambda e, ps=ps, kt=kt: e.copy(out=V[:, kt, 0:nvh, 0:64],
                                                         in_=ps[:, 0:nvh * 64].rearrange("p (h d) -> p h d", h=nvh)))
                for t in range(nkt_used):
                    for c in range(NCH):
                        cs = chs(c)
                        sumsq_max(KT[t][:, cs], BKT, full, onesA, 2 * t, 12)
                        sumsq_max(KT[t][:, cs], BKT, full, onesB, 2 * t + 1, 13)
            else:
                TR.dma(wuq.rearrange("p k h a d -> p (k h a d)"), wscr["w_uq"][l, 0], [], [Bwm])
                TR.dma(wukv, wscr["w_ukv"][l, 0], [], [Bwm])
                wckv, Bwckv = wslot()
                load_piece(wckv.rearrange("p k c -> p (k c)"), Bwckv, "w_in", l, P_ML_CKV, 2048)
                wkpe, Bwkpe = wslot()
                load_piece(wkpe.rearrange("p k c -> p (k c)"), Bwkpe, "w_in", l, P_ML_KPE, 2048)
                R = slice(64, 96)

                def mla_A(c):
                    cs = chs(c)
                    lb = c % 2
                    ps, Bps = proj_fm(wckv, Bwckv, 0, 128, lambda kc: uT[:, kc, cs], 8, [BuT[c]])
                    TR.op("act", [Bps], [Blat], lambda e: e.copy(out=lat32[:, lb, :], in_=ps[:, :]))
                    psA, BA = proj_fm(wkpe, Bwkpe, 0, 128, lambda kc: uT[:, kc, cs], 8, [BuT[c]])
                    psB, BB = proj_fm(wkpe, Bwkpe, 128, 128, lambda kc: uT[:, kc, cs], 8, [BuT[c]])
                    TR.op("pool", [Blat], [Bsqs[0]],
                          lambda e: e.tensor_tensor(out=sqs[0], in0=lat32[:, lb, :], in1=lat32[:, lb, :], op=ALU.mult))
                    ps2, Bps2 = gbank()
                    TR.op("pe", [Bsqs[0], Bconst], [Bps2],
                          lambda e: e.matmul(ps2[:, :], ones_bf[:, :], sqs[0], start=True, stop=True))
                    rope_evac(psA, BA, psB, BB, R, 2, cs, KT[0][:, cs], BKT)
                    rsqrt_act(lrs, Blrs, ps2[:, :], Bps2, 1.0 / 128)
                    TR.op("dve", [Blat, Blrs], [Bckvn],
                          lambda e: e.tensor_tensor(out=ckvn[:, cs], in0=lat32[:, lb, :], in1=lrs, op=ALU.mult))
                    for h in range(1, 4):
                        TR.op("pool", [BKT], [BKT],
                              lambda e, h=h: e.tensor_copy(out=KT[h][R, cs], in_=KT[0][R, cs]))

                def mla_B(c):
                    cs = chs(c)
                    for h in range(4):
                        ps, Bps = gbank()
                        TR.op("pe", [Bwm, Bckvn], [Bps],
                              lambda e, ps=ps, h=h: e.matmul(ps[0:64, :], wukv[:, h * 64:(h + 1) * 64], ckvn[:, cs],
                                                             start=True, stop=True))
                        TR.op("act", [Bps], [BKT],
                              lambda e, ps=ps, h=h: e.copy(out=KT[h][0:64, cs], in_=ps[0:64, :]))
                    for ti in range(4):
                        kt = c * 4 + ti
                        ps, Bps = gbank()
                        TR.op("pe", [Bwm, Bckvn], [Bps],
                              lambda e, ps=ps, kt=kt: e.matmul(ps[:, 0:256], ckvn[:, kt * 128:(kt + 1) * 128], wukv[:, 256:512],
                                                               start=True, stop=True))
                        TR.op("act", [Bps], [BV],
                              lambda e, ps=ps, kt=kt: e.copy(out=V[:, kt, :, 0:64],
                                                             in_=ps[:, 0:256].rearrange("p (h d) -> p h d", h=4)))
                    for h in range(4):
                        sumsq_max(KT[h][:, cs], BKT, slice(0, 96), ones_bf, h, 12)

                mla_A(0)
                for c in range(NCH):
                    if c + 1 < NCH:
                        mla_A(c + 1)
                    mla_B(c)

            if mx == "na":
                for h in range(4):
                    for v in range(2):
                        TR.dma(nst, nag_d[l, h, v], [], [Bnst])
                        TR.op("act", [Bnst], [Bnam], lambda e, h=h, v=v: e.activation(out=nam[:, h, v, :], in_=nst, func=AF.Exp))
            if not is_mla:
                pq = {"na": P_NA_Q, "swa": P_SW_Q, "dil": P_DI_Q}[mx]
                TR.dma(wq[0].rearrange("p k c -> p (k c)"), wscr["w_in"][l, pq], [], [Bwq])
                if mx in ("swa", "dil"):
                    pqs = {"swa": P_SW_QS, "dil": P_DI_QS}[mx]
                    TR.dma(wq[1].rearrange("p k c -> p (k c)"), wscr["w_in"][l, pqs], [], [Bwq])
            else:
                TR.dma(wq[0].rearrange("p k c -> p (k c)"), wscr["w_in"][l, P_ML_CQ], [], [Bwq])

            def head_map(h):
                if is_mla:
                    return h, slice(0, 96), h, h
                if mx == "swa":
                    half = h // 2
                    return h, slice(half * 64, half * 64 + 64), 0, half
                return h, slice((h % 2) * 64, (h % 2) * 64 + 64), h // 2, h

            def qtile_of(h):
                return (h % 2) if mx == "swa" else (h // 2)

            def pairs(qc, h):
                out = []
                for kt in range(16):
                    js = []
                    for j in range(4):
                        qt = 4 * qc + j
                        if mx == "mla":
                            ok = True
                        elif mx == "swa":
                            ok = abs(qt - kt) <= 1
                        elif mx == "dil":
                            ok = abs(qt - kt) <= 8
                        else:
                            if qt <= 1:
                                ok = kt <= 3
                            elif qt >= 14:
                                ok = kt >= 12
                            else:
                                ok = abs(qt - kt) <= 2
                        if ok:
                            js.append(j)
                    if js:
                        out.append((kt, js[0], js[-1]))
                return out

            def mask_ap(kt, qc, j0, j1, h):
                n = (j1 - j0 + 1) * 128
                q0 = (4 * qc + j0) * 128
                if mx == "swa":
                    o = q0 - 128 * kt + 128
                    return [(0, n, smask[:, o:o + n])]
                if mx == "dil":
                    o = q0 - 128 * kt + 1024
                    return [(0, n, dmask[:, o:o + n])]
                if mx == "na":
                    res = []
                    j = j0
                    while j <= j1:
                        qt = 4 * qc + j
                        v = 1 if (qt <= 1 or qt >= 14) else 0
                        j2 = j
                        while j2 + 1 <= j1 and (1 if (4 * qc + j2 + 1 <= 1 or 4 * qc + j2 + 1 >= 14) else 0) == v:
                            j2 += 1
                        o = (7 - 2 * (kt - qt)) * 64
                        w = (j2 - j + 1) * 128
                        res.append(((j - j0) * 128, w, nam[:, h, v, o:o + w]))
                        j = j2 + 1
                    return res
                return []

            def ss_sq(src, Bsrc, rows, i):
                TR.op("pool", [Bsrc], [Bsqs[i]],
                      lambda e: e.tensor_tensor(out=sqs[i][rows, :], in0=src[rows, :], in1=src[rows, :], op=ALU.mult))

            def ss_mm(rows, i, col):
                ps, Bps = gbank()
                TR.op("pe", [Bsqs[i], Bconst], [Bps],
                      lambda e: e.matmul(ps[:, :], ones_bf[rows, :], sqs[i][rows, :], start=True, stop=True))
                TR.op("dve", [Bps], [Bq2[col // 32]],
                      lambda e: e.reduce_max(out=stat[:, col:col + 1], in_=ps[:, :], axis=AX.X))

            def q_stages(qc):
                cs = chs(qc)
                qb = qc % 2
                SO = 32 * qb
                srows = slice(0, 96) if is_mla else full
                stages = []
                if not is_mla:
                    roped = mx in ("swa", "dil")

                    def s_proj():
                        for t in range(2):
                            hh = [h for h in range(4) if qtile_of(h) == t]
                            hh.sort(key=lambda h: head_map(h)[1].start)
                            dsts = [(QT[qb][h], head_map(h)[1]) for h in hh]
                            psA, BA = proj_fm(wq[0], Bwq, t * 128, 128, lambda kc: uT[:, kc, cs], 8, [BuT[qc]])
                            if roped:
                                psB, BB = proj_fm(wq[1], Bwq, t * 128, 128, lambda kc: uT[:, kc, cs], 8, [BuT[qc]])
                                rope_evac(psA, BA, psB, BB, full, 0, cs, None, BQT[qb], dsts=dsts)
                            else:
                                for (d_, rw) in dsts:
                                    TR.op("act", [BA], [BQT[qb]], lambda e, ps=psA: e.copy(out=d_[rw, :], in_=ps[rw, :]))
                    stages.append(s_proj)

                    def k2col(h):
                        _, rows, kt_, _ = head_map(h)
                        return 2 * kt_ + (1 if rows.start == 64 else 0)
                else:
                    cb = qc % 2

                    def s_lat():
                        for t in range(2):
                            ps, Bps = proj_fm(wq[0], Bwq, t * 128, 128, lambda kc: uT[:, kc, cs], 8, [BuT[qc]])
                            TR.op("act", [Bps], [Blat], lambda e, ps=ps, t=t: e.copy(out=lat32[:, t, :], in_=ps[:, :]))
                        TR.op("pool", [Blat], [Blsq],
                              lambda e: e.tensor_tensor(out=lsq, in0=lat32, in1=lat32, op=ALU.mult))

                    def s_norm():
                        ps2, Bps2 = gbank()

                        def mmss(e):
                            for kc in range(2):
                                ins = e.matmul(ps2[:, :], ones_bf[:, :], lsq[:, kc, :], start=(kc == 0), stop=(kc == 1))
                            return ins
                        TR.op("pe", [Blsq, Bconst], [Bps2], mmss)
                        rsqrt_act(lrs, Blrs, ps2[:, :], Bps2, 1.0 / 256)
                        TR.op("dve", [Blat, Blrs], [Bcqn[cb]],
                              lambda e: e.tensor_tensor(out=cqn[cb], in0=lat32,
                                                        in1=lrs.unsqueeze(1).to_broadcast([128, 2, CH]), op=ALU.mult))

                    def s_uq():
                        for h in range(4):
                            psA, BA = gbank()
                            psB, BB = gbank()

                            def mmq(e, ps, a, h=h):
                                for kc in range(2):
                                    ins = e.matmul(ps[0:96, :], wuq[:, kc, h, a, :], cqn[cb][:, kc, :], start=(kc == 0), stop=(kc == 1))
                                return ins
                            TR.op("pe", [Bwm, Bcqn[cb]], [BA], lambda e, ps=psA: mmq(e, ps, 0))
                            TR.op("pe", [Bwm, Bcqn[cb]], [BB], lambda e, ps=psB: mmq(e, ps, 1))
                            TR.op("act", [BA], [BQT[qb]], lambda e, ps=psA, h=h: e.copy(out=QT[qb][h][0:64, :], in_=ps[0:64, :]))
                            rope_evac(psA, BA, psB, BB, slice(64, 96), 2, cs, QT[qb][h], BQT[qb])
                    stages += [s_lat, s_norm, s_uq]

                    def k2col(h):
                        return h

                def s_sq01():
                    ss_sq(QT[qb][0], BQT[qb], srows, 0)
                    ss_sq(QT[qb][1], BQT[qb], srows, 1)

                def s_mm01():
                    ss_mm(srows, 0, SO + 4)
                    ss_mm(srows, 1, SO + 5)
                    ss_sq(QT[qb][2], BQT[qb], srows, 0)
                    ss_sq(QT[qb][3], BQT[qb], srows, 1)

                def s_mm23():
                    ss_mm(srows, 0, SO + 6)
                    ss_mm(srows, 1, SO + 7)

                def s_negm():
                    for h in range(4):
                        TR.op("dve", [Bq2[qb], Bk2], [Bneg[qb]],
                              lambda e, h=h: e.tensor_tensor(out=stat[:, SO + 8 + h:SO + 9 + h], in0=stat[:, SO + 4 + h:SO + 5 + h],
                                                             in1=stat[:, k2col(h):k2col(h) + 1], op=ALU.mult))
                    TR.op("act", [Bneg[qb]], [Bneg[qb]],
                          lambda e: e.activation(out=stat[:, SO + 8:SO + 12], in_=stat[:, SO + 8:SO + 12], func=AF.Ln))
                    TR.op("act", [Bneg[qb]], [Bneg[qb]],
                          lambda e: e.activation(out=stat[:, SO + 8:SO + 12], in_=stat[:, SO + 8:SO + 12], func=AF.Exp, scale=0.5))
                    TR.op("dve", [Bneg[qb]], [Bneg[qb]],
                          lambda e: e.tensor_scalar(out=stat[:, SO + 8:SO + 12], in0=stat[:, SO + 8:SO + 12], scalar1=-scale,
                                                    scalar2=None, op0=ALU.mult))
                    if mx == "swa":
                        for h in range(4):
                            TR.op("act", [Bneg[qb], Bconst], [Bneg[qb]],
                                  lambda e, h=h: e.activation(out=stat[:, SO + 16 + h:SO + 17 + h], in_=sink_sb[:, l, h:h + 1],
                                                              func=AF.Exp, bias=stat[:, SO + 8 + h:SO + 9 + h], scale=1.0))
                stages += [s_sq01, s_mm01, s_mm23, s_negm]
                return stages

            def attention(qc, inject):
                cs = chs(qc)
                qb = qc % 2
                SO = 32 * qb
                yb = 0
                items = []
                for h in range(4):
                    prs = pairs(qc, h)
                    for ii, (kt, j0, j1) in enumerate(prs):
                        items.append((h, kt, j0, j1, ii == 0, ii == len(prs) - 1))
                DEPTH = NPT - 1
                hob = {}
                slots = {}

                def emit_score(i):
                    h, kt, j0, j1, _, _ = items[i]
                    qt_i, rows, kt_i, vh = head_map(h)
                    n = (j1 - j0 + 1) * 128
                    crow = rows if is_mla else full
                    ps, Bps = gbank()
                    TR.op("pe", [BKT, BQT[qb]], [Bps],
                          lambda e: e.matmul(ps[:, 0:n], KT[kt_i][crow, kt * 128:(kt + 1) * 128],
                                             QT[qb][qt_i][crow, j0 * 128:j0 * 128 + n], start=True, stop=True))
                    rr["pt"] = rr.get("pt", 0) + 1
                    pi = rr["pt"] % NPT
                    slots[i] = pi
                    TR.op("act", [Bps, Bneg[qb]], [BPT[pi]],
                          lambda e: e.activation(out=PT[pi][:, 0:n], in_=ps[:, 0:n], func=AF.Exp,
                                                 bias=stat[:, SO + 8 + h:SO + 9 + h], scale=scale))
                    for (o, w, map_) in mask_ap(kt, qc, j0, j1, h):
                        rr["mk"] = rr.get("mk", 0) + 1
                        en = "pool" if rr["mk"] % 4 == 0 else "dve"
                        rd = [Bconst] if mx != "na" else [Bnam]
                        TR.op(en, [BPT[pi]] + rd, [BPT[pi]],
                              lambda e: e.tensor_tensor(out=PT[pi][:, o:o + w], in0=PT[pi][:, o:o + w], in1=map_, op=ALU.mult))

                def emit_pv(i):
                    h, kt, j0, j1, first, last = items[i]
                    qt_i, rows, kt_i, vh = head_map(h)
                    pi = slots[i]
                    if first:
                        hob[h] = obank()
                    ob, Bob = hob[h]

                    def pv(e):
                        for j in range(j0, j1 + 1):
                            ins = e.matmul(ob[:, j * 128:j * 128 + 65], PT[pi][:, (j - j0) * 128:(j - j0 + 1) * 128],
                                           V[:, kt, vh, :], start=(first and j == j0), stop=(last and j == j1))
                        return ins
                    TR.op("pe", [BPT[pi], BV], [Bob], pv)
                    if last:
                        obv = ob[:, :].rearrange("p (j c) -> p j c", j=4)
                        if mx == "swa":
                            TR.op("dve", [Bob, Bneg[qb]], [Brecs[h % 2]],
                                  lambda e: e.tensor_scalar(out=stat[:, 20 + 8 * (h % 2):24 + 8 * (h % 2)], in0=obv[:, :, 64], scalar1=stat[:, SO + 16 + h:SO + 17 + h],
                                                            scalar2=None, op0=ALU.add))
                            TR.op("dve", [Brecs[h % 2]], [Brecs[h % 2]], lambda e: e.reciprocal(out=stat[:, 20 + 8 * (h % 2):24 + 8 * (h % 2)], in_=stat[:, 20 + 8 * (h % 2):24 + 8 * (h % 2)]))
                        else:
                            TR.op("dve", [Bob], [Brecs[h % 2]], lambda e: e.reciprocal(out=stat[:, 20 + 8 * (h % 2):24 + 8 * (h % 2)], in_=obv[:, :, 64]))
                        TR.op("dve", [Bob, Brecs[h % 2]], [Bych[yb]],
                              lambda e: e.tensor_tensor(out=ych[yb][:, :, h, :], in0=obv[:, :, 0:64],
                                                        in1=stat[:, 20 + 8 * (h % 2):24 + 8 * (h % 2)].unsqueeze(2).to_broadcast([128, 4, 64]),
                                                        op=ALU.mult))

                for i in range(len(items) + DEPTH):
                    if i < len(items):
                        emit_score(i)
                    if i - DEPTH >= 0:
                        emit_pv(i - DEPTH)
                    while inject and inject[0][0] <= i:
                        inject.pop(0)[1]()
                while inject:
                    inject.pop(0)[1]()

            def fin1(qc):
                yb = 0
                yv = ych[yb].rearrange("p j h d -> p j (h d)")
                TR.op("pool", [Bych[yb]], [Bysq], lambda e, yv=yv: e.tensor_tensor(out=ysq, in0=yv, in1=yv, op=ALU.mult))
                TR.op("dve", [Bysq], [Bgn], lambda e: e.reduce_sum(out=stat[:, 24:28], in_=ysq, axis=AX.X))
                rsqrt_act(stat[:, 24:28], Bgn, stat[:, 24:28], Bgn, 1.0 / 256)
                TR.op("dve", [Bych[yb], Bgn], [Bych[yb]],
                      lambda e, yv=yv: e.tensor_tensor(out=yv, in0=yv, in1=stat[:, 24:28].unsqueeze(2).to_broadcast([128, 4, 256]),
                                                       op=ALU.mult))

            def fin2(qc):
                cs = chs(qc)
                yb = 0
                yv = ych[yb].rearrange("p j h d -> p j (h d)")
                for cc in range(2):
                    ps, Bps = gbank()

                    def tp(e, ps=ps, cc=cc):
                        for j in range(4):
                            ins = e.transpose(ps[:, j * 128:(j + 1) * 128], yv[:, j, cc * 128:(cc + 1) * 128], ident[:, :])
                        return ins
                    TR.op("pe", [Bych[yb], Bconst], [Bps], tp)
                    TR.op("act", [Bps], [BynT[yb]], lambda e, ps=ps, cc=cc: e.copy(out=ynT[yb][:, cc, :], in_=ps[:, :]))
                for dc in range(8):
                    ps, Bps = gbank()

                    def mmo(e, ps=ps, dc=dc):
                        for cc in range(2):
                            ins = e.matmul(ps[:, :], wo[:, cc, dc * 128:(dc + 1) * 128], ynT[yb][:, cc, :], start=(cc == 0), stop=(cc == 1))
                        return ins
                    TR.op("pe", [Bwo, BynT[yb]], [Bps], mmo)
                    TR.op("dve", [Bps, BxT[qc]], [BxT[qc]],
                          lambda e, ps=ps, dc=dc: e.tensor_tensor(out=xT[:, dc, cs], in0=ps[:, :], in1=xT[:, dc, cs], op=ALU.add))

            for st_ in q_stages(0):
                st_()
            for qc in range(NCH):
                inj = []
                if qc > 0:
                    inj.append((3, lambda qc=qc: fin2(qc - 1)))
                if qc + 1 < NCH:
                    for k_, st_ in enumerate(q_stages(qc + 1)):
                        inj.append((6 + 3 * k_, st_))
                attention(qc, inj)
                fin1(qc)
            fin2(NCH - 1)

        for s_idx in range(n_seq):
            AR.reset()
            xin = [AR.alloc(D * 4, F32) for _ in range(2)]
            Bxin = [Buf("xin0"), Buf("xin1")]
            for tt in range(16):
                s = tt % 2
                c = tt // 4
                TR.dma(xin[s], x_d[s_idx, tt * 128:(tt + 1) * 128, :], [], [Bxin[s]])
                for hb in range(2):
                    ps, Bps = gbank()

                    def tp(e, ps=ps, s=s, hb=hb):
                        for i in range(4):
                            kc = hb * 4 + i
                            ins = e.transpose(ps[:, i * 128:(i + 1) * 128], xin[s][:, kc * 128:(kc + 1) * 128], ident[:, :])
                        return ins
                    TR.op("pe", [Bxin[s], Bconst], [Bps], tp)
                    TR.op(alt("act", "dve"), [Bps], [BxT[c]],
                          lambda e, ps=ps, hb=hb, tt=tt: (e.copy if e is nc.scalar else e.tensor_copy)(
                              out=xT[:, hb * 4:hb * 4 + 4, tt * 128:(tt + 1) * 128],
                              in_=ps[:, :].rearrange("p (k t) -> p k t", k=4)))
            TR.barrier()
            carry = False
            for l in range(L):
                has_mix = bool(flags & {"na", "swa", "dil", "mla"})
                bg_on = BG_OK and s_idx == 0 and (l + 1) in bg_tasks
                if bg_on:
                    bg["q"] = list(bg_tasks[l + 1])
                if "ffn1" in flags:
                    ffn_phase(l, 1, post_norm=has_mix, pre_normed=carry, bg_on=bg_on)
                    carry = False
                if has_mix:
                    mixer_phase(l)
                if "ffn2" in flags:
                    ffn_phase(l, 2, post_norm=("ple" in flags), bg_on=bg_on)
                if "ple" in flags:
                    nxt = ("ffn1" in flags) and (l + 1 < L)
                    ple_phase(l, s_idx, pre_normed=("ffn2" in flags), post_norm=nxt)
                    carry = nxt
            AR.reset()
            sq = AR.alloc(8 * CH * 2).rearrange("p (k t) -> p k t", k=8)
            Bsq = Buf("sq")
            rstd = AR.alloc(CH * 4, F32)
            Brstd = Buf("rstd")
            yn = [AR.alloc(8 * 128 * 4, F32).rearrange("p (k t) -> p k t", k=8) for _ in range(2)]
            Byn = [Buf("yn0"), Buf("yn1")]
            yo = [AR.alloc(D * 4, F32) for _ in range(2)]
            Byo = [Buf("yo0"), Buf("yo1")]
            Bout = [Buf("out0"), Buf("out1")]
            for c in range(NCH):
                cs = chs(c)
                TR.op("pool", [BxT[c]], [Bsq],
                      lambda e, cs=cs: e.tensor_tensor(out=sq, in0=xT[:, :, cs], in1=xT[:, :, cs], op=ALU.mult))
                ps, Bps = gbank()

                def mm(e, ps=ps):
                    for kc in range(8):
                        ins = e.matmul(ps[:, :], ones_bf[:, :], sq[:, kc, :], start=(kc == 0), stop=(kc == 7))
                    return ins
                TR.op("pe", [Bsq, Bconst], [Bps], mm)
                rsqrt_act(rstd, Brstd, ps[:, :], Bps, 1.0 / D)
                for ti in range(4):
                    tt = c * 4 + ti
                    s = tt % 2
                    tsl = slice(tt * 128, (tt + 1) * 128)
                    TR.op("dve", [BxT[c], Brstd], [Byn[s]],
                          lambda e, s=s, tsl=tsl, ti=ti: e.tensor_tensor(
                              out=yn[s], in0=xT[:, :, tsl],
                              in1=rstd[:, ti * 128:(ti + 1) * 128].unsqueeze(1).to_broadcast([128, 8, 128]), op=ALU.mult))
                    TR.op("pool", [Byn[s], Bconst], [Byn[s]],
                          lambda e, s=s: e.tensor_tensor(out=yn[s], in0=yn[s], in1=gfin[:, :].unsqueeze(2).to_broadcast([128, 8, 128]),
                                                         op=ALU.mult))
                    for hb in range(2):
                        ps, Bps = gbank()

                        def tp(e, ps=ps, s=s, hb=hb):
                            for i in range(4):
                                ins = e.transpose(ps[:, i * 128:(i + 1) * 128], yn[s][:, hb * 4 + i, :], ident[:, :])
                            return ins
                        TR.op("pe", [Byn[s], Bconst], [Bps], tp)
                        TR.op(alt("act", "dve"), [Bps], [Byo[s]],
                              lambda e, ps=ps, s=s, hb=hb: (e.copy if e is nc.scalar else e.tensor_copy)(
                                  out=yo[s][:, hb * 512:(hb + 1) * 512], in_=ps[:, :]))
                    TR.dma(y_d[s_idx, tsl, :], yo[s], [Byo[s]], [], sembuf=Bout[s])
            TR.barrier()
        TR.barrier()
        info = {k: v["count"] + sum(c for _, c in v["old"]) for k, v in TR.engs.items()}
        info["nsem"] = TR.nsem
        print("program: engine op counts", info)
    return nc


def _pieces_kc(w, C):
    Lw, K, N = w.shape
    nk = K // 128
    a = w.reshape(Lw, nk, 128, N // C, C).transpose(0, 3, 2, 1, 4)
    return np.ascontiguousarray(a.reshape(Lw, N // C, 128, nk * C))


def _swap_halves(w, hd):
    sh = w.shape
    a = w.reshape(sh[:-1] + (sh[-1] // hd, 2, hd // 2))
    return a[..., ::-1, :].reshape(sh)


def _gain_cols(g):
    Lg, K = g.shape
    return g.reshape(Lg, K // 128, 128).transpose(2, 0, 1)


def prepare_weights(inp, Lw):
    f = lambda a: np.asarray(a, dtype=np.float32)
    out = {}
    ffn_w = {1: (inp["ffn1_w_gate"], inp["ffn1_w_up"], inp["ffn1_w_down"]),
             2: (inp["ffn2_w_gate"], inp["ffn2_w_up"], inp["ffn2_w_down"])}
    for i in (1, 2):
        g = _pieces_kc(f(ffn_w[i][0])[:Lw], 256)
        u = _pieces_kc(f(ffn_w[i][1])[:Lw], 256)
        out["w_gu%d" % i] = np.ascontiguousarray(np.concatenate([g, u], axis=1))
        wd = f(ffn_w[i][2])[:Lw]
        a = wd.reshape(Lw, NFC, 128, 8, 128).transpose(0, 3, 2, 1, 4)
        out["w_d%d" % i] = np.ascontiguousarray(a.reshape(Lw, 8, 128, NFC * 128))
    win = f(inp["w_in"])[:Lw]
    z = np.zeros((Lw, D, 64), np.float32)
    na, sw, di, ml = win[..., 0:768], win[..., 768:1280], win[..., 1280:2048], win[..., 2048:2464]
    swq = sw[..., 0:256].reshape(Lw, D, 4, 64)[:, :, [0, 2, 1, 3], :].reshape(Lw, D, 256)
    swk = sw[..., 256:384]
    kpe = ml[..., 384:416]
    z32 = z[..., 0:32]
    kpe_t = np.concatenate([z, kpe, z32, z, _swap_halves(kpe, 32), z32], axis=-1)
    pad128 = np.zeros((Lw, D, 128), np.float32)
    cols = [
        na[..., 0:256], na[..., 256:512], na[..., 512:768],
        swq, _swap_halves(swq, 64), np.concatenate([swk, _swap_halves(swk, 64)], axis=-1),
        np.concatenate([sw[..., 384:512], pad128], axis=-1),
        di[..., 0:256], _swap_halves(di[..., 0:256], 64), di[..., 256:512], _swap_halves(di[..., 256:512], 64),
        di[..., 512:768],
        ml[..., 0:256], np.concatenate([ml[..., 256:384], pad128], axis=-1), kpe_t,
    ]
    out["w_in"] = np.ascontiguousarray(np.concatenate([_pieces_kc(np.ascontiguousarray(c), 256) for c in cols], axis=1))
    wo = f(inp["w_out"])[:Lw]
    out["w_out"] = np.ascontiguousarray(
        wo.reshape(Lw, 4, 2, 128, 1024).transpose(0, 1, 3, 2, 4).reshape(Lw, 4, 128, 2048))
    out["w_pg"] = _pieces_kc(f(inp["ple_w_gate"])[:Lw], 256)
    out["w_pp"] = _pieces_kc(f(inp["ple_w_proj"])[:Lw], 1024)
    uq = f(inp["mla_w_uq"])[:Lw].reshape(Lw, 256, 4, 96)
    uqB = np.concatenate([np.zeros((Lw, 256, 4, 64), np.float32), _swap_halves(uq[..., 64:96], 32)], axis=-1)
    uqAB = np.stack([uq, uqB], axis=3)
    out["w_uq"] = np.ascontiguousarray(
        uqAB.reshape(Lw, 2, 128, 768).transpose(0, 2, 1, 3).reshape(Lw, 1, 128, 1536))
    ukv = f(inp["mla_w_ukv"])[:Lw].reshape(Lw, 128, 4, 128)
    out["w_ukv"] = np.ascontiguousarray(
        np.concatenate([ukv[..., 0:64].reshape(Lw, 128, 256), ukv[..., 64:128].reshape(Lw, 128, 256)], axis=-1)
    ).reshape(Lw, 1, 128, 512)
    g = np.zeros((128, Lw, NG), np.float32)
    g[:, :, G_FFN1:G_FFN1 + 8] = _gain_cols(f(inp["ffn1_norm"])[:Lw])
    g[:, :, G_MIX:G_MIX + 8] = _gain_cols(f(inp["mix_norm"])[:Lw])
    g[:, :, G_FFN2:G_FFN2 + 8] = _gain_cols(f(inp["ffn2_norm"])[:Lw])
    g[:, :, G_PLE:G_PLE + 8] = _gain_cols(f(inp["ple_norm"])[:Lw])
    g[:, :, G_GRP:G_GRP + 8] = _gain_cols(f(inp["group_norm"])[:Lw].reshape(Lw, 1024))
    g[:, :, G_QN:G_QN + 2] = _gain_cols(f(inp["mla_q_norm"])[:Lw])
    g[:, :, G_KVN:G_KVN + 1] = _gain_cols(f(inp["mla_kv_norm"])[:Lw])
    out["g_all"] = g
    out["final_g"] = np.ascontiguousarray(f(inp["final_norm"]).reshape(8, 128).T)
    out["sink_b"] = np.ascontiguousarray(np.broadcast_to(f(inp["swa_sink"])[:Lw][None], (128, Lw, 4)))
    nb = f(inp["na_bias"])[:Lw]
    kap = np.arange(64)[:, None]
    cc = np.arange(64)[None, :]
    ws = np.clip(cc - 8, 0, 48)
    cvalid = (kap >= ws) & (kap < ws + 16)
    dcol = np.clip(kap - cc, -15, 15) + 15
    nag = np.full((Lw, 4, 2, 128, 16, 64), NEG, np.float32)
    for i in range(2):
        for slot in range(16):
            dr = 14 - slot + i
            if dr < 0 or dr > 14:
                continue
            vals = nb[:, :, dr][:, :, dcol]
            vals = np.where(cvalid[None, None], vals, np.float32(NEG))
            nag[:, :, 1, i * 64:(i + 1) * 64, slot, :] = vals
            if 3 <= dr <= 10:
                nag[:, :, 0, i * 64:(i + 1) * 64, slot, :] = vals
    out["na_g"] = nag.reshape(Lw, 4, 2, 128, 1024)
    return out


def constants():
    pos = np.arange(T, dtype=np.float32)
    rope = np.zeros((4, 128, T), np.float32)
    inv64 = (10000.0 ** (-np.arange(32, dtype=np.float32) / 32)).astype(np.float32)
    ang = pos[None, :] * inv64[:, None]
    c64, s64 = np.cos(ang), np.sin(ang)
    for hh in range(2):
        rope[0, hh * 64:hh * 64 + 32] = c64
        rope[0, hh * 64 + 32:hh * 64 + 64] = c64
        rope[1, hh * 64:hh * 64 + 32] = -s64
        rope[1, hh * 64 + 32:hh * 64 + 64] = s64
    inv32 = (10000.0 ** (-np.arange(16, dtype=np.float32) / 16)).astype(np.float32)
    ang = pos[None, :] * inv32[:, None]
    c32, s32 = np.cos(ang), np.sin(ang)
    rope[2, 64:80] = c32
    rope[2, 80:96] = c32
    rope[3, 64:80] = -s32
    rope[3, 80:96] = s32
    kap = np.arange(128)[:, None]
    j = np.arange(2176)[None, :]
    d = j - kap - 1024
    dm = ((np.abs(d) <= 64).astype(np.float32) + ((d % 4 == 0) & (np.abs(d) <= 256)).astype(np.float32)
          + ((d % 16 == 0) & (np.abs(d) <= 1024)).astype(np.float32))
    j = np.arange(384)[None, :]
    sm = (np.abs(j - kap - 128) <= 128).astype(np.float32)
    return {"rope_t": rope.astype(np.float32), "dil_mask": dm.astype(np.float32), "swa_mask": sm,
            "ident": np.eye(128, dtype=np.float32)}


_CACHE = {}


def kernel(**inputs):
    n_cores = 8
    x = np.asarray(inputs["x"], dtype=np.float32)
    p = np.asarray(inputs["p"], dtype=np.float32)
    w = prepare_weights(inputs, L_ALL)
    w.update(constants())
    if "nc" not in _CACHE:
        _CACHE["nc"] = build_program(L_ALL, 2, ALL_FLAGS)
    nc = _CACHE["nc"]
    in_maps = []
    for c in range(n_cores):
        m = dict(w)
        m["x"] = np.ascontiguousarray(x[2 * c:2 * c + 2])
        m["p"] = np.ascontiguousarray(p[:, 2 * c:2 * c + 2])
        in_maps.append(m)
    res = run_bass_kernel_spmd(nc, in_maps, core_ids=list(range(n_cores)))
    return np.concatenate([np.asarray(r["y"], dtype=np.float32) for r in res.results], axis=0)
```
